# Optimizing a Trainium2 kernel written in Bass

```python
import math
import jax, jax.numpy as jnp
from jax import lax
import numpy as np

D_MODEL = 1024
BATCH = 8
SEQ = 2048
DEPTH = 2

RG_WIDTH = 512
RG_BLOCKS = 4
RG_BLOCK = RG_WIDTH // RG_BLOCKS
RG_CONV = 4
RG_C = 8.0
DA_HEADS = 4
DA_HEAD_DIM = 64
DA_WIDTH = DA_HEADS * 2 * DA_HEAD_DIM
ROPE_THETA = 500000.0
ROPE_DIM = DA_HEAD_DIM // 4
Q_BLOCK = 128
NEG_INF = -1e30
ML_HEADS = 4
ML_HEAD_DIM = 128
ML_WIDTH = ML_HEADS * ML_HEAD_DIM
ML_CONV = 4
ML_CHUNK = 64
D_MIX = RG_WIDTH + DA_WIDTH + ML_WIDTH
D_FF = 2816
FFN_CONV = 3
EPS = 1e-6
IN_WIDTHS = (RG_WIDTH, RG_WIDTH, DA_WIDTH, DA_WIDTH, DA_WIDTH,
             ML_WIDTH, ML_WIDTH, ML_WIDTH, ML_WIDTH, ML_HEADS, ML_HEADS)
D_IN = sum(IN_WIDTHS)

kernel_name = 'hybrid_parallel_heads_rglru_diffattn_mlstm'


def rmsnorm(x, g):
    xf = x.astype(jnp.float32)
    y = xf * lax.rsqrt(jnp.mean(xf * xf, axis=-1, keepdims=True) + EPS)
    return (y * g.astype(jnp.float32)).astype(x.dtype)


def causal_dwconv(x, w, b):
    K = w.shape[0]
    S = x.shape[1]
    xp = jnp.pad(x, ((0, 0), (K - 1, 0), (0, 0)))
    return sum(xp[:, j:j + S] * w[j] for j in range(K)) + b


def rope_tables(positions):
    inv_freq = ROPE_THETA ** (-jnp.arange(0, ROPE_DIM, 2, dtype=jnp.float32) / ROPE_DIM)
    ang = positions.astype(jnp.float32)[..., None] * inv_freq
    return jnp.cos(ang), jnp.sin(ang)


def apply_partial_rope(x, cos, sin):
    half = ROPE_DIM // 2
    xr = x[..., :ROPE_DIM].astype(jnp.float32)
    x1, x2 = xr[..., :half], xr[..., half:]
    rot = jnp.concatenate([x1 * cos - x2 * sin, x2 * cos + x1 * sin], axis=-1)
    return jnp.concatenate([rot.astype(x.dtype), x[..., ROPE_DIM:]], axis=-1)


def rglru_group(xb, gb, conv_w, conv_b, wa, ba, wx, bx, lam, norm_g):
    B, S, _ = xb.shape
    u = causal_dwconv(xb, conv_w, conv_b)
    ub = u.reshape(B, S, RG_BLOCKS, RG_BLOCK)
    r = jax.nn.sigmoid(jnp.einsum('bsni,nij->bsnj', ub, wa).reshape(B, S, RG_WIDTH) + ba)
    i = jax.nn.sigmoid(jnp.einsum('bsni,nij->bsnj', ub, wx).reshape(B, S, RG_WIDTH) + bx)
    log_a = RG_C * r.astype(jnp.float32) * jax.nn.log_sigmoid(lam.astype(jnp.float32))
    a = jnp.exp(log_a)
    bt = jnp.sqrt(-jnp.expm1(2.0 * log_a)) * (i * u).astype(jnp.float32)

    def combine(left, right):
        a1, b1 = left
        a2, b2 = right
        return a1 * a2, a2 * b1 + b2

    _, h = lax.associative_scan(combine, (a, bt), axis=1)
    y = jax.nn.gelu(gb.astype(jnp.float32)) * h
    return rmsnorm(y.astype(xb.dtype), norm_g)


def diff_attention_group(q, k, v, positions, lam_p, norm_g, lambda_init):
    B, S, _ = q.shape
    H, dh = DA_HEADS, DA_HEAD_DIM
    q = q.reshape(B, S, H, 2, dh)
    k = k.reshape(B, S, H, 2, dh)
    v = v.reshape(B, S, H, 2 * dh)
    cos, sin = rope_tables(positions)
    cos, sin = cos[:, :, None, None, :], sin[:, :, None, None, :]
    q = apply_partial_rope(q, cos, sin)
    k = apply_partial_rope(k, cos, sin)
    lp = lam_p.astype(jnp.float32)
    lam = jnp.exp(jnp.sum(lp[0] * lp[1])) - jnp.exp(jnp.sum(lp[2] * lp[3])) + lambda_init
    nb = S // Q_BLOCK
    qb = q.reshape(B, nb, Q_BLOCK, H, 2, dh).transpose(1, 0, 2, 3, 4, 5)
    key_pos = jnp.arange(S)
    scale = dh ** -0.5

    def block(args):
        qi, bi = args
        s = jnp.einsum('bqhcd,bkhcd->bhcqk', qi, k).astype(jnp.float32) * scale
        qpos = bi * Q_BLOCK + jnp.arange(Q_BLOCK)
        mask = qpos[:, None] >= key_pos[None, :]
        p = jax.nn.softmax(jnp.where(mask, s, NEG_INF), axis=-1)
        p = p[:, :, 0] - lam * p[:, :, 1]
        return jnp.einsum('bhqk,bkhe->bqhe', p.astype(v.dtype), v)

    o = lax.map(block, (qb, jnp.arange(nb)))
    o = o.transpose(1, 0, 2, 3, 4).reshape(B, S, H, 2 * dh)
    o = rmsnorm(o, norm_g) * (1.0 - lambda_init)
    return o.reshape(B, S, DA_WIDTH)


def mlstm_chunk_step(carry, xs):
    C, n, m = carry
    qc, kc, vc, li, lf = xs
    L = qc.shape[2]
    b = jnp.cumsum(lf, axis=-1)
    causal = jnp.tril(jnp.ones((L, L), dtype=bool))
    D = jnp.where(causal, b[..., :, None] - b[..., None, :] + li[..., None, :], -jnp.inf)
    m_inter = b + m[..., None]
    m_t = jnp.maximum(m_inter, jnp.max(D, axis=-1))
    w_intra = jnp.einsum('bhtd,bhsd->bhts', qc, kc) * jnp.exp(D - m_t[..., None])
    inter = jnp.exp(m_inter - m_t)
    num = inter[..., None] * jnp.einsum('bhtd,bhde->bhte', qc, C) + jnp.einsum('bhts,bhse->bhte', w_intra, vc)
    den = inter * jnp.einsum('bhtd,bhd->bht', qc, n) + jnp.sum(w_intra, axis=-1)
    h = num / jnp.maximum(jnp.abs(den), jnp.exp(-m_t))[..., None]
    bL = b[..., -1]
    g = bL[..., None] - b + li
    m_next = jnp.maximum(bL + m, jnp.max(g, axis=-1))
    decay = jnp.exp(bL + m - m_next)
    wk = jnp.exp(g - m_next[..., None])
    C_new = decay[..., None, None] * C + jnp.einsum('bhs,bhsd,bhse->bhde', wk, kc, vc)
    n_new = decay[..., None] * n + jnp.einsum('bhs,bhsd->bhd', wk, kc)
    return (C_new, n_new, m_next), h


def mlstm_group(q, k, v, o_pre, i_pre, f_pre, conv_w, conv_b, i_bias, f_bias, norm_g):
    B, S, _ = q.shape
    H, dh = ML_HEADS, ML_HEAD_DIM
    qk = jax.nn.silu(causal_dwconv(jnp.concatenate([q, k], axis=-1), conv_w, conv_b))
    q, k = qk[..., :ML_WIDTH], qk[..., ML_WIDTH:]
    qh = q.reshape(B, S, H, dh).astype(jnp.float32)
    kh = k.reshape(B, S, H, dh).astype(jnp.float32) * (dh ** -0.5)
    vh = v.reshape(B, S, H, dh).astype(jnp.float32)
    log_i = (i_pre + i_bias).astype(jnp.float32)
    log_f = jax.nn.log_sigmoid((f_pre + f_bias).astype(jnp.float32))
    nc = S // ML_CHUNK

    def chunks4(t):
        return t.reshape(B, nc, ML_CHUNK, H, dh).transpose(1, 0, 3, 2, 4)

    def chunks3(t):
        return t.reshape(B, nc, ML_CHUNK, H).transpose(1, 0, 3, 2)

    init = (jnp.zeros((B, H, dh, dh), jnp.float32), jnp.zeros((B, H, dh), jnp.float32),
            jnp.zeros((B, H), jnp.float32))
    _, h = lax.scan(mlstm_chunk_step, init,
                    (chunks4(qh), chunks4(kh), chunks4(vh), chunks3(log_i), chunks3(log_f)))
    h = h.transpose(1, 0, 3, 2, 4).reshape(B, S, H, dh).astype(q.dtype)
    h = rmsnorm(h, norm_g.reshape(H, dh)).reshape(B, S, ML_WIDTH)
    return h * jax.nn.sigmoid(o_pre)


def conv_gated_mlp(x, w_up, conv_w, conv_b, w_down):
    u = causal_dwconv(x @ w_up, conv_w, conv_b)
    g, val = u[..., :D_FF], u[..., D_FF:]
    return (jax.nn.silu(g) * val) @ w_down


def setup_inputs(seed: int = 0) -> dict:
    key = jax.random.key(seed)
    ks = iter(jax.random.split(key, 40))
    f32 = jnp.float32

    def nrm(shape, scale):
        return jax.random.normal(next(ks), shape, f32) * scale

    def gain(shape):
        return 1.0 + nrm(shape, 0.02)

    x = jax.random.normal(next(ks), (BATCH, SEQ, D_MODEL), f32)
    offset = jax.random.randint(next(ks), (BATCH, 1), 0, 1024, dtype=jnp.int32)
    positions = offset + jnp.arange(SEQ, dtype=jnp.int32)[None, :]
    u = jax.random.uniform(next(ks), (DEPTH, RG_WIDTH), f32, 0.9, 0.999) ** (1.0 / RG_C)
    rg_lambda = jnp.log(u) - jnp.log1p(-u)
    ml_f_bias = jnp.linspace(3.0, 6.0, ML_HEADS, dtype=f32)[None, :] + nrm((DEPTH, ML_HEADS), 0.1)
    return {
        'x': x,
        'positions': positions,
        'attn_norm': gain((DEPTH, D_MODEL)),
        'w_in': nrm((DEPTH, D_MODEL, D_IN), D_MODEL ** -0.5),
        'rg_conv_w': nrm((DEPTH, RG_CONV, RG_WIDTH), RG_CONV ** -0.5),
        'rg_conv_b': nrm((DEPTH, RG_WIDTH), 0.01),
        'rg_wa': nrm((DEPTH, RG_BLOCKS, RG_BLOCK, RG_BLOCK), RG_BLOCK ** -0.5),
        'rg_ba': nrm((DEPTH, RG_WIDTH), 0.01),
        'rg_wx': nrm((DEPTH, RG_BLOCKS, RG_BLOCK, RG_BLOCK), RG_BLOCK ** -0.5),
        'rg_bx': nrm((DEPTH, RG_WIDTH), 0.01),
        'rg_lambda': rg_lambda,
        'rg_norm': gain((DEPTH, RG_WIDTH)),
        'da_lambda': nrm((DEPTH, 4, DA_HEAD_DIM), 0.1),
        'da_norm': gain((DEPTH, 2 * DA_HEAD_DIM)),
        'ml_conv_w': nrm((DEPTH, ML_CONV, 2 * ML_WIDTH), ML_CONV ** -0.5),
        'ml_conv_b': nrm((DEPTH, 2 * ML_WIDTH), 0.01),
        'ml_i_bias': nrm((DEPTH, ML_HEADS), 0.1),
        'ml_f_bias': ml_f_bias,
        'ml_norm': gain((DEPTH, ML_WIDTH)),
        'w_out': nrm((DEPTH, D_MIX, D_MODEL), D_MIX ** -0.5),
        'mlp_norm': gain((DEPTH, D_MODEL)),
        'w_up': nrm((DEPTH, D_MODEL, 2 * D_FF), D_MODEL ** -0.5),
        'ffn_conv_w': nrm((DEPTH, FFN_CONV, 2 * D_FF), FFN_CONV ** -0.5),
        'ffn_conv_b': nrm((DEPTH, 2 * D_FF), 0.01),
        'w_down': nrm((DEPTH, D_FF, D_MODEL), D_FF ** -0.5),
        'final_norm': gain((D_MODEL,)),
    }


def reference(x, positions, attn_norm, w_in, rg_conv_w, rg_conv_b, rg_wa, rg_ba, rg_wx, rg_bx,
              rg_lambda, rg_norm, da_lambda, da_norm, ml_conv_w, ml_conv_b, ml_i_bias, ml_f_bias,
              ml_norm, w_out, mlp_norm, w_up, ffn_conv_w, ffn_conv_b, w_down, final_norm):
    cuts = np.cumsum(IN_WIDTHS)[:-1].tolist()
    for l in range(DEPTH):
        lambda_init = 0.8 - 0.6 * math.exp(-0.3 * l)
        h = rmsnorm(x, attn_norm[l])
        z = h @ w_in[l]
        rg_x, rg_g, da_q, da_k, da_v, ml_q, ml_k, ml_v, ml_o, ml_i, ml_f = jnp.split(z, cuts, axis=-1)
        y_rg = rglru_group(rg_x, rg_g, rg_conv_w[l], rg_conv_b[l], rg_wa[l], rg_ba[l], rg_wx[l],
                           rg_bx[l], rg_lambda[l], rg_norm[l])
        y_da = diff_attention_group(da_q, da_k, da_v, positions, da_lambda[l], da_norm[l], lambda_init)
        y_ml = mlstm_group(ml_q, ml_k, ml_v, ml_o, ml_i, ml_f, ml_conv_w[l], ml_conv_b[l],
                           ml_i_bias[l], ml_f_bias[l], ml_norm[l])
        x = x + jnp.concatenate([y_rg, y_da, y_ml], axis=-1) @ w_out[l]
        h = rmsnorm(x, mlp_norm[l])
        x = x + conv_gated_mlp(h, w_up[l], ffn_conv_w[l], ffn_conv_b[l], w_down[l])
    return rmsnorm(x, final_norm)
```

```python
import math
import numpy as np
from contextlib import ExitStack
import concourse.bass as bass
import concourse.mybir as mybir
from concourse.bass_utils import run_bass_kernel_spmd

F32 = mybir.dt.float32
BF16 = mybir.dt.bfloat16
I32 = mybir.dt.int32
AF = mybir.ActivationFunctionType
ALU = mybir.AluOpType

S = 2048
D = 1024
NL = 2
NTT = 4
TT = 512
NB = 16
EPS = 1e-6
DFF = 2816
NJ = 22
FFN_GROUPS = [list(range(0, 6)), list(range(6, 12)), list(range(12, 17)), list(range(17, 22))]
NSLOT = 12
RG_C = 8.0

_VEC = [("g1", 8), ("g2", 8), ("rgcw", 16), ("rgcb", 4), ("rgba", 4), ("rgbx", 4), ("rglam", 4), ("rgnorm", 4),
        ("mlcw", 32), ("mlcb", 8), ("fcw", 132), ("fcb", 44), ("danorm", 1), ("mlnorm", 4),
        ("ibias", 4), ("fbias", 4), ("dalam", 256)]
VOFF = {}
_o = 0
for _n, _w in _VEC:
    VOFF[_n] = _o
    _o += _w
NV = _o
GOFF = {"fn": 0, "invf": 8, "sgn": 9}
NG = 10

OFF_IN = dict(rg_x=0, rg_g=512, da_q=1024, da_k=1536, da_v=2048, ml_q=2560, ml_k=3072, ml_v=3584, ml_o=4096,
              ml_i=4608, ml_f=4612)


def block_plan():
    plan = []
    for n in range(4):
        plan.append(("in", OFF_IN["rg_x"] + n * 128))
    for n in range(4):
        plan.append(("in", OFF_IN["rg_g"] + n * 128))
    plan.append(("gatew",))
    for pr in range(4):
        plan.append(("out", 0, pr))
    for h in range(4):
        plan.append(("in", OFF_IN["da_q"] + h * 128))
        plan.append(("in", OFF_IN["da_k"] + h * 128))
        plan.append(("in", OFF_IN["da_v"] + h * 128))
    for pr in range(4):
        plan.append(("out", 1, pr))
    plan.append(("gates",))
    for h in range(4):
        plan.append(("in", OFF_IN["ml_q"] + h * 128))
        plan.append(("in", OFF_IN["ml_k"] + h * 128))
        plan.append(("in", OFF_IN["ml_v"] + h * 128))
        plan.append(("in", OFF_IN["ml_o"] + h * 128))
    for pr in range(4):
        plan.append(("out", 2, pr))
    for G in FFN_GROUPS:
        for j in G:
            plan.append(("up", j))
            plan.append(("up", DFF // 128 * 0 + j + 1000))
        for j in G:
            plan.append(("down", j))
    return plan


PLAN = block_plan()
NBLK = len(PLAN)


def pack_blocks(inp, l):
    w_in = inp["w_in"][l]
    w_out = inp["w_out"][l]
    w_up = inp["w_up"][l]
    w_down = inp["w_down"][l]
    out = np.zeros((NBLK, 128, 1024), np.float32)

    def fm(w, c0, ncols=128):
        blk = np.zeros((8, 128, 128), np.float32)
        blk[:, :, :ncols] = w[:, c0:c0 + ncols].reshape(8, 128, ncols)
        return blk.transpose(1, 0, 2).reshape(128, 1024)

    for i, b in enumerate(PLAN):
        k = b[0]
        if k == "in":
            out[i] = fm(w_in, b[1])
        elif k == "gates":
            out[i] = fm(w_in, OFF_IN["ml_i"], 8)
        elif k == "gatew":
            wa = inp["rg_wa"][l].transpose(1, 0, 2)
            wx = inp["rg_wx"][l].transpose(1, 0, 2)
            out[i] = np.stack([wa, wx], axis=1).reshape(128, 1024)
        elif k == "out":
            g, pr = b[1], b[2]
            sub = w_out[g * 512:(g + 1) * 512, pr * 256:(pr + 1) * 256]
            out[i] = sub.reshape(4, 128, 2, 128).transpose(1, 2, 0, 3).reshape(128, 1024)
        elif k == "up":
            j = b[1]
            c0 = j * 128 if j < 1000 else DFF + (j - 1000) * 128
            out[i] = fm(w_up, c0)
        elif k == "down":
            j = b[1]
            out[i] = w_down[j * 128:(j + 1) * 128, :]
    return out


def pack_vec(inp, l):
    v = np.zeros((128, NV), np.float32)

    def put(name, arr):
        arr = np.asarray(arr, np.float32)
        v[:, VOFF[name]:VOFF[name] + arr.shape[1]] = arr

    put("g1", inp["attn_norm"][l].reshape(8, 128).T)
    put("g2", inp["mlp_norm"][l].reshape(8, 128).T)
    put("rgcw", inp["rg_conv_w"][l].reshape(4, 4, 128).transpose(2, 1, 0).reshape(128, 16))
    put("rgcb", inp["rg_conv_b"][l].reshape(4, 128).T)
    put("rgba", inp["rg_ba"][l].reshape(4, 128).T)
    put("rgbx", inp["rg_bx"][l].reshape(4, 128).T)
    put("rglam", inp["rg_lambda"][l].reshape(4, 128).T)
    put("rgnorm", inp["rg_norm"][l].reshape(4, 128).T)
    put("mlcw", inp["ml_conv_w"][l].reshape(4, 8, 128).transpose(2, 1, 0).reshape(128, 32))
    put("mlcb", inp["ml_conv_b"][l].reshape(8, 128).T)
    fw = inp["ffn_conv_w"][l]
    fb = inp["ffn_conv_b"][l]
    fcw = np.zeros((128, 44, 3), np.float32)
    fcb = np.zeros((128, 44), np.float32)
    for j in range(NJ):
        fcw[:, 2 * j, :] = fw[:, j * 128:(j + 1) * 128].T
        fcw[:, 2 * j + 1, :] = fw[:, DFF + j * 128:DFF + (j + 1) * 128].T
        fcb[:, 2 * j] = fb[j * 128:(j + 1) * 128]
        fcb[:, 2 * j + 1] = fb[DFF + j * 128:DFF + (j + 1) * 128]
    put("fcw", fcw.reshape(128, 132))
    put("fcb", fcb)
    put("danorm", inp["da_norm"][l].reshape(128, 1))
    put("mlnorm", inp["ml_norm"][l].reshape(4, 128).T)
    put("ibias", np.broadcast_to(inp["ml_i_bias"][l][None, :], (128, 4)))
    put("fbias", np.broadcast_to(inp["ml_f_bias"][l][None, :], (128, 4)))
    put("dalam", np.broadcast_to(inp["da_lambda"][l].reshape(1, 256), (128, 256)))
    return v


def const_mats():
    ident = np.eye(128, dtype=np.float32)
    perm = np.zeros((128, 128), np.float32)
    for base in (0, 64):
        for r in range(8):
            perm[base + r + 8, base + r] = 1.0
            perm[base + r, base + r + 8] = 1.0
    kk = np.arange(128)[:, None]
    qq = np.arange(128)[None, :]
    maskneg = np.where(kk > qq, -1e30, 0.0).astype(np.float32)
    tri = (qq >= kk).astype(np.float32)
    ones = np.ones((128, 128), np.float32)
    return np.stack([ident, perm, maskneg, tri, ones], axis=1)


def pack_gvec(inp):
    g = np.zeros((128, NG), np.float32)
    g[:, 0:8] = inp["final_norm"].reshape(8, 128).T
    inv = (500000.0 ** (-np.arange(0, 16, 2, dtype=np.float32) / 16.0)).astype(np.float32)
    for base in (0, 64):
        for r in range(8):
            g[base + r, 8] = inv[r]
            g[base + r + 8, 8] = inv[r]
            g[base + r, 9] = -1.0
            g[base + r + 8, 9] = 1.0
    return g


class Dep:
    __slots__ = ("w", "r", "sem", "nd", "x", "e")

    def __init__(self):
        self.x = False
        self.e = []
        self.w = None
        self.r = {}
        self.sem = None
        self.nd = 0


class Queue:
    def __init__(self, name, eng):
        self.name = name
        self.eng = eng
        self.sem = None
        self.cnt = 0
        self.seen = {}
        self.nsem = 0


class Prog:
    MAXC = 20000

    def __init__(self, nc, es):
        self.nc = nc
        self.es = es
        self.deps = {}
        self.pe = Queue("pe", nc.tensor)
        self.act = Queue("act", nc.scalar)
        self.dve = Queue("dve", nc.vector)
        self.pool = Queue("pool", nc.gpsimd)
        self.sp = Queue("sp", nc.sync)
        self.nsems = 0
        self.semkeep = []

    def newsem(self, name):
        self.nsems += 1
        h = self.es.enter_context(self.nc.semaphore(name))
        self.semkeep.append(h)
        return h

    def D(self, *key):
        d = self.deps.get(key)
        if d is None:
            d = Dep()
            self.deps[key] = d
        return d

    def _wait(self, q, toks):
        need = {}
        for t in toks:
            if t is None:
                continue
            sem, val = t
            k = id(sem)
            if q.seen.get(k, 0) >= val:
                continue
            if k not in need or need[k][1] < val:
                need[k] = (sem, val)
        for k, (sem, val) in need.items():
            q.eng.wait_ge(sem, val)
            q.seen[k] = val

    def _collect(self, R, W):
        toks = []
        for d in R:
            toks.append(d.w)
            toks.extend(d.e)
            if d.x:
                toks.extend(d.r.values())
        for d in W:
            toks.append(d.w)
            toks.extend(d.r.values())
        return toks

    def _mark(self, tok, R, W):
        k = id(tok[0])
        for d in R:
            d.r[k] = tok
        for d in W:
            d.w = tok
            d.r = {}

    def _signal(self, q, ins):
        if q.sem is None or q.cnt >= self.MAXC:
            q.nsem += 1
            q.sem = self.newsem(f"{q.name}{q.nsem}")
            q.cnt = 0
        q.cnt += 1
        ins.then_inc(q.sem, 1)
        return (q.sem, q.cnt)

    def op(self, q, fn, R=(), W=()):
        self._wait(q, self._collect(R, W))
        ins = fn(q.eng)
        tok = self._signal(q, ins)
        self._mark(tok, R, W)
        return tok

    def mm(self, mms, R=(), W=()):
        q = self.pe
        self._wait(q, self._collect(R, W))
        ins = None
        for kw in mms:
            if kw.pop("tr", False):
                ins = q.eng.transpose(kw["out"], kw["in_"], kw["identity"])
            else:
                ins = q.eng.matmul(kw["out"], lhsT=kw["lhsT"], rhs=kw["rhs"], start=kw.get("start", True),
                                   stop=kw.get("stop", True), skip_group_check=kw.get("sgc", False))
        tok = self._signal(q, ins)
        self._mark(tok, R, W)
        return tok

    def barrier(self):
        toks = []
        for key, d in self.deps.items():
            if key[0] == "wslot":
                continue
            toks.append(d.w)
            toks.extend(d.r.values())
        self._wait(self.dve, toks)
        ins = self.dve.eng.memset(self.bar_ap, 0.0)
        tok = self._signal(self.dve, ins)
        for q in (self.pe, self.act, self.pool, self.sp):
            self._wait(q, [tok])

    def dma(self, q, out, in_, semdep, R=(), W=(), **kw):
        self._wait(q, self._collect(R, W))
        ins = q.eng.dma_start(out=out, in_=in_, **kw)
        if semdep.sem is None:
            semdep.sem = self.newsem(f"dma{self.nsems}")
        semdep.nd += 1
        ins.then_inc(semdep.sem, 16)
        tok = (semdep.sem, 16 * semdep.nd)
        self._mark(tok, R, W)
        return tok


def build(nl=NL, stage="all", dbg_cols=0):
    nc = bass.Bass("TRN2", target_bir_lowering=False)
    xT_d = nc.dram_tensor("xT", [128, 8, S], F32, kind="ExternalInput").ap()
    pos_d = nc.dram_tensor("pos", [128, S], I32, kind="ExternalInput").ap()
    cm_d = nc.dram_tensor("cmat", [128, 5, 128], F32, kind="ExternalInput").ap()
    gv_d = nc.dram_tensor("gvec", [128, NG], F32, kind="ExternalInput").ap()
    vec_d = nc.dram_tensor("vec", [nl, 128, NV], F32, kind="ExternalInput").ap()
    wb_d = nc.dram_tensor("wblk", [nl * NBLK, 128, 1024], F32, kind="ExternalInput").ap()
    out_d = nc.dram_tensor("outT", [128, 8, S], F32, kind="ExternalOutput").ap()
    dbg_d = None
    if dbg_cols:
        dbg_d = nc.dram_tensor("dbg", [128, dbg_cols], F32, kind="ExternalOutput").ap()

    with ExitStack() as es:
        P = Prog(nc, es)
        D = P.D
        pe, act, dve, pool, sp = P.pe, P.act, P.dve, P.pool, P.sp

        def sb(name, shape, dt=F32):
            return es.enter_context(nc.sbuf_tensor("s_" + name, shape, dt))

        xT = sb("xT", [128, 8, S])
        hT = sb("hT", [128, 8, S], BF16)
        yT = sb("yT", [128, 4, S], BF16)
        ropeC = sb("ropeC", [128, S])
        ropeS = sb("ropeS", [128, S])
        cmb = sb("cmb", [128, 5, 128], BF16)
        cmf = sb("cmf", [128, 2, 128], F32)
        gv = sb("gv", [128, NG])
        vec = sb("vec", [128, nl, NV])
        wst = sb("wst", [128, NSLOT, 1024], BF16)
        SCR = 46 * 1024
        scr = sb("scr", [128, SCR // 4])
        ps = [es.enter_context(nc.psum_tensor(f"ps{i}", [128, 512], F32)) for i in range(7)]
        psT = es.enter_context(nc.psum_tensor("psT", [128, 1024], BF16))
        PSD = [D("ps", i) for i in range(7)]
        PSTD = D("psT")
        for d_ in PSD + [PSTD]:
            d_.x = True

        ident_b = cmb[:, 0, :]
        perm_b = cmb[:, 1, :]
        maskneg_b = cmb[:, 2, :]
        tri_b = cmb[:, 3, :]
        ones_b = cmb[:, 4, :]
        tri_f = cmf[:, 0, :]
        ones_f = cmf[:, 1, :]

        class Carver:
            def __init__(self):
                self.off = 0

            def reset(self):
                self.off = 0

            def take(self, shape, dt=F32):
                n = 1
                for s_ in shape[1:]:
                    n *= s_
                words = n if dt == F32 or dt == I32 else (n + 1) // 2
                assert self.off + words <= SCR // 4, (self.off, words)
                v = scr[:, self.off:self.off + words]
                self.off += words
                if dt == BF16:
                    v = v.bitcast(BF16)[:, 0:n]
                elif dt == I32:
                    v = v.bitcast(I32)
                if len(shape) == 3:
                    v = v.rearrange("p (a b) -> p a b", a=shape[1])
                elif len(shape) == 4:
                    v = v.rearrange("p (a b c) -> p a b c", a=shape[1], b=shape[2])
                return v

        cv = Carver()
        SCRD = D("scr")

        wstate = {"next_load": 0, "next_use": 0}
        total_blocks = nl * NBLK

        def wslotD(i):
            return D("wslot", i % NSLOT)

        def prefetch(upto):
            upto = min(upto, total_blocks)
            while wstate["next_load"] < upto:
                i = wstate["next_load"]
                P.dma(pool, wst[:, i % NSLOT, :], wb_d[i], wslotD(i), W=[wslotD(i)], max_dma_last_dim=4096)
                wstate["next_load"] += 1

        def wnext(expect=None):
            i = wstate["next_use"]
            if expect is not None:
                assert PLAN[i % NBLK][0] == expect, (PLAN[i % NBLK], expect)
            prefetch(i + 1)
            wstate["next_use"] += 1
            return wst[:, i % NSLOT, :], wslotD(i), i

        def wdone(i):
            prefetch(i + NSLOT + 1)

        dbgstate = {"off": 0}

        def dump(ap, deps, ncols, cast=False):
            o = dbgstate["off"]
            q = pool if cast else sp
            P.dma(q, dbg_d[:, o:o + ncols], ap, D("dbgout"), R=deps)
            dbgstate["off"] += ncols

        def finish():
            toks = [(d.sem, 16 * d.nd) for d in P.deps.values() if d.sem is not None]
            for q in (sp, pool):
                P._wait(q, toks)

        XD = [[D("xT", c, t) for t in range(NTT)] for c in range(8)]
        HD = [[D("hT", c, t) for t in range(NTT)] for c in range(8)]
        YD = [[D("yT", c, t) for t in range(NTT)] for c in range(4)]
        CONST = D("const")
        for c in range(8):
            xtok = P.dma(sp, xT[:, c, :], xT_d[:, c, :], D("xload"), W=[XD[c][t] for t in range(NTT)])
        for c in range(8):
            for t in range(NTT):
                XD[c][t].w = xtok
        P.dma(sp, cmf[:], cm_d[:, 3:5, :], CONST, W=[CONST])
        P.dma(sp, gv[:], gv_d, CONST, W=[CONST])
        for l in range(nl):
            P.dma(sp, vec[:, l, :], vec_d[l], CONST, W=[CONST])
        CONST.e.append(P.dma(pool, cmb[:], cm_d, D("constb")))
        prefetch(NSLOT)

        D_F = 1024.0
        epst = sb("epst", [128, 4])
        P.op(dve, lambda e: e.memset(epst[:, 0:1], EPS), W=[D("epst")])
        P.op(dve, lambda e: e.memset(epst[:, 1:2], 1.0), W=[D("epst")])
        P.op(dve, lambda e: e.memset(epst[:, 2:3], 0.0), W=[D("epst")])
        P.bar_ap = epst[:, 3:4]
        EPS_AP = epst[:, 0:1]
        ONE_AP = epst[:, 1:2]
        CONSTS = [CONST, D("epst")]
        ROPE = D("rope")
        cv.reset()
        posi = cv.take([128, S], I32)
        tA = cv.take([128, S])
        tB = cv.take([128, S])
        tK = cv.take([128, S], I32)
        P.dma(sp, posi, pos_d, D("posload"), W=[D("posi")])
        TWO_PI = 2.0 * math.pi
        C1 = 6.28125
        C2 = TWO_PI - C1
        P.op(dve, lambda e: e.tensor_copy(out=tA, in_=posi), R=[D("posi")], W=[D("tA")])
        P.op(dve, lambda e: e.tensor_scalar(out=tA, in0=tA, scalar1=gv[:, 8:9], scalar2=None, op0=ALU.mult),
             R=[CONST], W=[D("tA")])
        P.op(dve, lambda e: e.tensor_scalar(out=tK, in0=tA, scalar1=1.0 / TWO_PI, scalar2=None, op0=ALU.mult),
             R=[D("tA")], W=[D("tK")])
        P.op(dve, lambda e: e.tensor_copy(out=tB, in_=tK), R=[D("tK")], W=[D("tB")])
        P.op(dve, lambda e: e.scalar_tensor_tensor(out=tA, in0=tB, scalar=-C1, in1=tA, op0=ALU.mult, op1=ALU.add),
             R=[D("tB")], W=[D("tA")])
        P.op(dve, lambda e: e.scalar_tensor_tensor(out=tA, in0=tB, scalar=-C2, in1=tA, op0=ALU.mult, op1=ALU.add),
             R=[D("tB")], W=[D("tA")])

        def wrap(t, dname):
            P.op(dve, lambda e: e.tensor_scalar(out=tB, in0=t, scalar1=math.pi, scalar2=-TWO_PI, op0=ALU.is_gt,
                                                op1=ALU.mult), R=[D(dname)], W=[D("tB")])
            P.op(dve, lambda e: e.tensor_tensor(out=t, in0=t, in1=tB, op=ALU.add), R=[D("tB")], W=[D(dname)])
            P.op(dve, lambda e: e.tensor_scalar(out=tB, in0=t, scalar1=-math.pi, scalar2=TWO_PI, op0=ALU.is_lt,
                                                op1=ALU.mult), R=[D(dname)], W=[D("tB")])
            P.op(dve, lambda e: e.tensor_tensor(out=t, in0=t, in1=tB, op=ALU.add), R=[D("tB")], W=[D(dname)])
            P.op(dve, lambda e: e.tensor_scalar(out=t, in0=t, scalar1=3.1415925, scalar2=-3.1415925, op0=ALU.min,
                                                op1=ALU.max), R=[], W=[D(dname)])

        wrap(tA, "tA")
        P.op(act, lambda e: e.activation(out=ropeS[:], in_=tA, func=AF.Sin, scale=gv[:, 9:10]),
             R=[D("tA"), CONST], W=[ROPE])
        P.op(dve, lambda e: e.tensor_scalar(out=tA, in0=tA, scalar1=math.pi / 2, scalar2=None, op0=ALU.add),
             R=[ROPE], W=[D("tA")])
        wrap(tA, "tA")
        P.op(act, lambda e: e.activation(out=ropeC[:], in_=tA, func=AF.Sin), R=[D("tA")], W=[ROPE])
        P.barrier()

        def V(l, name, j=0, n=1):
            o = VOFF[name] + j
            return vec[:, l, o:o + n]

        def rmsnorm_to_hT(l, gname):
            cv.reset()
            sq = [cv.take([128, TT], BF16) for _ in range(4)]
            lnv = [cv.take([128, TT]) for _ in range(2)]
            rstd = [cv.take([128, TT]) for _ in range(2)]
            for t in range(NTT):
                tsl = slice(t * TT, (t + 1) * TT)
                for c in range(8):
                    b = sq[c % 4]
                    bd = D("nsq", c % 4)
                    if c % 2 == 0:
                        P.op(act, lambda e, b=b, c=c: e.activation(out=b, in_=xT[:, c, tsl], func=AF.Square),
                             R=[XD[c][t]], W=[bd])
                    else:
                        P.op(pool, lambda e, b=b, c=c: e.tensor_tensor(out=b, in0=xT[:, c, tsl], in1=xT[:, c, tsl],
                                                                          op=ALU.mult), R=[XD[c][t]], W=[bd])
                    P.mm([dict(out=ps[0][:], lhsT=ones_b, rhs=b, start=(c == 0), stop=(c == 7))],
                         R=[bd, CONST], W=[PSD[0]])
                ld = D("nln", t % 2)
                rd = D("nrs", t % 2)
                P.op(act, lambda e: e.activation(out=lnv[t % 2], in_=ps[0][:], func=AF.Ln, scale=1.0 / D_F, bias=EPS_AP),
                     R=[PSD[0]], W=[ld])
                P.op(act, lambda e: e.activation(out=rstd[t % 2], in_=lnv[t % 2], func=AF.Exp, scale=-0.5),
                     R=[ld], W=[rd])
                for c in range(8):
                    P.op(dve, lambda e, c=c: e.scalar_tensor_tensor(out=hT[:, c, tsl], in0=xT[:, c, tsl],
                                                                    scalar=V(l, gname, c), in1=rstd[t % 2],
                                                                    op0=ALU.mult, op1=ALU.mult),
                         R=[XD[c][t], rd, CONST], W=[HD[c][t]])
            P.barrier()


        def conv_A(src_ps, srcD, u, uD, halo, haloD, t, wcol, bcol, ntap):
            K1 = ntap - 1
            hm = len(haloD)
            P.op(act, lambda e: e.activation(out=u, in_=src_ps, func=AF.Identity, scale=wcol(K1), bias=bcol),
                 R=[srcD] + CONSTS, W=[uD])
            if t < NTT - 1:
                P.op(act, lambda e: e.activation(out=halo[:, t % hm, :], in_=src_ps[:, TT - K1:TT], func=AF.Copy),
                     R=[srcD], W=[haloD[t % hm]])

        def conv_B(src_ps, srcD, u, uD, halo, haloD, t, wcol, ntap):
            K1 = ntap - 1
            hm = len(haloD)
            for j in range(K1):
                sh = K1 - j
                P.op(dve, lambda e, j=j, sh=sh: e.scalar_tensor_tensor(out=u[:, sh:TT], in0=src_ps[:, 0:TT - sh],
                                                                       scalar=wcol(j), in1=u[:, sh:TT],
                                                                       op0=ALU.mult, op1=ALU.add),
                     R=[srcD] + CONSTS, W=[uD])
                if t > 0:
                    hp = halo[:, (t - 1) % hm, :]
                    P.op(dve, lambda e, j=j, sh=sh, hp=hp: e.scalar_tensor_tensor(
                        out=u[:, 0:sh], in0=hp[:, K1 - sh:K1], scalar=wcol(j), in1=u[:, 0:sh], op0=ALU.mult,
                        op1=ALU.add), R=[haloD[(t - 1) % hm]] + CONSTS, W=[uD])

        def conv_taps(src_ps, srcD, u, uD, halo, haloD, t, wcol, bcol, ntap, l):
            conv_A(src_ps, srcD, u, uD, halo, haloD, t, wcol, bcol, ntap)
            conv_B(src_ps, srcD, u, uD, halo, haloD, t, wcol, ntap)

        def wout_group(l, g):
            for pr in range(4):
                wsl, wd, wi = wnext("out")
                w4 = wsl.rearrange("p (d f c) -> p d f c", d=2, f=4)
                for d2 in range(2):
                    dc = pr * 2 + d2
                    for t in range(NTT):
                        tsl = slice(t * TT, (t + 1) * TT)
                        bk = (d2 * NTT + t) % 2
                        P.mm([dict(out=ps[bk][:], lhsT=w4[:, d2, f, :], rhs=yT[:, f, tsl], start=(f == 0),
                                   stop=(f == 3)) for f in range(4)],
                             R=[wd] + [YD[f][t] for f in range(4)], W=[PSD[bk]])
                        P.op(dve, lambda e, dc=dc, bk=bk: e.tensor_tensor(out=xT[:, dc, tsl], in0=ps[bk][:],
                                                                           in1=xT[:, dc, tsl], op=ALU.add),
                             R=[PSD[bk]], W=[XD[dc][t]])
                wdone(wi)

        def rg_group(l):
            cv.reset()
            ws = [wnext("in") for _ in range(8)]
            gw, gwd, gwi = wnext("gatew")
            gw4 = gw.rearrange("p (a n j) -> p a n j", a=2, n=4)
            nls = cv.take([128, 4])
            tmp4 = cv.take([128, 4])
            P.op(act, lambda e: e.activation(out=tmp4, in_=V(l, "rglam", 0, 4), func=AF.Exp, scale=-1.0),
                 R=CONSTS, W=[D("rgtmp4")])
            P.op(act, lambda e: e.activation(out=tmp4, in_=tmp4, func=AF.Ln, bias=ONE_AP), R=CONSTS,
                 W=[D("rgtmp4")])
            P.op(dve, lambda e: e.tensor_scalar(out=nls, in0=tmp4, scalar1=-RG_C, scalar2=None, op0=ALU.mult),
                 R=[D("rgtmp4")], W=[D("rgnls")])
            nls2 = cv.take([128, 4])
            P.op(dve, lambda e: e.tensor_scalar(out=nls2, in0=tmp4, scalar1=-2.0 * RG_C, scalar2=None,
                                                op0=ALU.mult), R=[D("rgtmp4")], W=[D("rgnls")])
            halo = [cv.take([128, 4, 3]) for _ in range(4)]
            HAL = [[D("rghalo", n, i_) for i_ in range(4)] for n in range(4)]
            hst = cv.take([128, 4, 2])
            u = [cv.take([128, TT]) for _ in range(3)]
            gg = [cv.take([128, TT]) for _ in range(3)]
            ub = [cv.take([128, TT], BF16) for _ in range(2)]
            rr = [cv.take([128, TT]) for _ in range(2)]
            ig = [cv.take([128, TT]) for _ in range(2)]
            aa = [cv.take([128, TT]) for _ in range(2)]
            hh = [cv.take([128, TT]) for _ in range(2)]
            bt = cv.take([128, TT])
            ypre = cv.take([128, 4, TT])
            ysq = [cv.take([128, TT], BF16) for _ in range(2)]
            lnv = cv.take([128, TT])
            units = [(t, n) for t in range(NTT) for n in range(4)]
            NU = len(units)

            def S1(i):
                t, n = units[i]
                tsl = slice(t * TT, (t + 1) * TT)
                k3, b = i % 3, i % 2
                xb, gb = b, 2 + b
                wx_, wxd, _ = ws[n]
                wg_, wgd, _ = ws[4 + n]
                w8x = wx_.rearrange("p (k c) -> p k c", k=8)
                w8g = wg_.rearrange("p (k c) -> p k c", k=8)
                P.mm([dict(out=ps[xb][:], lhsT=w8x[:, kc, :], rhs=hT[:, kc, tsl], start=(kc == 0), stop=(kc == 7))
                      for kc in range(8)], R=[wxd] + [HD[kc][t] for kc in range(8)], W=[PSD[xb]])
                P.mm([dict(out=ps[gb][:], lhsT=w8g[:, kc, :], rhs=hT[:, kc, tsl], start=(kc == 0), stop=(kc == 7))
                      for kc in range(8)], R=[wgd] + [HD[kc][t] for kc in range(8)], W=[PSD[gb]])
                conv_A(ps[xb][:], PSD[xb], u[k3], D("rgu", k3), halo[n], HAL[n], t,
                       lambda j, n=n: V(l, "rgcw", n * 4 + j), V(l, "rgcb", n), 4)
                P.op(act, lambda e: e.activation(out=gg[k3], in_=ps[gb][:], func=AF.Square), R=[PSD[gb]],
                     W=[D("rgg", k3)])

            def S2(i):
                t, n = units[i]
                k3, b = i % 3, i % 2
                xb, gb, rb, ib = b, 2 + b, 4, 5
                uD, gD = D("rgu", k3), D("rgg", k3)
                conv_B(ps[xb][:], PSD[xb], u[k3], uD, halo[n], HAL[n], t, lambda j, n=n: V(l, "rgcw", n * 4 + j), 4)
                P.op(dve, lambda e: e.tensor_copy(out=ub[b], in_=u[k3]), R=[uD], W=[D("rgub", b)])
                P.op(dve, lambda e: e.tensor_scalar(out=gg[k3], in0=gg[k3], scalar1=0.044715, scalar2=1.0,
                                                    op0=ALU.mult, op1=ALU.add), R=[], W=[gD])
                P.op(dve, lambda e: e.tensor_tensor(out=gg[k3], in0=ps[gb][:], in1=gg[k3], op=ALU.mult),
                     R=[PSD[gb]], W=[gD])
                P.mm([dict(out=ps[rb][:], lhsT=gw4[:, 0, n, :], rhs=ub[b])], R=[gwd, D("rgub", b)], W=[PSD[rb]])
                P.mm([dict(out=ps[ib][:], lhsT=gw4[:, 1, n, :], rhs=ub[b])], R=[gwd, D("rgub", b)], W=[PSD[ib]])
                P.op(act, lambda e: e.activation(out=gg[k3], in_=gg[k3], func=AF.Sigmoid, scale=1.5957691216057308),
                     R=[], W=[gD])
                P.op(act, lambda e: e.activation(out=rr[b], in_=ps[rb][:], func=AF.Sigmoid, bias=V(l, "rgba", n)),
                     R=[PSD[rb]] + CONSTS, W=[D("rgr", b)])
                P.op(act, lambda e: e.activation(out=ig[b], in_=ps[ib][:], func=AF.Sigmoid, bias=V(l, "rgbx", n)),
                     R=[PSD[ib]] + CONSTS, W=[D("rgi", b)])
                P.op(dve, lambda e: e.tensor_tensor(out=gg[k3], in0=ps[gb][:], in1=gg[k3], op=ALU.mult),
                     R=[PSD[gb]], W=[gD])
                P.op(pool, lambda e: e.tensor_tensor(out=ig[b], in0=ig[b], in1=u[k3], op=ALU.mult), R=[uD],
                     W=[D("rgi", b)])

            def S3(i):
                t, n = units[i]
                tsl = slice(t * TT, (t + 1) * TT)
                k3, b = i % 3, i % 2
                uD, gD = D("rgu", k3), D("rgg", k3)
                aD, bD, hD = D("rga", b), D("rgbt"), D("rgh", b)
                P.op(act, lambda e: e.activation(out=aa[b], in_=rr[b], func=AF.Exp, scale=nls[:, n:n + 1]),
                     R=[D("rgr", b), D("rgnls")], W=[aD])
                P.op(act, lambda e: e.activation(out=bt, in_=rr[b], func=AF.Exp, scale=nls2[:, n:n + 1]),
                     R=[D("rgr", b), D("rgnls")], W=[bD])
                P.op(act, lambda e: e.activation(out=bt, in_=bt, func=AF.Ln, scale=-1.0, bias=ONE_AP), R=CONSTS,
                     W=[bD])
                P.op(act, lambda e: e.activation(out=bt, in_=bt, func=AF.Exp, scale=0.5), R=[], W=[bD])
                P.op(dve, lambda e: e.tensor_tensor(out=bt, in0=bt, in1=ig[b], op=ALU.mult), R=[D("rgi", b)],
                     W=[bD])
                if t == 0:
                    init, initR = 0.0, []
                else:
                    init = hst[:, n, (t - 1) % 2:(t - 1) % 2 + 1]
                    initR = [D("rghst", n, (t - 1) % 2)]
                P.op(dve, lambda e: e.tensor_tensor_scan(out=hh[b], data0=aa[b], data1=bt, initial=init,
                                                         op0=ALU.mult, op1=ALU.add), R=[aD, bD] + initR, W=[hD])
                if t < NTT - 1:
                    P.op(pool, lambda e: e.tensor_copy(out=hst[:, n, t % 2:t % 2 + 1], in_=hh[b][:, TT - 1:TT]),
                         R=[hD], W=[D("rghst", n, t % 2)])
                yD = D("rgy", n)
                P.op(dve, lambda e: e.tensor_tensor(out=ypre[:, n, :], in0=gg[k3], in1=hh[b], op=ALU.mult),
                     R=[gD, hD], W=[yD])
                sD = D("rgysq", b)
                P.op(pool, lambda e: e.tensor_tensor(out=ysq[b], in0=ypre[:, n, :], in1=ypre[:, n, :], op=ALU.mult),
                     R=[yD], W=[sD])

            def SS(i):
                t, n = units[i]
                tsl = slice(t * TT, (t + 1) * TT)
                b = i % 2
                P.mm([dict(out=ps[6][:], lhsT=ones_b, rhs=ysq[b], start=(n == 0), stop=(n == 3))],
                     R=[D("rgysq", b), CONST], W=[PSD[6]])
                if n == 3:
                    P.op(act, lambda e: e.activation(out=lnv, in_=ps[6][:], func=AF.Ln, scale=1.0 / 512.0,
                                                     bias=EPS_AP), R=[PSD[6]] + CONSTS, W=[D("rgln")])
                    P.op(act, lambda e: e.activation(out=lnv, in_=lnv, func=AF.Exp, scale=-0.5), R=[],
                         W=[D("rgln")])
                    for n2 in range(4):
                        P.op(dve, lambda e, n2=n2: e.scalar_tensor_tensor(out=yT[:, n2, tsl], in0=ypre[:, n2, :],
                                                                          scalar=V(l, "rgnorm", n2), in1=lnv,
                                                                          op0=ALU.mult, op1=ALU.mult),
                             R=[D("rgy", n2), D("rgln")] + CONSTS, W=[YD[n2][t]])

            S1(0)
            S1(1)
            S2(0)
            for i in range(NU):
                if i + 2 < NU:
                    S1(i + 2)
                if i + 1 < NU:
                    S2(i + 1)
                if i > 0:
                    SS(i - 1)
                S3(i)
            SS(NU - 1)
            wdone(gwi)
            P.barrier()

        def da_group(l):
            lambda_init = 0.8 - 0.6 * math.exp(-0.3 * l)
            cv.reset()
            junk = cv.take([128, 128])
            s12 = cv.take([128, 2])
            nlam = cv.take([128, 1])
            dl = V(l, "dalam", 0, 256)
            LD = D("dalam_t")
            for i_ in range(2):
                P.op(dve, lambda e, i_=i_: e.tensor_tensor(out=junk[:, 0:64], in0=dl[:, i_ * 128:i_ * 128 + 64],
                                                           in1=dl[:, i_ * 128 + 64:i_ * 128 + 128], op=ALU.mult),
                     R=CONSTS, W=[LD])
                P.op(dve, lambda e, i_=i_: e.tensor_scalar(out=junk[:, 64:128], in0=junk[:, 0:64], scalar1=1.0,
                                                           scalar2=None, op0=ALU.mult, op1=ALU.add,
                                                           accum_out=s12[:, i_:i_ + 1]), R=[], W=[LD])
            P.op(act, lambda e: e.activation(out=s12, in_=s12, func=AF.Exp), R=[], W=[LD])
            P.op(dve, lambda e: e.tensor_tensor(out=nlam, in0=s12[:, 1:2], in1=s12[:, 0:1], op=ALU.subtract), R=[],
                 W=[LD])
            P.op(dve, lambda e: e.tensor_scalar(out=nlam, in0=nlam, scalar1=-lambda_init, scalar2=None, op0=ALU.add),
                 R=[], W=[LD])
            if stage == "dalam":
                dump(nlam, [LD], 1)
                return True
            qT = cv.take([128, S], BF16)
            kTz = [cv.take([128, S], BF16) for _ in range(2)]
            kT = kTz[0]
            vaug = cv.take([128, NB, 130], BF16)
            qb = [cv.take([128, TT], BF16) for _ in range(2)]
            t1 = [cv.take([128, TT]) for _ in range(2)]
            t2 = [cv.take([128, TT]) for _ in range(2)]
            PT = [[cv.take([128, TT], BF16) for _ in range(2)] for _ in range(2)]
            rrA = cv.take([128, TT])
            rrB = cv.take([128, TT])
            o1 = cv.take([128, TT])
            o2 = cv.take([128, TT])
            sqb = cv.take([128, TT], BF16)
            lnv = cv.take([128, TT])
            lbias = cv.take([128, 1])
            P.op(dve, lambda e: e.memset(lbias, math.log(1.0 - lambda_init)), W=[D("dalb")])
            psTf = psT[:].bitcast(F32)
            pending = []
            QD = [D("daq", t) for t in range(NTT)]
            KD = [D("dak", t) for t in range(NTT)]
            VD = [D("dav", g) for g in range(4)]
            P.op(dve, lambda e: e.memset(vaug[:, :, 128:129], 1.0), W=VD)
            P.op(dve, lambda e: e.memset(kTz[0][64:128, :], 0.0), W=[D("dakz")])
            P.op(dve, lambda e: e.memset(kTz[1][0:64, :], 0.0), W=[D("dakz")])
            ctr = {"rp": 0, "st": 0, "ep": 0, "tp": 0}
            for h in range(4):
                wq, wqd, _ = wnext("in")
                wk, wkd, _ = wnext("in")
                wv, wvd, wvi = wnext("in")
                wq8 = wq.rearrange("p (k c) -> p k c", k=8)
                wk8 = wk.rearrange("p (k c) -> p k c", k=8)
                wv8 = wv.rearrange("p (k c) -> p k c", k=8)
                punits = [(t, which) for t in range(NTT) for which in range(2)]
                pinfo = [(wq8, wqd, qT, QD, 0.125), (wk8, wkd, kT, KD, 1.0)]
                base = ctr["rp"]

                def prA(ui):
                    t, which = punits[ui]
                    w8, wd_, dst, dstD, scl = pinfo[which]
                    tsl = slice(t * TT, (t + 1) * TT)
                    k = (base + ui) % 2
                    pb = k
                    P.mm([dict(out=ps[pb][:], lhsT=w8[:, kc, :], rhs=hT[:, kc, tsl], start=(kc == 0),
                               stop=(kc == 7)) for kc in range(8)],
                         R=[wd_] + [HD[kc][t] for kc in range(8)], W=[PSD[pb]])
                    P.op(act, lambda e: e.activation(out=qb[k], in_=ps[pb][:], func=AF.Copy), R=[PSD[pb]],
                         W=[D("daqb", k)])

                def prB(ui):
                    t, which = punits[ui]
                    w8, wd_, dst, dstD, scl = pinfo[which]
                    tsl = slice(t * TT, (t + 1) * TT)
                    k = (base + ui) % 2
                    pb, sbk = k, 2 + k
                    P.mm([dict(out=ps[sbk][:], lhsT=perm_b, rhs=qb[k])], R=[D("daqb", k), CONST], W=[PSD[sbk]])
                    P.op(dve, lambda e: e.scalar_tensor_tensor(out=t1[k], in0=ps[pb][:], scalar=scl,
                                                               in1=ropeC[:, tsl], op0=ALU.mult, op1=ALU.mult),
                         R=[PSD[pb], ROPE], W=[D("dat1", k)])
                    P.op(dve, lambda e: e.scalar_tensor_tensor(out=t2[k], in0=ps[sbk][:], scalar=scl,
                                                               in1=ropeS[:, tsl], op0=ALU.mult, op1=ALU.mult),
                         R=[PSD[sbk], ROPE], W=[D("dat2", k)])
                    if which == 0:
                        P.op(pool, lambda e: e.tensor_tensor(out=dst[:, tsl], in0=t1[k], in1=t2[k], op=ALU.add),
                             R=[D("dat1", k), D("dat2", k)], W=[dstD[t]])
                    else:
                        for c_ in range(2):
                            pr_ = slice(c_ * 64, (c_ + 1) * 64)
                            P.op(pool, lambda e, c_=c_, pr_=pr_: e.tensor_tensor(
                                out=kTz[c_][pr_, tsl], in0=t1[k][pr_, :], in1=t2[k][pr_, :], op=ALU.add),
                                R=[D("dat1", k), D("dat2", k), D("dakz")], W=[dstD[t]])

                prA(0)
                for ui in range(len(punits)):
                    if ui + 1 < len(punits):
                        prA(ui + 1)
                    prB(ui)
                ctr["rp"] += len(punits)
                for g4 in range(4):
                    vb = 4 + (g4 % 2)
                    mms = []
                    for i in range(4):
                        tb = g4 * 4 + i
                        for kc in range(8):
                            mms.append(dict(out=ps[vb][:, i * 128:(i + 1) * 128],
                                            lhsT=hT[:, kc, tb * 128:(tb + 1) * 128], rhs=wv8[:, kc, :],
                                            start=(kc == 0), stop=(kc == 7)))
                    P.mm(mms, R=[wvd] + [HD[kc][g4] for kc in range(8)], W=[PSD[vb]])
                    P.op(act, lambda e, g4=g4, vb=vb: e.activation(
                        out=vaug[:, g4 * 4:(g4 + 1) * 4, 0:128],
                        in_=ps[vb][:].rearrange("p (a b) -> p a b", a=4), func=AF.Copy), R=[PSD[vb]], W=[VD[g4]])
                wdone(wvi)
                if stage == "daq":
                    dump(qT, QD, S, cast=True)
                    dump(kTz[0], KD, S, cast=True)
                    dump(vaug.rearrange("p a b -> p (a b)"), VD, NB * 130, cast=True)
                    return True
                for qg in range(4 if stage != "da1" else 1):
                    qsl = slice(qg * TT, (qg + 1) * TT)
                    UB = [ps[4], ps[5]]
                    RB = [ps[6], psTf]
                    UD_ = [PSD[4], PSD[5]]
                    RD_ = [PSD[6], PSTD]
                    nj = 4 * qg + 4
                    deferred = []
                    for j in range(nj):
                        r = max(0, j - 4 * qg)
                        c0 = r * 128
                        par = j % 2
                        cur = []
                        for c in range(2):
                            sbank = c * 2 + par
                            prow = slice(c * 64, (c + 1) * 64)
                            mms = [dict(out=ps[sbank][:, c0:TT], lhsT=kTz[c][:, j * 128:(j + 1) * 128],
                                        rhs=qT[:, qg * TT + c0:(qg + 1) * TT], start=True, stop=(j < 4 * qg),
                                        sgc=True)]
                            if j >= 4 * qg:
                                mms.append(dict(out=ps[sbank][:, c0:c0 + 128], lhsT=ident_b, rhs=maskneg_b,
                                                start=False, stop=True, sgc=True))
                            P.mm(mms, R=[KD[j // 4], QD[qg], CONST], W=[PSD[sbank]])
                        for c in range(2):
                            sbank = c * 2 + par
                            ptD = D("dapt", c, par)
                            P.op(act, lambda e, c=c, par=par, sbank=sbank, c0=c0: e.activation(
                                out=PT[c][par][:, c0:TT], in_=ps[sbank][:, c0:TT], func=AF.Exp),
                                R=[PSD[sbank]], W=[ptD])

                            def acc(c=c, par=par, c0=c0, j=j, ptD=ptD):
                                P.mm([dict(out=UB[c][:, c0:TT], lhsT=vaug[:, j, 0:128], rhs=PT[c][par][:, c0:TT],
                                           start=(j == 0), stop=(j == nj - 1), sgc=True)],
                                     R=[ptD, VD[j // 4]], W=[UD_[c]])
                                P.mm([dict(out=RB[c][:, c0:TT], lhsT=ones_b, rhs=PT[c][par][:, c0:TT],
                                           start=(j == 0), stop=(j == nj - 1), sgc=True)],
                                     R=[ptD, CONST], W=[RD_[c]])

                            cur.append(acc)
                        if j == 1 and pending:
                            pending.pop(0)()
                        for f_ in deferred:
                            f_()
                        deferred = cur
                    for f_ in deferred:
                        f_()

                    aD, bD, o1D, o2D = D("darrA"), D("darrB"), D("dao1"), D("dao2")
                    P.op(act, lambda e: e.activation(out=rrA, in_=RB[0][:, :], func=AF.Ln), R=[RD_[0]], W=[aD])
                    P.op(act, lambda e: e.activation(out=rrB, in_=RB[1][:, :], func=AF.Ln), R=[RD_[1]], W=[bD])
                    P.op(act, lambda e: e.activation(out=rrA, in_=rrA, func=AF.Exp, scale=-1.0), R=[], W=[aD])
                    P.op(act, lambda e: e.activation(out=rrB, in_=rrB, func=AF.Exp, scale=-1.0), R=[], W=[bD])
                    P.op(dve, lambda e: e.tensor_tensor(out=o1, in0=UB[0][:, :], in1=rrA, op=ALU.mult),
                         R=[UD_[0], aD], W=[o1D])
                    P.op(dve, lambda e: e.scalar_tensor_tensor(out=o2, in0=UB[1][:, :], scalar=nlam[:, 0:1], in1=rrB,
                                                               op0=ALU.mult, op1=ALU.mult),
                         R=[UD_[1], bD, LD], W=[o2D])

                    def tail(qg=qg, h=h, qsl=qsl):
                        o1D, o2D, sD, lD = D("dao1"), D("dao2"), D("dasq"), D("daln")
                        P.op(pool, lambda e: e.tensor_tensor(out=o1, in0=o1, in1=o2, op=ALU.add), R=[o2D], W=[o1D])
                        P.op(pool, lambda e: e.tensor_tensor(out=sqb, in0=o1, in1=o1, op=ALU.mult), R=[o1D], W=[sD])
                        P.mm([dict(out=ps[6][:, :], lhsT=ones_b, rhs=sqb)], R=[sD, CONST], W=[PSD[6]])
                        P.op(act, lambda e: e.activation(out=lnv, in_=ps[6][:, :], func=AF.Ln, scale=1.0 / 128.0,
                                                         bias=EPS_AP), R=[PSD[6]] + CONSTS, W=[lD])
                        P.op(act, lambda e: e.activation(out=lnv, in_=lnv, func=AF.Exp, scale=-0.5, bias=lbias),
                             R=[D("dalb")], W=[lD])
                        P.op(dve, lambda e: e.scalar_tensor_tensor(out=yT[:, h, qsl], in0=o1,
                                                                   scalar=V(l, "danorm", 0), in1=lnv, op0=ALU.mult,
                                                                   op1=ALU.mult),
                             R=[o1D, lD] + CONSTS, W=[YD[h][qg]])

                    pending.append(tail)
                while pending:
                    pending.pop(0)()
            P.barrier()

        def ml_group(l):
            cv.reset()
            wg, wgd, wgi = wnext("gates")
            wg8 = wg.rearrange("p (k c) -> p k c", k=8)
            mms = []
            for c in range(NB):
                for kc in range(8):
                    mms.append(dict(out=ps[0][:, c * 8:(c + 1) * 8], lhsT=hT[:, kc, c * 128:(c + 1) * 128],
                                    rhs=wg8[:, kc, 0:8], start=(kc == 0), stop=(kc == 7)))
            P.mm(mms, R=[wgd] + [HD[kc][t] for kc in range(8) for t in range(NTT)], W=[PSD[0]])
            wdone(wgi)
            gview = ps[0][:, 0:128].rearrange("p (c g) -> p g c", g=8)
            li = cv.take([128, 4, NB])
            fp = cv.take([128, 4, NB])
            aa = cv.take([128, 4, NB])
            ebL = cv.take([128, 4, NB])
            d1 = cv.take([128, 4, NB])
            d0 = cv.take([128, 4, NB])
            Em = cv.take([128, 4, NB])
            rZ = cv.take([128, 4, NB])
            dcy = cv.take([128, 4, NB + 1])
            esk = cv.take([128, 4, NB])
            bnd = cv.take([128, 4, NB])
            Emp = cv.take([128, 4, NB])
            GD = D("mlg")

            def f2(x):
                return x.rearrange("p a b -> p (a b)")

            for h in range(4):
                P.op(dve, lambda e, h=h: e.tensor_scalar(out=li[:, h, :], in0=gview[:, h, :],
                                                         scalar1=V(l, "ibias", h), scalar2=None, op0=ALU.add),
                     R=[PSD[0]] + CONSTS, W=[GD])
                P.op(dve, lambda e, h=h: e.tensor_scalar(out=fp[:, h, :], in0=gview[:, 4 + h, :],
                                                         scalar1=V(l, "fbias", h), scalar2=None, op0=ALU.add),
                     R=[PSD[0]] + CONSTS, W=[GD])
            P.op(act, lambda e: e.activation(out=f2(fp), in_=f2(fp), func=AF.Exp, scale=-1.0), R=[], W=[GD])
            P.op(act, lambda e: e.activation(out=f2(fp), in_=f2(fp), func=AF.Ln, bias=ONE_AP), R=CONSTS, W=[GD])
            P.mm([dict(out=ps[1][:, 0:64], lhsT=tri_f, rhs=f2(fp))], R=[GD, CONST], W=[PSD[1]])
            P.mm([dict(out=ps[2][:, 0:64], lhsT=ones_f, rhs=f2(fp))], R=[GD, CONST], W=[PSD[2]])
            P.op(dve, lambda e: e.tensor_tensor(out=f2(aa), in0=ps[1][:, 0:64], in1=f2(li), op=ALU.add),
                 R=[PSD[1]], W=[GD])
            P.op(act, lambda e: e.activation(out=f2(aa), in_=f2(aa), func=AF.Exp), R=[], W=[GD])
            P.mm([dict(out=ps[3][:, 0:64], lhsT=ones_f, rhs=f2(aa))], R=[GD, CONST], W=[PSD[3]])
            P.op(act, lambda e: e.activation(out=f2(ebL), in_=ps[2][:, 0:64], func=AF.Exp, scale=-1.0), R=[PSD[2]],
                 W=[GD])
            P.op(dve, lambda e: e.tensor_tensor(out=f2(d1), in0=ps[3][:, 0:64], in1=f2(ebL), op=ALU.mult),
                 R=[PSD[3]], W=[GD])
            P.op(dve, lambda e: e.tensor_tensor(out=d1[:, :, 0:1], in0=d1[:, :, 0:1], in1=ebL[:, :, 0:1], op=ALU.add),
                 R=[], W=[GD])
            P.op(dve, lambda e: e.tensor_copy(out=f2(d0), in_=f2(ebL)), R=[], W=[GD])
            P.op(dve, lambda e: e.memset(d0[:, :, 0:1], 0.0), R=[], W=[GD])
            P.op(dve, lambda e: e.tensor_tensor_scan(out=f2(Em), data0=f2(d0), data1=f2(d1), initial=0.0,
                                                     op0=ALU.mult, op1=ALU.add), R=[], W=[GD])
            P.op(dve, lambda e: e.reciprocal(out=f2(rZ), in_=f2(Em)), R=[], W=[GD])
            P.op(dve, lambda e: e.tensor_tensor(out=f2(rZ), in0=f2(rZ), in1=f2(ebL), op=ALU.mult), R=[], W=[GD])
            P.op(dve, lambda e: e.memset(Emp[:, :, 0:1], 1.0), R=[], W=[GD])
            P.op(dve, lambda e: e.tensor_copy(out=Emp[:, :, 1:NB], in_=Em[:, :, 0:NB - 1]), R=[], W=[GD])
            P.op(dve, lambda e: e.memset(dcy[:, :, NB:NB + 1], 1.0), R=[], W=[GD])
            P.op(dve, lambda e: e.tensor_tensor(out=dcy[:, :, 0:NB], in0=Emp[:, :, :], in1=rZ[:, :, :], op=ALU.mult),
                 R=[], W=[GD])
            P.op(dve, lambda e: e.scalar_tensor_tensor(out=f2(esk), in0=f2(aa), scalar=128.0 ** -0.5, in1=f2(rZ),
                                                       op0=ALU.mult, op1=ALU.mult), R=[], W=[GD])
            P.op(act, lambda e: e.activation(out=f2(bnd), in_=ps[1][:, 0:64], func=AF.Exp), R=[PSD[1]], W=[GD])
            P.op(dve, lambda e: e.tensor_tensor(out=f2(bnd), in0=f2(bnd), in1=f2(rZ), op=ALU.mult), R=[], W=[GD])

            qT = cv.take([128, S], BF16)
            kT = cv.take([128, S], BF16)
            vaug = cv.take([128, NB, 130], BF16)
            sgo = cv.take([128, NB, 128], BF16)
            u = [cv.take([128, TT]) for _ in range(3)]
            halo = [cv.take([128, 4, 3]) for _ in range(2)]
            MLH = [[D("mlhalo", w_, i_) for i_ in range(4)] for w_ in range(2)]
            WT = [cv.take([128, 128], BF16) for _ in range(2)]
            ktok = [cv.take([128, 128], BF16) for _ in range(2)]
            Cst = cv.take([128, 130])
            Cs = [cv.take([128, 130], BF16) for _ in range(2)]
            E4 = [cv.take([128, 4, 129]) for _ in range(2)]
            hn = cv.take([128, 4, 128])
            sq4 = cv.take([128, 4, 128])
            yb4 = [cv.take([128, 4, 128], BF16) for _ in range(2)]
            a4 = cv.take([128, 4])
            ss4 = cv.take([128, 4])
            QD = [D("mlq", t) for t in range(NTT)]
            KD = [D("mlk", t) for t in range(NTT)]
            VD = [D("mlv", g) for g in range(4)]
            OD = [D("mlo", g) for g in range(4)]
            P.op(dve, lambda e: e.memset(vaug[:, :, 128:129], 1.0), W=VD)
            ctr = {"u": 0, "c": 0}
            epi_parts = []
            for h in range(4):
                wq, wqd, _ = wnext("in")
                wk, wkd, _ = wnext("in")
                wv, wvd, _ = wnext("in")
                wo, wod, woi = wnext("in")
                w8 = [x.rearrange("p (k c) -> p k c", k=8) for x in (wq, wk, wv, wo)]
                punits = [(which, t) for which in range(2) for t in range(NTT)]
                pinfo = [(wqd, qT, QD), (wkd, kT, KD)]

                def pA(ui):
                    which, t = punits[ui]
                    wd_, dst, dstD = pinfo[which]
                    ch = which * 4 + h
                    tsl = slice(t * TT, (t + 1) * TT)
                    n_ = ctr["u"] + ui
                    k, pb = n_ % 3, n_ % 2
                    P.mm([dict(out=ps[pb][:], lhsT=w8[which][:, kc, :], rhs=hT[:, kc, tsl], start=(kc == 0),
                               stop=(kc == 7)) for kc in range(8)],
                         R=[wd_] + [HD[kc][t] for kc in range(8)], W=[PSD[pb]])
                    conv_A(ps[pb][:], PSD[pb], u[k], D("mlu", k), halo[which], MLH[which], t,
                           lambda j, ch=ch: V(l, "mlcw", ch * 4 + j), V(l, "mlcb", ch), 4)

                def pB(ui):
                    which, t = punits[ui]
                    wd_, dst, dstD = pinfo[which]
                    ch = which * 4 + h
                    tsl = slice(t * TT, (t + 1) * TT)
                    n_ = ctr["u"] + ui
                    k, pb = n_ % 3, n_ % 2
                    conv_B(ps[pb][:], PSD[pb], u[k], D("mlu", k), halo[which], MLH[which], t,
                           lambda j, ch=ch: V(l, "mlcw", ch * 4 + j), 4)
                    P.op(act, lambda e, k=k, dst=dst: e.activation(out=dst[:, tsl], in_=u[k], func=AF.Silu),
                         R=[D("mlu", k)], W=[dstD[t]])

                pA(0)
                for ui in range(len(punits)):
                    if ui + 1 < len(punits):
                        pA(ui + 1)
                    pB(ui)
                    if epi_parts:
                        epi_parts.pop(0)()
                ctr["u"] += len(punits)
                for g4 in range(4):
                    for which, (wd_, dD) in ((2, (wvd, VD)), (3, (wod, OD))):
                        vb = 2 + (which - 2)
                        mms = []
                        for i in range(4):
                            tb = g4 * 4 + i
                            for kc in range(8):
                                mms.append(dict(out=ps[vb][:, i * 128:(i + 1) * 128],
                                                lhsT=hT[:, kc, tb * 128:(tb + 1) * 128], rhs=w8[which][:, kc, :],
                                                start=(kc == 0), stop=(kc == 7)))
                        P.mm(mms, R=[wd_] + [HD[kc][g4] for kc in range(8)], W=[PSD[vb]])
                        if which == 2:
                            P.op(act, lambda e, g4=g4, vb=vb: e.activation(
                                out=vaug[:, g4 * 4:(g4 + 1) * 4, 0:128],
                                in_=ps[vb][:].rearrange("p (a b) -> p a b", a=4), func=AF.Copy),
                                R=[PSD[vb]], W=[dD[g4]])
                        else:
                            P.op(act, lambda e, g4=g4, vb=vb: e.activation(
                                out=sgo[:, g4 * 4:(g4 + 1) * 4, :],
                                in_=ps[vb][:].rearrange("p (a b) -> p a b", a=4), func=AF.Sigmoid),
                                R=[PSD[vb]], W=[dD[g4]])
                wdone(woi)
                CD = D("mlC")
                P.op(dve, lambda e: e.memset(Cst, 0.0), R=[], W=[CD])

                def stageA(c):
                    csl = slice(c * 128, (c + 1) * 128)
                    k = c % 2
                    eskc = f2(esk)[:, h * NB + c:h * NB + c + 1]
                    sb_ = 4 + k
                    P.mm([dict(out=ps[sb_][:, 0:128], lhsT=kT[:, csl], rhs=qT[:, csl])], R=[KD[c // 4], QD[c // 4]],
                         W=[PSD[sb_]])
                    P.op(dve, lambda e: e.scalar_tensor_tensor(out=WT[k], in0=ps[sb_][:, 0:128], scalar=eskc,
                                                               in1=tri_f, op0=ALU.mult, op1=ALU.mult),
                         R=[PSD[sb_], GD, CONST], W=[D("mlWT", k)])
                    P.mm([dict(tr=True, out=psT[:, k * 128:(k + 1) * 128], in_=kT[:, csl], identity=ident_b)],
                         R=[KD[c // 4], CONST], W=[PSTD])
                    P.op(act, lambda e: e.activation(out=ktok[k], in_=psT[:, k * 128:(k + 1) * 128], func=AF.Copy,
                                                     scale=eskc), R=[PSTD, GD], W=[D("mlktok", k)])
                    if c < NB - 1:
                        P.mm([dict(out=ps[2 + k][:, 0:129], lhsT=ktok[k], rhs=vaug[:, c, 0:129])],
                             R=[D("mlktok", k), VD[c // 4]], W=[PSD[2 + k]])

                stageA(0)
                for c in range(NB):
                    csl = slice(c * 128, (c + 1) * 128)
                    k = c % 2
                    eb = (c // 4) % 2
                    if c + 1 < NB:
                        stageA(c + 1)
                    mms = []
                    if c > 0:
                        mms.append(dict(out=ps[6][:, 0:129], lhsT=qT[:, csl], rhs=Cs[(c - 1) % 2][:, 0:129],
                                        start=True, stop=False))
                    mms.append(dict(out=ps[6][:, 0:129], lhsT=WT[k], rhs=vaug[:, c, 0:129], start=(c == 0),
                                    stop=True))
                    P.mm(mms, R=[QD[c // 4], D("mlWT", k), VD[c // 4]] + ([D("mlCs", (c - 1) % 2)] if c > 0 else []),
                         W=[PSD[6]])
                    P.op(act, lambda e, c=c, eb=eb: e.activation(out=E4[eb][:, c % 4, :], in_=ps[6][:, 0:129],
                                                                 func=AF.Copy), R=[PSD[6]], W=[D("mlE4", eb)])
                    if c < NB - 1:
                        dc_ = dcy[:, h, c:c + 1]
                        dn_ = dcy[:, h, c + 1:c + 2]
                        P.op(dve, lambda e, dc_=dc_, k=k: e.scalar_tensor_tensor(
                            out=Cst[:, 0:129], in0=Cst[:, 0:129], scalar=dc_, in1=ps[2 + k][:, 0:129],
                            op0=ALU.mult, op1=ALU.add), R=[PSD[2 + k], GD], W=[CD])
                        P.op(dve, lambda e, dn_=dn_, c=c: e.tensor_scalar(out=Cs[c % 2][:, 0:129], in0=Cst[:, 0:129],
                                                                          scalar1=dn_, scalar2=None, op0=ALU.mult),
                             R=[GD], W=[D("mlCs", c % 2)])
                    if epi_parts:
                        epi_parts.pop(0)()
                    if c % 4 == 3:
                        c0 = c - 3
                        E_ = E4[eb]
                        eD, aD, hD, qD, sD, yD = (D("mlE4", eb), D("mla4"), D("mlhn"), D("mlsq4"), D("mlss4"),
                                                  D("mlyb4", eb))

                        def p1(E_=E_, eD=eD, aD=aD, hD=hD, qD=qD, c0=c0, h=h):
                            dn4 = E_[:, :, 128]
                            P.op(dve, lambda e: e.scalar_tensor_tensor(out=a4, in0=dn4, scalar=-1.0, in1=dn4,
                                                                       op0=ALU.mult, op1=ALU.max), R=[eD], W=[aD])
                            P.op(dve, lambda e: e.tensor_tensor(out=a4, in0=a4, in1=bnd[:, h, c0:c0 + 4],
                                                                op=ALU.max), R=[GD], W=[aD])
                            P.op(dve, lambda e: e.reciprocal(out=a4, in_=a4), R=[], W=[aD])
                            P.op(dve, lambda e: e.tensor_tensor(
                                out=hn, in0=E_[:, :, 0:128], in1=a4.unsqueeze(2).to_broadcast([128, 4, 128]),
                                op=ALU.mult), R=[eD, aD], W=[hD])
                            P.op(pool, lambda e: e.tensor_tensor(out=sq4, in0=hn, in1=hn, op=ALU.mult), R=[hD],
                                 W=[qD])

                        def p2(qD=qD, sD=sD):
                            P.op(dve, lambda e: e.tensor_reduce(out=ss4, in_=sq4, axis=mybir.AxisListType.X,
                                                                op=ALU.add), R=[qD], W=[sD])
                            P.op(act, lambda e: e.activation(out=ss4, in_=ss4, func=AF.Ln, scale=1.0 / 128.0,
                                                             bias=EPS_AP), R=CONSTS, W=[sD])
                            P.op(act, lambda e: e.activation(out=ss4, in_=ss4, func=AF.Exp, scale=-0.5), R=[],
                                 W=[sD])

                        def p3(sD=sD, qD=qD, hD=hD, yD=yD, c0=c0, eb=eb, c=c):
                            P.op(dve, lambda e: e.tensor_tensor(
                                out=hn, in0=hn, in1=ss4.unsqueeze(2).to_broadcast([128, 4, 128]), op=ALU.mult),
                                R=[sD, qD], W=[hD])
                            P.op(pool, lambda e: e.tensor_tensor(out=yb4[eb], in0=hn, in1=sgo[:, c0:c0 + 4, :],
                                                                 op=ALU.mult), R=[hD, OD[c // 4]], W=[yD])

                        def p4(yD=yD, c0=c0, eb=eb, c=c, h=h):
                            P.mm([dict(tr=True, out=psT[:, 512 + ii * 128:512 + (ii + 1) * 128],
                                       in_=yb4[eb][:, ii, :], identity=ident_b) for ii in range(4)],
                                 R=[yD, CONST], W=[PSTD])
                            P.op(act, lambda e: e.activation(out=yT[:, h, c0 * 128:(c0 + 4) * 128],
                                                             in_=psT[:, 512:1024], func=AF.Copy,
                                                             scale=V(l, "mlnorm", h)),
                                 R=[PSTD] + CONSTS, W=[YD[h][c // 4]])

                        while epi_parts:
                            epi_parts.pop(0)()
                        epi_parts.extend([p1, p2, p3, p4])
                if h == 3:
                    while epi_parts:
                        epi_parts.pop(0)()
            P.barrier()

        def ffn(l):
            cv.reset()
            aT = cv.take([128, 6, S], BF16)
            NBUF = 3
            ug = [cv.take([128, TT]) for _ in range(NBUF)]
            uv = [cv.take([128, TT]) for _ in range(NBUF)]
            sg = [cv.take([128, TT]) for _ in range(NBUF)]
            hg = cv.take([128, 4, 2])
            hv = cv.take([128, 4, 2])
            HG = [D("fhg", i_) for i_ in range(4)]
            HV = [D("fhv", i_) for i_ in range(4)]
            it = 0
            for G in FFN_GROUPS:
                units = []
                for gi, j in enumerate(G):
                    wgk, wgd, _ = wnext("up")
                    wvk, wvd, wvi = wnext("up")
                    wg8 = wgk.rearrange("p (k c) -> p k c", k=8)
                    wv8 = wvk.rearrange("p (k c) -> p k c", k=8)
                    for t in range(NTT):
                        units.append((gi, j, t, wg8, wgd, wv8, wvd, wvi))

                def stA(u_, it_):
                    gi, j, t, wg8, wgd, wv8, wvd, wvi = u_
                    tsl = slice(t * TT, (t + 1) * TT)
                    k, b = it_ % NBUF, it_ % 2
                    gbk, vbk = b, 2 + b
                    P.mm([dict(out=ps[gbk][:], lhsT=wg8[:, kc, :], rhs=hT[:, kc, tsl], start=(kc == 0),
                               stop=(kc == 7)) for kc in range(8)],
                         R=[wgd] + [HD[kc][t] for kc in range(8)], W=[PSD[gbk]])
                    P.mm([dict(out=ps[vbk][:], lhsT=wv8[:, kc, :], rhs=hT[:, kc, tsl], start=(kc == 0),
                               stop=(kc == 7)) for kc in range(8)],
                         R=[wvd] + [HD[kc][t] for kc in range(8)], W=[PSD[vbk]])
                    if t == NTT - 1:
                        wdone(wvi)
                    conv_A(ps[gbk][:], PSD[gbk], ug[k], D("fug", k), hg, HG, t,
                           lambda jj, j=j: V(l, "fcw", (2 * j) * 3 + jj), V(l, "fcb", 2 * j), 3)
                    conv_A(ps[vbk][:], PSD[vbk], uv[k], D("fuv", k), hv, HV, t,
                           lambda jj, j=j: V(l, "fcw", (2 * j + 1) * 3 + jj), V(l, "fcb", 2 * j + 1), 3)

                def stBC(u_, it_):
                    gi, j, t, wg8, wgd, wv8, wvd, wvi = u_
                    tsl = slice(t * TT, (t + 1) * TT)
                    k, b = it_ % NBUF, it_ % 2
                    gbk, vbk = b, 2 + b
                    conv_B(ps[gbk][:], PSD[gbk], ug[k], D("fug", k), hg, HG, t,
                           lambda jj, j=j: V(l, "fcw", (2 * j) * 3 + jj), 3)
                    conv_B(ps[vbk][:], PSD[vbk], uv[k], D("fuv", k), hv, HV, t,
                           lambda jj, j=j: V(l, "fcw", (2 * j + 1) * 3 + jj), 3)
                    P.op(act, lambda e, k=k: e.activation(out=sg[k], in_=ug[k], func=AF.Silu), R=[D("fug", k)],
                         W=[D("fsg", k)])
                    P.op(pool, lambda e, k=k, gi=gi: e.tensor_tensor(out=aT[:, gi, tsl], in0=sg[k], in1=uv[k],
                                                                      op=ALU.mult),
                         R=[D("fsg", k), D("fuv", k)], W=[D("faT", gi, t)])

                stA(units[0], it)
                for ui in range(len(units)):
                    if ui + 1 < len(units):
                        stA(units[ui + 1], it + ui + 1)
                    stBC(units[ui], it + ui)
                it += len(units)
                wds = [wnext("down") for _ in G]
                for dc in range(8):
                    for t in range(NTT):
                        tsl = slice(t * TT, (t + 1) * TT)
                        bk = 4 + (dc * NTT + t) % 2
                        P.mm([dict(out=ps[bk][:], lhsT=wds[gi][0][:, dc * 128:(dc + 1) * 128], rhs=aT[:, gi, tsl],
                                   start=(gi == 0), stop=(gi == len(G) - 1)) for gi in range(len(G))],
                             R=[w_[1] for w_ in wds] + [D("faT", gi, t) for gi in range(len(G))], W=[PSD[bk]])
                        P.op(dve, lambda e, dc=dc, bk=bk: e.tensor_tensor(out=xT[:, dc, tsl], in0=ps[bk][:],
                                                                           in1=xT[:, dc, tsl], op=ALU.add),
                             R=[PSD[bk]], W=[XD[dc][t]])
                wdone(wds[-1][2])
            P.barrier()

        def final_norm():
            cv.reset()
            sq = [cv.take([128, TT], BF16) for _ in range(4)]
            lnv = [cv.take([128, TT]) for _ in range(2)]
            rstd = [cv.take([128, TT]) for _ in range(2)]
            ost = [cv.take([128, 8, TT]) for _ in range(2)]
            OUTD = D("outd")
            for t in range(NTT):
                tsl = slice(t * TT, (t + 1) * TT)
                for c in range(8):
                    b = sq[c % 4]
                    bd = D("nsq", c % 4)
                    P.op(act, lambda e, b=b, c=c: e.activation(out=b, in_=xT[:, c, tsl], func=AF.Square),
                         R=[XD[c][t]], W=[bd])
                    P.mm([dict(out=ps[0][:], lhsT=ones_b, rhs=b, start=(c == 0), stop=(c == 7))],
                         R=[bd, CONST], W=[PSD[0]])
                ld = D("nln", t % 2)
                rd = D("nrs", t % 2)
                P.op(act, lambda e: e.activation(out=lnv[t % 2], in_=ps[0][:], func=AF.Ln, scale=1.0 / D_F,
                                                 bias=EPS_AP), R=[PSD[0]] + CONSTS, W=[ld])
                P.op(act, lambda e: e.activation(out=rstd[t % 2], in_=lnv[t % 2], func=AF.Exp, scale=-0.5),
                     R=[ld], W=[rd])
                oD = D("ost", t % 2)
                for c in range(8):
                    P.op(dve, lambda e, c=c: e.scalar_tensor_tensor(out=ost[t % 2][:, c, :], in0=xT[:, c, tsl],
                                                                    scalar=gv[:, c:c + 1], in1=rstd[t % 2],
                                                                    op0=ALU.mult, op1=ALU.mult),
                         R=[XD[c][t], rd, CONST], W=[oD])
                P.dma(sp, out_d[:, :, tsl], ost[t % 2], D("outd", t % 2), R=[oD])

        for l in range(nl):
            last = (l == nl - 1)
            rmsnorm_to_hT(l, "g1")
            if stage == "big" and last:
                for rep in range(200):
                    P.mm([dict(out=ps[5][:, 0:128], lhsT=ident_b, rhs=ones_b) for _ in range(125)],
                         R=[CONST], W=[PSD[5]])
                stage = "n1"
            if stage == "n1" and last:
                dump(hT[:, :, :].rearrange("p c s -> p (c s)"), [HD[c][t] for c in range(8) for t in range(NTT)],
                     8 * S, cast=True)
                finish()
                return nc
            rg_group(l)
            if stage == "rg" and last:
                dump(yT[:, :, :].rearrange("p c s -> p (c s)"), [YD[c][t] for c in range(4) for t in range(NTT)],
                     4 * S, cast=True)
                finish()
                return nc
            wout_group(l, 0)
            if stage == "wo0" and last:
                dump(xT[:, :, :].rearrange("p c s -> p (c s)"), [XD[c][t] for c in range(8) for t in range(NTT)],
                     8 * S)
                finish()
                return nc
            if da_group(l):
                finish()
                return nc
            if stage in ("da", "da1") and last:
                dump(yT[:, :, :].rearrange("p c s -> p (c s)"), [YD[c][t] for c in range(4) for t in range(NTT)],
                     4 * S, cast=True)
                finish()
                return nc
            wout_group(l, 1)
            ml_group(l)
            if stage == "ml" and last:
                dump(yT[:, :, :].rearrange("p c s -> p (c s)"), [YD[c][t] for c in range(4) for t in range(NTT)],
                     4 * S, cast=True)
                finish()
                return nc
            wout_group(l, 2)
            if stage == "x1" and last:
                dump(xT[:, :, :].rearrange("p c s -> p (c s)"), [XD[c][t] for c in range(8) for t in range(NTT)],
                     8 * S)
                finish()
                return nc
            rmsnorm_to_hT(l, "g2")
            ffn(l)
            if stage == "x2" and last:
                dump(xT[:, :, :].rearrange("p c s -> p (c s)"), [XD[c][t] for c in range(8) for t in range(NTT)],
                     8 * S)
                finish()
                return nc
        final_norm()
        finish()
    return nc


_CACHE = {}


def make_in_maps(inputs, nl=NL, ncores=8):
    inp = {k: np.asarray(v) for k, v in inputs.items()}
    cm = const_mats()
    gvv = pack_gvec(inp)
    vecs = np.stack([pack_vec(inp, l) for l in range(nl)], axis=0)
    wblk = np.concatenate([pack_blocks(inp, l) for l in range(nl)], axis=0)
    maps = []
    for b in range(ncores):
        xb = np.ascontiguousarray(inp["x"][b].T.reshape(8, 128, S).transpose(1, 0, 2))
        posb = np.ascontiguousarray(np.broadcast_to(inp["positions"][b][None, :].astype(np.int32), (128, S)))
        maps.append({"xT": xb, "pos": posb, "cmat": cm, "gvec": gvv, "vec": vecs, "wblk": wblk})
    return maps


def kernel(**inputs):
    if "nc" not in _CACHE:
        _CACHE["nc"] = build()
    nc = _CACHE["nc"]
    maps = make_in_maps(inputs)
    res = run_bass_kernel_spmd(nc, maps, core_ids=list(range(8)))
    outs = []
    for b in range(8):
        o = res.results[b]["outT"]
        outs.append(o.transpose(2, 1, 0).reshape(S, 1024))
    return np.stack(outs, axis=0).astype(np.float32)
```

```python
import math
import numpy as np
from contextlib import ExitStack
import concourse.bass as bass
import concourse.mybir as mybir
from concourse.bass_utils import run_bass_kernel_spmd

F32 = mybir.dt.float32
BF16 = mybir.dt.bfloat16
I32 = mybir.dt.int32
AF = mybir.ActivationFunctionType
ALU = mybir.AluOpType

S = 2048
D = 1024
NL = 2
NTT = 4
TT = 512
NB = 16
EPS = 1e-6
DFF = 2816
NJ = 22
FFN_GROUPS = [list(range(0, 6)), list(range(6, 12)), list(range(12, 17)), list(range(17, 22))]
NSLOT = 12
RG_C = 8.0

_VEC = [("g1", 8), ("g2", 8), ("rgcw", 16), ("rgcb", 4), ("rgba", 4), ("rgbx", 4), ("rglam", 4), ("rgnorm", 4),
        ("mlcw", 32), ("mlcb", 8), ("fcw", 132), ("fcb", 44), ("danorm", 1), ("mlnorm", 4),
        ("ibias", 4), ("fbias", 4), ("dalam", 256)]
VOFF = {}
_o = 0
for _n, _w in _VEC:
    VOFF[_n] = _o
    _o += _w
NV = _o
GOFF = {"fn": 0, "invf": 8, "sgn": 9}
NG = 10

OFF_IN = dict(rg_x=0, rg_g=512, da_q=1024, da_k=1536, da_v=2048, ml_q=2560, ml_k=3072, ml_v=3584, ml_o=4096,
              ml_i=4608, ml_f=4612)


def block_plan():
    plan = []
    for n in range(4):
        plan.append(("in", OFF_IN["rg_x"] + n * 128))
    for n in range(4):
        plan.append(("in", OFF_IN["rg_g"] + n * 128))
    plan.append(("gatew",))
    for pr in range(4):
        plan.append(("out", 0, pr))
    for h in range(4):
        plan.append(("in", OFF_IN["da_q"] + h * 128))
        plan.append(("in", OFF_IN["da_k"] + h * 128))
        plan.append(("in", OFF_IN["da_v"] + h * 128))
    for pr in range(4):
        plan.append(("out", 1, pr))
    plan.append(("gates",))
    for h in range(4):
        plan.append(("in", OFF_IN["ml_q"] + h * 128))
        plan.append(("in", OFF_IN["ml_k"] + h * 128))
        plan.append(("in", OFF_IN["ml_v"] + h * 128))
        plan.append(("in", OFF_IN["ml_o"] + h * 128))
    for pr in range(4):
        plan.append(("out", 2, pr))
    for G in FFN_GROUPS:
        for j in G:
            plan.append(("up", j))
            plan.append(("up", DFF // 128 * 0 + j + 1000))
        for j in G:
            plan.append(("down", j))
    return plan


PLAN = block_plan()
NBLK = len(PLAN)


def pack_blocks(inp, l):
    w_in = inp["w_in"][l]
    w_out = inp["w_out"][l]
    w_up = inp["w_up"][l]
    w_down = inp["w_down"][l]
    out = np.zeros((NBLK, 128, 1024), np.float32)

    def fm(w, c0, ncols=128):
        blk = np.zeros((8, 128, 128), np.float32)
        blk[:, :, :ncols] = w[:, c0:c0 + ncols].reshape(8, 128, ncols)
        return blk.transpose(1, 0, 2).reshape(128, 1024)

    for i, b in enumerate(PLAN):
        k = b[0]
        if k == "in":
            out[i] = fm(w_in, b[1])
        elif k == "gates":
            out[i] = fm(w_in, OFF_IN["ml_i"], 8)
        elif k == "gatew":
            wa = inp["rg_wa"][l].transpose(1, 0, 2)
            wx = inp["rg_wx"][l].transpose(1, 0, 2)
            out[i] = np.stack([wa, wx], axis=1).reshape(128, 1024)
        elif k == "out":
            g, pr = b[1], b[2]
            sub = w_out[g * 512:(g + 1) * 512, pr * 256:(pr + 1) * 256]
            out[i] = sub.reshape(4, 128, 2, 128).transpose(1, 2, 0, 3).reshape(128, 1024)
        elif k == "up":
            j = b[1]
            c0 = j * 128 if j < 1000 else DFF + (j - 1000) * 128
            out[i] = fm(w_up, c0)
        elif k == "down":
            j = b[1]
            out[i] = w_down[j * 128:(j + 1) * 128, :]
    return out


def pack_vec(inp, l):
    v = np.zeros((128, NV), np.float32)

    def put(name, arr):
        arr = np.asarray(arr, np.float32)
        v[:, VOFF[name]:VOFF[name] + arr.shape[1]] = arr

    put("g1", inp["attn_norm"][l].reshape(8, 128).T)
    put("g2", inp["mlp_norm"][l].reshape(8, 128).T)
    put("rgcw", inp["rg_conv_w"][l].reshape(4, 4, 128).transpose(2, 1, 0).reshape(128, 16))
    put("rgcb", inp["rg_conv_b"][l].reshape(4, 128).T)
    put("rgba", inp["rg_ba"][l].reshape(4, 128).T)
    put("rgbx", inp["rg_bx"][l].reshape(4, 128).T)
    put("rglam", inp["rg_lambda"][l].reshape(4, 128).T)
    put("rgnorm", inp["rg_norm"][l].reshape(4, 128).T)
    put("mlcw", inp["ml_conv_w"][l].reshape(4, 8, 128).transpose(2, 1, 0).reshape(128, 32))
    put("mlcb", inp["ml_conv_b"][l].reshape(8, 128).T)
    fw = inp["ffn_conv_w"][l]
    fb = inp["ffn_conv_b"][l]
    fcw = np.zeros((128, 44, 3), np.float32)
    fcb = np.zeros((128, 44), np.float32)
    for j in range(NJ):
        fcw[:, 2 * j, :] = fw[:, j * 128:(j + 1) * 128].T
        fcw[:, 2 * j + 1, :] = fw[:, DFF + j * 128:DFF + (j + 1) * 128].T
        fcb[:, 2 * j] = fb[j * 128:(j + 1) * 128]
        fcb[:, 2 * j + 1] = fb[DFF + j * 128:DFF + (j + 1) * 128]
    put("fcw", fcw.reshape(128, 132))
    put("fcb", fcb)
    put("danorm", inp["da_norm"][l].reshape(128, 1))
    put("mlnorm", inp["ml_norm"][l].reshape(4, 128).T)
    put("ibias", np.broadcast_to(inp["ml_i_bias"][l][None, :], (128, 4)))
    put("fbias", np.broadcast_to(inp["ml_f_bias"][l][None, :], (128, 4)))
    put("dalam", np.broadcast_to(inp["da_lambda"][l].reshape(1, 256), (128, 256)))
    return v


def const_mats():
    ident = np.eye(128, dtype=np.float32)
    perm = np.zeros((128, 128), np.float32)
    for base in (0, 64):
        for r in range(8):
            perm[base + r + 8, base + r] = 1.0
            perm[base + r, base + r + 8] = 1.0
    kk = np.arange(128)[:, None]
    qq = np.arange(128)[None, :]
    maskneg = np.where(kk > qq, -1e30, 0.0).astype(np.float32)
    tri = (qq >= kk).astype(np.float32)
    ones = np.ones((128, 128), np.float32)
    return np.stack([ident, perm, maskneg, tri, ones], axis=1)


def pack_gvec(inp):
    g = np.zeros((128, NG), np.float32)
    g[:, 0:8] = inp["final_norm"].reshape(8, 128).T
    inv = (500000.0 ** (-np.arange(0, 16, 2, dtype=np.float32) / 16.0)).astype(np.float32)
    for base in (0, 64):
        for r in range(8):
            g[base + r, 8] = inv[r]
            g[base + r + 8, 8] = inv[r]
            g[base + r, 9] = -1.0
            g[base + r + 8, 9] = 1.0
    return g


class Dep:
    __slots__ = ("w", "r", "sem", "nd", "x", "e")

    def __init__(self):
        self.x = False
        self.e = []
        self.w = None
        self.r = {}
        self.sem = None
        self.nd = 0


class Queue:
    def __init__(self, name, eng):
        self.name = name
        self.eng = eng
        self.sem = None
        self.cnt = 0
        self.seen = {}
        self.nsem = 0


class Prog:
    MAXC = 20000

    def __init__(self, nc, es):
        self.nc = nc
        self.es = es
        self.deps = {}
        self.pe = Queue("pe", nc.tensor)
        self.act = Queue("act", nc.scalar)
        self.dve = Queue("dve", nc.vector)
        self.pool = Queue("pool", nc.gpsimd)
        self.sp = Queue("sp", nc.sync)
        self.nsems = 0
        self.semkeep = []

    def newsem(self, name):
        self.nsems += 1
        h = self.es.enter_context(self.nc.semaphore(name))
        self.semkeep.append(h)
        return h

    def D(self, *key):
        d = self.deps.get(key)
        if d is None:
            d = Dep()
            self.deps[key] = d
        return d

    def _wait(self, q, toks):
        need = {}
        for t in toks:
            if t is None:
                continue
            sem, val = t
            k = id(sem)
            if q.seen.get(k, 0) >= val:
                continue
            if k not in need or need[k][1] < val:
                need[k] = (sem, val)
        for k, (sem, val) in need.items():
            q.eng.wait_ge(sem, val)
            q.seen[k] = val

    def _collect(self, R, W):
        toks = []
        for d in R:
            toks.append(d.w)
            toks.extend(d.e)
            if d.x:
                toks.extend(d.r.values())
        for d in W:
            toks.append(d.w)
            toks.extend(d.r.values())
        return toks

    def _mark(self, tok, R, W):
        k = id(tok[0])
        for d in R:
            d.r[k] = tok
        for d in W:
            d.w = tok
            d.r = {}

    def _signal(self, q, ins):
        if q.sem is None or q.cnt >= self.MAXC:
            q.nsem += 1
            q.sem = self.newsem(f"{q.name}{q.nsem}")
            q.cnt = 0
        q.cnt += 1
        ins.then_inc(q.sem, 1)
        return (q.sem, q.cnt)

    def op(self, q, fn, R=(), W=()):
        self._wait(q, self._collect(R, W))
        ins = fn(q.eng)
        tok = self._signal(q, ins)
        self._mark(tok, R, W)
        return tok

    def mm(self, mms, R=(), W=()):
        q = self.pe
        self._wait(q, self._collect(R, W))
        ins = None
        for kw in mms:
            if kw.pop("tr", False):
                ins = q.eng.transpose(kw["out"], kw["in_"], kw["identity"])
            else:
                ins = q.eng.matmul(kw["out"], lhsT=kw["lhsT"], rhs=kw["rhs"], start=kw.get("start", True),
                                   stop=kw.get("stop", True), skip_group_check=kw.get("sgc", False))
        tok = self._signal(q, ins)
        self._mark(tok, R, W)
        return tok

    def barrier(self):
        toks = []
        for key, d in self.deps.items():
            if key[0] == "wslot":
                continue
            toks.append(d.w)
            toks.extend(d.r.values())
        self._wait(self.dve, toks)
        ins = self.dve.eng.memset(self.bar_ap, 0.0)
        tok = self._signal(self.dve, ins)
        for q in (self.pe, self.act, self.pool, self.sp):
            self._wait(q, [tok])

    def dma(self, q, out, in_, semdep, R=(), W=(), **kw):
        self._wait(q, self._collect(R, W))
        ins = q.eng.dma_start(out=out, in_=in_, **kw)
        if semdep.sem is None:
            semdep.sem = self.newsem(f"dma{self.nsems}")
        semdep.nd += 1
        ins.then_inc(semdep.sem, 16)
        tok = (semdep.sem, 16 * semdep.nd)
        self._mark(tok, R, W)
        return tok


def build(nl=NL, stage="all", dbg_cols=0):
    nc = bass.Bass("TRN2", target_bir_lowering=False)
    xT_d = nc.dram_tensor("xT", [128, 8, S], F32, kind="ExternalInput").ap()
    pos_d = nc.dram_tensor("pos", [128, S], I32, kind="ExternalInput").ap()
    cm_d = nc.dram_tensor("cmat", [128, 5, 128], F32, kind="ExternalInput").ap()
    gv_d = nc.dram_tensor("gvec", [128, NG], F32, kind="ExternalInput").ap()
    vec_d = nc.dram_tensor("vec", [nl, 128, NV], F32, kind="ExternalInput").ap()
    wb_d = nc.dram_tensor("wblk", [nl * NBLK, 128, 1024], F32, kind="ExternalInput").ap()
    out_d = nc.dram_tensor("outT", [128, 8, S], F32, kind="ExternalOutput").ap()
    dbg_d = None
    if dbg_cols:
        dbg_d = nc.dram_tensor("dbg", [128, dbg_cols], F32, kind="ExternalOutput").ap()

    with ExitStack() as es:
        P = Prog(nc, es)
        D = P.D
        pe, act, dve, pool, sp = P.pe, P.act, P.dve, P.pool, P.sp

        def sb(name, shape, dt=F32):
            return es.enter_context(nc.sbuf_tensor("s_" + name, shape, dt))

        xT = sb("xT", [128, 8, S])
        hT = sb("hT", [128, 8, S], BF16)
        yT = sb("yT", [128, 4, S], BF16)
        ropeC = sb("ropeC", [128, S])
        ropeS = sb("ropeS", [128, S])
        cmb = sb("cmb", [128, 5, 128], BF16)
        cmf = sb("cmf", [128, 2, 128], F32)
        gv = sb("gv", [128, NG])
        vec = sb("vec", [128, nl, NV])
        wst = sb("wst", [128, NSLOT, 1024], BF16)
        SCR = 46 * 1024
        scr = sb("scr", [128, SCR // 4])
        ps = [es.enter_context(nc.psum_tensor(f"ps{i}", [128, 512], F32)) for i in range(7)]
        psT = es.enter_context(nc.psum_tensor("psT", [128, 1024], BF16))
        PSD = [D("ps", i) for i in range(7)]
        PSTD = D("psT")
        for d_ in PSD + [PSTD]:
            d_.x = True

        ident_b = cmb[:, 0, :]
        perm_b = cmb[:, 1, :]
        maskneg_b = cmb[:, 2, :]
        tri_b = cmb[:, 3, :]
        ones_b = cmb[:, 4, :]
        tri_f = cmf[:, 0, :]
        ones_f = cmf[:, 1, :]

        class Carver:
            def __init__(self):
                self.off = 0

            def reset(self):
                self.off = 0

            def take(self, shape, dt=F32):
                n = 1
                for s_ in shape[1:]:
                    n *= s_
                words = n if dt == F32 or dt == I32 else (n + 1) // 2
                assert self.off + words <= SCR // 4, (self.off, words)
                v = scr[:, self.off:self.off + words]
                self.off += words
                if dt == BF16:
                    v = v.bitcast(BF16)[:, 0:n]
                elif dt == I32:
                    v = v.bitcast(I32)
                if len(shape) == 3:
                    v = v.rearrange("p (a b) -> p a b", a=shape[1])
                elif len(shape) == 4:
                    v = v.rearrange("p (a b c) -> p a b c", a=shape[1], b=shape[2])
                return v

        cv = Carver()
        SCRD = D("scr")

        wstate = {"next_load": 0, "next_use": 0}
        total_blocks = nl * NBLK

        def wslotD(i):
            return D("wslot", i % NSLOT)

        def prefetch(upto):
            upto = min(upto, total_blocks)
            while wstate["next_load"] < upto:
                i = wstate["next_load"]
                P.dma(pool, wst[:, i % NSLOT, :], wb_d[i], wslotD(i), W=[wslotD(i)], max_dma_last_dim=4096)
                wstate["next_load"] += 1

        def wnext(expect=None):
            i = wstate["next_use"]
            if expect is not None:
                assert PLAN[i % NBLK][0] == expect, (PLAN[i % NBLK], expect)
            prefetch(i + 1)
            wstate["next_use"] += 1
            return wst[:, i % NSLOT, :], wslotD(i), i

        def wdone(i):
            prefetch(i + NSLOT + 1)

        dbgstate = {"off": 0}

        def dump(ap, deps, ncols, cast=False):
            o = dbgstate["off"]
            q = pool if cast else sp
            P.dma(q, dbg_d[:, o:o + ncols], ap, D("dbgout"), R=deps)
            dbgstate["off"] += ncols

        def finish():
            toks = [(d.sem, 16 * d.nd) for d in P.deps.values() if d.sem is not None]
            for q in (sp, pool):
                P._wait(q, toks)

        XD = [[D("xT", c, t) for t in range(NTT)] for c in range(8)]
        HD = [[D("hT", c, t) for t in range(NTT)] for c in range(8)]
        YD = [[D("yT", c, t) for t in range(NTT)] for c in range(4)]
        CONST = D("const")
        for c in range(8):
            xtok = P.dma(sp, xT[:, c, :], xT_d[:, c, :], D("xload"), W=[XD[c][t] for t in range(NTT)])
        for c in range(8):
            for t in range(NTT):
                XD[c][t].w = xtok
        P.dma(sp, cmf[:], cm_d[:, 3:5, :], CONST, W=[CONST])
        P.dma(sp, gv[:], gv_d, CONST, W=[CONST])
        for l in range(nl):
            P.dma(sp, vec[:, l, :], vec_d[l], CONST, W=[CONST])
        CONST.e.append(P.dma(pool, cmb[:], cm_d, D("constb")))
        prefetch(NSLOT)

        D_F = 1024.0
        epst = sb("epst", [128, 4])
        P.op(dve, lambda e: e.memset(epst[:, 0:1], EPS), W=[D("epst")])
        P.op(dve, lambda e: e.memset(epst[:, 1:2], 1.0), W=[D("epst")])
        P.op(dve, lambda e: e.memset(epst[:, 2:3], 0.0), W=[D("epst")])
        P.bar_ap = epst[:, 3:4]
        EPS_AP = epst[:, 0:1]
        ONE_AP = epst[:, 1:2]
        CONSTS = [CONST, D("epst")]
        ROPE = D("rope")
        cv.reset()
        posi = cv.take([128, S], I32)
        tA = cv.take([128, S])
        tB = cv.take([128, S])
        tK = cv.take([128, S], I32)
        P.dma(sp, posi, pos_d, D("posload"), W=[D("posi")])
        TWO_PI = 2.0 * math.pi
        C1 = 6.28125
        C2 = TWO_PI - C1
        P.op(dve, lambda e: e.tensor_copy(out=tA, in_=posi), R=[D("posi")], W=[D("tA")])
        P.op(dve, lambda e: e.tensor_scalar(out=tA, in0=tA, scalar1=gv[:, 8:9], scalar2=None, op0=ALU.mult),
             R=[CONST], W=[D("tA")])
        P.op(dve, lambda e: e.tensor_scalar(out=tK, in0=tA, scalar1=1.0 / TWO_PI, scalar2=None, op0=ALU.mult),
             R=[D("tA")], W=[D("tK")])
        P.op(dve, lambda e: e.tensor_copy(out=tB, in_=tK), R=[D("tK")], W=[D("tB")])
        P.op(dve, lambda e: e.scalar_tensor_tensor(out=tA, in0=tB, scalar=-C1, in1=tA, op0=ALU.mult, op1=ALU.add),
             R=[D("tB")], W=[D("tA")])
        P.op(dve, lambda e: e.scalar_tensor_tensor(out=tA, in0=tB, scalar=-C2, in1=tA, op0=ALU.mult, op1=ALU.add),
             R=[D("tB")], W=[D("tA")])

        def wrap(t, dname):
            P.op(dve, lambda e: e.tensor_scalar(out=tB, in0=t, scalar1=math.pi, scalar2=-TWO_PI, op0=ALU.is_gt,
                                                op1=ALU.mult), R=[D(dname)], W=[D("tB")])
            P.op(dve, lambda e: e.tensor_tensor(out=t, in0=t, in1=tB, op=ALU.add), R=[D("tB")], W=[D(dname)])
            P.op(dve, lambda e: e.tensor_scalar(out=tB, in0=t, scalar1=-math.pi, scalar2=TWO_PI, op0=ALU.is_lt,
                                                op1=ALU.mult), R=[D(dname)], W=[D("tB")])
            P.op(dve, lambda e: e.tensor_tensor(out=t, in0=t, in1=tB, op=ALU.add), R=[D("tB")], W=[D(dname)])
            P.op(dve, lambda e: e.tensor_scalar(out=t, in0=t, scalar1=3.1415925, scalar2=-3.1415925, op0=ALU.min,
                                                op1=ALU.max), R=[], W=[D(dname)])

        wrap(tA, "tA")
        P.op(act, lambda e: e.activation(out=ropeS[:], in_=tA, func=AF.Sin, scale=gv[:, 9:10]),
             R=[D("tA"), CONST], W=[ROPE])
        P.op(dve, lambda e: e.tensor_scalar(out=tA, in0=tA, scalar1=math.pi / 2, scalar2=None, op0=ALU.add),
             R=[ROPE], W=[D("tA")])
        wrap(tA, "tA")
        P.op(act, lambda e: e.activation(out=ropeC[:], in_=tA, func=AF.Sin), R=[D("tA")], W=[ROPE])
        P.barrier()

        def V(l, name, j=0, n=1):
            o = VOFF[name] + j
            return vec[:, l, o:o + n]

        def rmsnorm_to_hT(l, gname):
            cv.reset()
            sq = [cv.take([128, TT], BF16) for _ in range(4)]
            lnv = [cv.take([128, TT]) for _ in range(2)]
            rstd = [cv.take([128, TT]) for _ in range(2)]
            for t in range(NTT):
                tsl = slice(t * TT, (t + 1) * TT)
                for c in range(8):
                    b = sq[c % 4]
                    bd = D("nsq", c % 4)
                    if c % 2 == 0:
                        P.op(act, lambda e, b=b, c=c: e.activation(out=b, in_=xT[:, c, tsl], func=AF.Square),
                             R=[XD[c][t]], W=[bd])
                    else:
                        P.op(pool, lambda e, b=b, c=c: e.tensor_tensor(out=b, in0=xT[:, c, tsl], in1=xT[:, c, tsl],
                                                                          op=ALU.mult), R=[XD[c][t]], W=[bd])
                    P.mm([dict(out=ps[0][:], lhsT=ones_b, rhs=b, start=(c == 0), stop=(c == 7))],
                         R=[bd, CONST], W=[PSD[0]])
                ld = D("nln", t % 2)
                rd = D("nrs", t % 2)
                P.op(act, lambda e: e.activation(out=lnv[t % 2], in_=ps[0][:], func=AF.Ln, scale=1.0 / D_F, bias=EPS_AP),
                     R=[PSD[0]], W=[ld])
                P.op(act, lambda e: e.activation(out=rstd[t % 2], in_=lnv[t % 2], func=AF.Exp, scale=-0.5),
                     R=[ld], W=[rd])
                for c in range(8):
                    P.op(dve, lambda e, c=c: e.scalar_tensor_tensor(out=hT[:, c, tsl], in0=xT[:, c, tsl],
                                                                    scalar=V(l, gname, c), in1=rstd[t % 2],
                                                                    op0=ALU.mult, op1=ALU.mult),
                         R=[XD[c][t], rd, CONST], W=[HD[c][t]])
            P.barrier()


        def conv_A(src_ps, srcD, u, uD, halo, haloD, t, wcol, bcol, ntap):
            K1 = ntap - 1
            hm = len(haloD)
            P.op(act, lambda e: e.activation(out=u, in_=src_ps, func=AF.Identity, scale=wcol(K1), bias=bcol),
                 R=[srcD] + CONSTS, W=[uD])
            if t < NTT - 1:
                P.op(act, lambda e: e.activation(out=halo[:, t % hm, :], in_=src_ps[:, TT - K1:TT], func=AF.Copy),
                     R=[srcD], W=[haloD[t % hm]])

        def conv_B(src_ps, srcD, u, uD, halo, haloD, t, wcol, ntap):
            K1 = ntap - 1
            hm = len(haloD)
            for j in range(K1):
                sh = K1 - j
                P.op(dve, lambda e, j=j, sh=sh: e.scalar_tensor_tensor(out=u[:, sh:TT], in0=src_ps[:, 0:TT - sh],
                                                                       scalar=wcol(j), in1=u[:, sh:TT],
                                                                       op0=ALU.mult, op1=ALU.add),
                     R=[srcD] + CONSTS, W=[uD])
                if t > 0:
                    hp = halo[:, (t - 1) % hm, :]
                    P.op(dve, lambda e, j=j, sh=sh, hp=hp: e.scalar_tensor_tensor(
                        out=u[:, 0:sh], in0=hp[:, K1 - sh:K1], scalar=wcol(j), in1=u[:, 0:sh], op0=ALU.mult,
                        op1=ALU.add), R=[haloD[(t - 1) % hm]] + CONSTS, W=[uD])

        def conv_taps(src_ps, srcD, u, uD, halo, haloD, t, wcol, bcol, ntap, l):
            conv_A(src_ps, srcD, u, uD, halo, haloD, t, wcol, bcol, ntap)
            conv_B(src_ps, srcD, u, uD, halo, haloD, t, wcol, ntap)

        def wout_group(l, g):
            for pr in range(4):
                wsl, wd, wi = wnext("out")
                w4 = wsl.rearrange("p (d f c) -> p d f c", d=2, f=4)
                for d2 in range(2):
                    dc = pr * 2 + d2
                    for t in range(NTT):
                        tsl = slice(t * TT, (t + 1) * TT)
                        bk = (d2 * NTT + t) % 2
                        P.mm([dict(out=ps[bk][:], lhsT=w4[:, d2, f, :], rhs=yT[:, f, tsl], start=(f == 0),
                                   stop=(f == 3)) for f in range(4)],
                             R=[wd] + [YD[f][t] for f in range(4)], W=[PSD[bk]])
                        P.op(dve, lambda e, dc=dc, bk=bk: e.tensor_tensor(out=xT[:, dc, tsl], in0=ps[bk][:],
                                                                           in1=xT[:, dc, tsl], op=ALU.add),
                             R=[PSD[bk]], W=[XD[dc][t]])
                wdone(wi)

        def rg_group(l):
            cv.reset()
            ws = [wnext("in") for _ in range(8)]
            gw, gwd, gwi = wnext("gatew")
            gw4 = gw.rearrange("p (a n j) -> p a n j", a=2, n=4)
            nls = cv.take([128, 4])
            tmp4 = cv.take([128, 4])
            P.op(act, lambda e: e.activation(out=tmp4, in_=V(l, "rglam", 0, 4), func=AF.Exp, scale=-1.0),
                 R=CONSTS, W=[D("rgtmp4")])
            P.op(act, lambda e: e.activation(out=tmp4, in_=tmp4, func=AF.Ln, bias=ONE_AP), R=CONSTS,
                 W=[D("rgtmp4")])
            P.op(dve, lambda e: e.tensor_scalar(out=nls, in0=tmp4, scalar1=-RG_C, scalar2=None, op0=ALU.mult),
                 R=[D("rgtmp4")], W=[D("rgnls")])
            nls2 = cv.take([128, 4])
            P.op(dve, lambda e: e.tensor_scalar(out=nls2, in0=tmp4, scalar1=-2.0 * RG_C, scalar2=None,
                                                op0=ALU.mult), R=[D("rgtmp4")], W=[D("rgnls")])
            halo = [cv.take([128, 4, 3]) for _ in range(4)]
            HAL = [[D("rghalo", n, i_) for i_ in range(4)] for n in range(4)]
            hst = cv.take([128, 4, 2])
            u = [cv.take([128, TT]) for _ in range(3)]
            gg = [cv.take([128, TT]) for _ in range(3)]
            ub = [cv.take([128, TT], BF16) for _ in range(2)]
            rr = [cv.take([128, TT]) for _ in range(2)]
            ig = [cv.take([128, TT]) for _ in range(2)]
            aa = [cv.take([128, TT]) for _ in range(2)]
            hh = [cv.take([128, TT]) for _ in range(2)]
            bt = cv.take([128, TT])
            ypre = cv.take([128, 4, TT])
            ysq = [cv.take([128, TT], BF16) for _ in range(2)]
            lnv = cv.take([128, TT])
            units = [(t, n) for t in range(NTT) for n in range(4)]
            NU = len(units)

            def S1(i):
                t, n = units[i]
                tsl = slice(t * TT, (t + 1) * TT)
                k3, b = i % 3, i % 2
                xb, gb = b, 2 + b
                wx_, wxd, _ = ws[n]
                wg_, wgd, _ = ws[4 + n]
                w8x = wx_.rearrange("p (k c) -> p k c", k=8)
                w8g = wg_.rearrange("p (k c) -> p k c", k=8)
                P.mm([dict(out=ps[xb][:], lhsT=w8x[:, kc, :], rhs=hT[:, kc, tsl], start=(kc == 0), stop=(kc == 7))
                      for kc in range(8)], R=[wxd] + [HD[kc][t] for kc in range(8)], W=[PSD[xb]])
                P.mm([dict(out=ps[gb][:], lhsT=w8g[:, kc, :], rhs=hT[:, kc, tsl], start=(kc == 0), stop=(kc == 7))
                      for kc in range(8)], R=[wgd] + [HD[kc][t] for kc in range(8)], W=[PSD[gb]])
                conv_A(ps[xb][:], PSD[xb], u[k3], D("rgu", k3), halo[n], HAL[n], t,
                       lambda j, n=n: V(l, "rgcw", n * 4 + j), V(l, "rgcb", n), 4)
                P.op(act, lambda e: e.activation(out=gg[k3], in_=ps[gb][:], func=AF.Square), R=[PSD[gb]],
                     W=[D("rgg", k3)])

            def S2(i):
                t, n = units[i]
                k3, b = i % 3, i % 2
                xb, gb, rb, ib = b, 2 + b, 4, 5
                uD, gD = D("rgu", k3), D("rgg", k3)
                conv_B(ps[xb][:], PSD[xb], u[k3], uD, halo[n], HAL[n], t, lambda j, n=n: V(l, "rgcw", n * 4 + j), 4)
                P.op(dve, lambda e: e.tensor_copy(out=ub[b], in_=u[k3]), R=[uD], W=[D("rgub", b)])
                P.op(dve, lambda e: e.tensor_scalar(out=gg[k3], in0=gg[k3], scalar1=0.044715, scalar2=1.0,
                                                    op0=ALU.mult, op1=ALU.add), R=[], W=[gD])
                P.op(dve, lambda e: e.tensor_tensor(out=gg[k3], in0=ps[gb][:], in1=gg[k3], op=ALU.mult),
                     R=[PSD[gb]], W=[gD])
                P.mm([dict(out=ps[rb][:], lhsT=gw4[:, 0, n, :], rhs=ub[b])], R=[gwd, D("rgub", b)], W=[PSD[rb]])
                P.mm([dict(out=ps[ib][:], lhsT=gw4[:, 1, n, :], rhs=ub[b])], R=[gwd, D("rgub", b)], W=[PSD[ib]])
                P.op(act, lambda e: e.activation(out=gg[k3], in_=gg[k3], func=AF.Sigmoid, scale=1.5957691216057308),
                     R=[], W=[gD])
                P.op(act, lambda e: e.activation(out=rr[b], in_=ps[rb][:], func=AF.Sigmoid, bias=V(l, "rgba", n)),
                     R=[PSD[rb]] + CONSTS, W=[D("rgr", b)])
                P.op(act, lambda e: e.activation(out=ig[b], in_=ps[ib][:], func=AF.Sigmoid, bias=V(l, "rgbx", n)),
                     R=[PSD[ib]] + CONSTS, W=[D("rgi", b)])
                P.op(dve, lambda e: e.tensor_tensor(out=gg[k3], in0=ps[gb][:], in1=gg[k3], op=ALU.mult),
                     R=[PSD[gb]], W=[gD])
                P.op(pool, lambda e: e.tensor_tensor(out=ig[b], in0=ig[b], in1=u[k3], op=ALU.mult), R=[uD],
                     W=[D("rgi", b)])

            def S3(i):
                t, n = units[i]
                tsl = slice(t * TT, (t + 1) * TT)
                k3, b = i % 3, i % 2
                uD, gD = D("rgu", k3), D("rgg", k3)
                aD, bD, hD = D("rga", b), D("rgbt"), D("rgh", b)
                P.op(act, lambda e: e.activation(out=aa[b], in_=rr[b], func=AF.Exp, scale=nls[:, n:n + 1]),
                     R=[D("rgr", b), D("rgnls")], W=[aD])
                P.op(act, lambda e: e.activation(out=bt, in_=rr[b], func=AF.Exp, scale=nls2[:, n:n + 1]),
                     R=[D("rgr", b), D("rgnls")], W=[bD])
                P.op(act, lambda e: e.activation(out=bt, in_=bt, func=AF.Ln, scale=-1.0, bias=ONE_AP), R=CONSTS,
                     W=[bD])
                P.op(act, lambda e: e.activation(out=bt, in_=bt, func=AF.Exp, scale=0.5), R=[], W=[bD])
                P.op(dve, lambda e: e.tensor_tensor(out=bt, in0=bt, in1=ig[b], op=ALU.mult), R=[D("rgi", b)],
                     W=[bD])
                if t == 0:
                    init, initR = 0.0, []
                else:
                    init = hst[:, n, (t - 1) % 2:(t - 1) % 2 + 1]
                    initR = [D("rghst", n, (t - 1) % 2)]
                P.op(dve, lambda e: e.tensor_tensor_scan(out=hh[b], data0=aa[b], data1=bt, initial=init,
                                                         op0=ALU.mult, op1=ALU.add), R=[aD, bD] + initR, W=[hD])
                if t < NTT - 1:
                    P.op(pool, lambda e: e.tensor_copy(out=hst[:, n, t % 2:t % 2 + 1], in_=hh[b][:, TT - 1:TT]),
                         R=[hD], W=[D("rghst", n, t % 2)])
                yD = D("rgy", n)
                P.op(dve, lambda e: e.tensor_tensor(out=ypre[:, n, :], in0=gg[k3], in1=hh[b], op=ALU.mult),
                     R=[gD, hD], W=[yD])
                sD = D("rgysq", b)
                P.op(pool, lambda e: e.tensor_tensor(out=ysq[b], in0=ypre[:, n, :], in1=ypre[:, n, :], op=ALU.mult),
                     R=[yD], W=[sD])

            def SS(i):
                t, n = units[i]
                tsl = slice(t * TT, (t + 1) * TT)
                b = i % 2
                P.mm([dict(out=ps[6][:], lhsT=ones_b, rhs=ysq[b], start=(n == 0), stop=(n == 3))],
                     R=[D("rgysq", b), CONST], W=[PSD[6]])
                if n == 3:
                    P.op(act, lambda e: e.activation(out=lnv, in_=ps[6][:], func=AF.Ln, scale=1.0 / 512.0,
                                                     bias=EPS_AP), R=[PSD[6]] + CONSTS, W=[D("rgln")])
                    P.op(act, lambda e: e.activation(out=lnv, in_=lnv, func=AF.Exp, scale=-0.5), R=[],
                         W=[D("rgln")])
                    for n2 in range(4):
                        P.op(dve, lambda e, n2=n2: e.scalar_tensor_tensor(out=yT[:, n2, tsl], in0=ypre[:, n2, :],
                                                                          scalar=V(l, "rgnorm", n2), in1=lnv,
                                                                          op0=ALU.mult, op1=ALU.mult),
                             R=[D("rgy", n2), D("rgln")] + CONSTS, W=[YD[n2][t]])

            S1(0)
            S1(1)
            S2(0)
            for i in range(NU):
                if i + 2 < NU:
                    S1(i + 2)
                if i + 1 < NU:
                    S2(i + 1)
                if i > 0:
                    SS(i - 1)
                S3(i)
            SS(NU - 1)
            wdone(gwi)
            P.barrier()

        def da_group(l):
            lambda_init = 0.8 - 0.6 * math.exp(-0.3 * l)
            cv.reset()
            junk = cv.take([128, 128])
            s12 = cv.take([128, 2])
            nlam = cv.take([128, 1])
            dl = V(l, "dalam", 0, 256)
            LD = D("dalam_t")
            for i_ in range(2):
                P.op(dve, lambda e, i_=i_: e.tensor_tensor(out=junk[:, 0:64], in0=dl[:, i_ * 128:i_ * 128 + 64],
                                                           in1=dl[:, i_ * 128 + 64:i_ * 128 + 128], op=ALU.mult),
                     R=CONSTS, W=[LD])
                P.op(dve, lambda e, i_=i_: e.tensor_scalar(out=junk[:, 64:128], in0=junk[:, 0:64], scalar1=1.0,
                                                           scalar2=None, op0=ALU.mult, op1=ALU.add,
                                                           accum_out=s12[:, i_:i_ + 1]), R=[], W=[LD])
            P.op(act, lambda e: e.activation(out=s12, in_=s12, func=AF.Exp), R=[], W=[LD])
            P.op(dve, lambda e: e.tensor_tensor(out=nlam, in0=s12[:, 1:2], in1=s12[:, 0:1], op=ALU.subtract), R=[],
                 W=[LD])
            P.op(dve, lambda e: e.tensor_scalar(out=nlam, in0=nlam, scalar1=-lambda_init, scalar2=None, op0=ALU.add),
                 R=[], W=[LD])
            if stage == "dalam":
                dump(nlam, [LD], 1)
                return True
            qT = cv.take([128, S], BF16)
            kTz = [cv.take([128, S], BF16) for _ in range(2)]
            kT = kTz[0]
            vaug = cv.take([128, NB, 130], BF16)
            qb = [cv.take([128, TT], BF16) for _ in range(2)]
            t1 = [cv.take([128, TT]) for _ in range(2)]
            t2 = [cv.take([128, TT]) for _ in range(2)]
            PT = [[cv.take([128, TT], BF16) for _ in range(2)] for _ in range(2)]
            rrA = cv.take([128, TT])
            rrB = cv.take([128, TT])
            o1 = cv.take([128, TT])
            o2 = cv.take([128, TT])
            sqb = cv.take([128, TT], BF16)
            lnv = cv.take([128, TT])
            lbias = cv.take([128, 1])
            P.op(dve, lambda e: e.memset(lbias, math.log(1.0 - lambda_init)), W=[D("dalb")])
            psTf = psT[:].bitcast(F32)
            pending = []
            QD = [D("daq", t) for t in range(NTT)]
            KD = [D("dak", t) for t in range(NTT)]
            VD = [D("dav", g) for g in range(4)]
            P.op(dve, lambda e: e.memset(vaug[:, :, 128:129], 1.0), W=VD)
            P.op(dve, lambda e: e.memset(kTz[0][64:128, :], 0.0), W=[D("dakz")])
            P.op(dve, lambda e: e.memset(kTz[1][0:64, :], 0.0), W=[D("dakz")])
            ctr = {"rp": 0, "st": 0, "ep": 0, "tp": 0}
            for h in range(4):
                wq, wqd, _ = wnext("in")
                wk, wkd, _ = wnext("in")
                wv, wvd, wvi = wnext("in")
                wq8 = wq.rearrange("p (k c) -> p k c", k=8)
                wk8 = wk.rearrange("p (k c) -> p k c", k=8)
                wv8 = wv.rearrange("p (k c) -> p k c", k=8)
                punits = [(t, which) for t in range(NTT) for which in range(2)]
                pinfo = [(wq8, wqd, qT, QD, 0.125), (wk8, wkd, kT, KD, 1.0)]
                base = ctr["rp"]

                def prA(ui):
                    t, which = punits[ui]
                    w8, wd_, dst, dstD, scl = pinfo[which]
                    tsl = slice(t * TT, (t + 1) * TT)
                    k = (base + ui) % 2
                    pb = k
                    P.mm([dict(out=ps[pb][:], lhsT=w8[:, kc, :], rhs=hT[:, kc, tsl], start=(kc == 0),
                               stop=(kc == 7)) for kc in range(8)],
                         R=[wd_] + [HD[kc][t] for kc in range(8)], W=[PSD[pb]])
                    P.op(act, lambda e: e.activation(out=qb[k], in_=ps[pb][:], func=AF.Copy), R=[PSD[pb]],
                         W=[D("daqb", k)])

                def prB(ui):
                    t, which = punits[ui]
                    w8, wd_, dst, dstD, scl = pinfo[which]
                    tsl = slice(t * TT, (t + 1) * TT)
                    k = (base + ui) % 2
                    pb, sbk = k, 2 + k
                    P.mm([dict(out=ps[sbk][:], lhsT=perm_b, rhs=qb[k])], R=[D("daqb", k), CONST], W=[PSD[sbk]])
                    P.op(dve, lambda e: e.scalar_tensor_tensor(out=t1[k], in0=ps[pb][:], scalar=scl,
                                                               in1=ropeC[:, tsl], op0=ALU.mult, op1=ALU.mult),
                         R=[PSD[pb], ROPE], W=[D("dat1", k)])
                    P.op(dve, lambda e: e.scalar_tensor_tensor(out=t2[k], in0=ps[sbk][:], scalar=scl,
                                                               in1=ropeS[:, tsl], op0=ALU.mult, op1=ALU.mult),
                         R=[PSD[sbk], ROPE], W=[D("dat2", k)])
                    if which == 0:
                        P.op(pool, lambda e: e.tensor_tensor(out=dst[:, tsl], in0=t1[k], in1=t2[k], op=ALU.add),
                             R=[D("dat1", k), D("dat2", k)], W=[dstD[t]])
                    else:
                        for c_ in range(2):
                            pr_ = slice(c_ * 64, (c_ + 1) * 64)
                            P.op(pool, lambda e, c_=c_, pr_=pr_: e.tensor_tensor(
                                out=kTz[c_][pr_, tsl], in0=t1[k][pr_, :], in1=t2[k][pr_, :], op=ALU.add),
                                R=[D("dat1", k), D("dat2", k), D("dakz")], W=[dstD[t]])

                prA(0)
                for ui in range(len(punits)):
                    if ui + 1 < len(punits):
                        prA(ui + 1)
                    prB(ui)
                ctr["rp"] += len(punits)
                for g4 in range(4):
                    vb = 4 + (g4 % 2)
                    mms = []
                    for i in range(4):
                        tb = g4 * 4 + i
                        for kc in range(8):
                            mms.append(dict(out=ps[vb][:, i * 128:(i + 1) * 128],
                                            lhsT=hT[:, kc, tb * 128:(tb + 1) * 128], rhs=wv8[:, kc, :],
                                            start=(kc == 0), stop=(kc == 7)))
                    P.mm(mms, R=[wvd] + [HD[kc][g4] for kc in range(8)], W=[PSD[vb]])
                    P.op(act, lambda e, g4=g4, vb=vb: e.activation(
                        out=vaug[:, g4 * 4:(g4 + 1) * 4, 0:128],
                        in_=ps[vb][:].rearrange("p (a b) -> p a b", a=4), func=AF.Copy), R=[PSD[vb]], W=[VD[g4]])
                wdone(wvi)
                if stage == "daq":
                    dump(qT, QD, S, cast=True)
                    dump(kTz[0], KD, S, cast=True)
                    dump(vaug.rearrange("p a b -> p (a b)"), VD, NB * 130, cast=True)
                    return True
                for qg in range(4 if stage != "da1" else 1):
                    qsl = slice(qg * TT, (qg + 1) * TT)
                    UB = [ps[4], ps[5]]
                    RB = [ps[6], psTf]
                    UD_ = [PSD[4], PSD[5]]
                    RD_ = [PSD[6], PSTD]
                    nj = 4 * qg + 4
                    deferred = []
                    for j in range(nj):
                        r = max(0, j - 4 * qg)
                        c0 = r * 128
                        par = j % 2
                        cur = []
                        for c in range(2):
                            sbank = c * 2 + par
                            prow = slice(c * 64, (c + 1) * 64)
                            mms = [dict(out=ps[sbank][:, c0:TT], lhsT=kTz[c][:, j * 128:(j + 1) * 128],
                                        rhs=qT[:, qg * TT + c0:(qg + 1) * TT], start=True, stop=(j < 4 * qg),
                                        sgc=True)]
                            if j >= 4 * qg:
                                mms.append(dict(out=ps[sbank][:, c0:c0 + 128], lhsT=ident_b, rhs=maskneg_b,
                                                start=False, stop=True, sgc=True))
                            P.mm(mms, R=[KD[j // 4], QD[qg], CONST], W=[PSD[sbank]])
                        for c in range(2):
                            sbank = c * 2 + par
                            ptD = D("dapt", c, par)
                            P.op(act, lambda e, c=c, par=par, sbank=sbank, c0=c0: e.activation(
                                out=PT[c][par][:, c0:TT], in_=ps[sbank][:, c0:TT], func=AF.Exp),
                                R=[PSD[sbank]], W=[ptD])

                            def acc(c=c, par=par, c0=c0, j=j, ptD=ptD):
                                P.mm([dict(out=UB[c][:, c0:TT], lhsT=vaug[:, j, 0:128], rhs=PT[c][par][:, c0:TT],
                                           start=(j == 0), stop=(j == nj - 1), sgc=True)],
                                     R=[ptD, VD[j // 4]], W=[UD_[c]])
                                P.mm([dict(out=RB[c][:, c0:TT], lhsT=ones_b, rhs=PT[c][par][:, c0:TT],
                                           start=(j == 0), stop=(j == nj - 1), sgc=True)],
                                     R=[ptD, CONST], W=[RD_[c]])

                            cur.append(acc)
                        if j == 1 and pending:
                            pending.pop(0)()
                        for f_ in deferred:
                            f_()
                        deferred = cur
                    for f_ in deferred:
                        f_()

                    aD, bD, o1D, o2D = D("darrA"), D("darrB"), D("dao1"), D("dao2")
                    P.op(act, lambda e: e.activation(out=rrA, in_=RB[0][:, :], func=AF.Ln), R=[RD_[0]], W=[aD])
                    P.op(act, lambda e: e.activation(out=rrB, in_=RB[1][:, :], func=AF.Ln), R=[RD_[1]], W=[bD])
                    P.op(act, lambda e: e.activation(out=rrA, in_=rrA, func=AF.Exp, scale=-1.0), R=[], W=[aD])
                    P.op(act, lambda e: e.activation(out=rrB, in_=rrB, func=AF.Exp, scale=-1.0), R=[], W=[bD])
                    P.op(dve, lambda e: e.tensor_tensor(out=o1, in0=UB[0][:, :], in1=rrA, op=ALU.mult),
                         R=[UD_[0], aD], W=[o1D])
                    P.op(dve, lambda e: e.scalar_tensor_tensor(out=o2, in0=UB[1][:, :], scalar=nlam[:, 0:1], in1=rrB,
                                                               op0=ALU.mult, op1=ALU.mult),
                         R=[UD_[1], bD, LD], W=[o2D])

                    def tail(qg=qg, h=h, qsl=qsl):
                        o1D, o2D, sD, lD = D("dao1"), D("dao2"), D("dasq"), D("daln")
                        P.op(pool, lambda e: e.tensor_tensor(out=o1, in0=o1, in1=o2, op=ALU.add), R=[o2D], W=[o1D])
                        P.op(pool, lambda e: e.tensor_tensor(out=sqb, in0=o1, in1=o1, op=ALU.mult), R=[o1D], W=[sD])
                        P.mm([dict(out=ps[6][:, :], lhsT=ones_b, rhs=sqb)], R=[sD, CONST], W=[PSD[6]])
                        P.op(act, lambda e: e.activation(out=lnv, in_=ps[6][:, :], func=AF.Ln, scale=1.0 / 128.0,
                                                         bias=EPS_AP), R=[PSD[6]] + CONSTS, W=[lD])
                        P.op(act, lambda e: e.activation(out=lnv, in_=lnv, func=AF.Exp, scale=-0.5, bias=lbias),
                             R=[D("dalb")], W=[lD])
                        P.op(dve, lambda e: e.scalar_tensor_tensor(out=yT[:, h, qsl], in0=o1,
                                                                   scalar=V(l, "danorm", 0), in1=lnv, op0=ALU.mult,
                                                                   op1=ALU.mult),
                             R=[o1D, lD] + CONSTS, W=[YD[h][qg]])

                    pending.append(tail)
                while pending:
                    pending.pop(0)()
            P.barrier()

        def ml_group(l):
            cv.reset()
            wg, wgd, wgi = wnext("gates")
            wg8 = wg.rearrange("p (k c) -> p k c", k=8)
            mms = []
            for c in range(NB):
                for kc in range(8):
                    mms.append(dict(out=ps[0][:, c * 8:(c + 1) * 8], lhsT=hT[:, kc, c * 128:(c + 1) * 128],
                                    rhs=wg8[:, kc, 0:8], start=(kc == 0), stop=(kc == 7)))
            P.mm(mms, R=[wgd] + [HD[kc][t] for kc in range(8) for t in range(NTT)], W=[PSD[0]])
            wdone(wgi)
            gview = ps[0][:, 0:128].rearrange("p (c g) -> p g c", g=8)
            li = cv.take([128, 4, NB])
            fp = cv.take([128, 4, NB])
            aa = cv.take([128, 4, NB])
            ebL = cv.take([128, 4, NB])
            d1 = cv.take([128, 4, NB])
            d0 = cv.take([128, 4, NB])
            Em = cv.take([128, 4, NB])
            rZ = cv.take([128, 4, NB])
            dcy = cv.take([128, 4, NB + 1])
            esk = cv.take([128, 4, NB])
            bnd = cv.take([128, 4, NB])
            Emp = cv.take([128, 4, NB])
            GD = D("mlg")

            def f2(x):
                return x.rearrange("p a b -> p (a b)")

            for h in range(4):
                P.op(dve, lambda e, h=h: e.tensor_scalar(out=li[:, h, :], in0=gview[:, h, :],
                                                         scalar1=V(l, "ibias", h), scalar2=None, op0=ALU.add),
                     R=[PSD[0]] + CONSTS, W=[GD])
                P.op(dve, lambda e, h=h: e.tensor_scalar(out=fp[:, h, :], in0=gview[:, 4 + h, :],
                                                         scalar1=V(l, "fbias", h), scalar2=None, op0=ALU.add),
                     R=[PSD[0]] + CONSTS, W=[GD])
            P.op(act, lambda e: e.activation(out=f2(fp), in_=f2(fp), func=AF.Exp, scale=-1.0), R=[], W=[GD])
            P.op(act, lambda e: e.activation(out=f2(fp), in_=f2(fp), func=AF.Ln, bias=ONE_AP), R=CONSTS, W=[GD])
            P.mm([dict(out=ps[1][:, 0:64], lhsT=tri_f, rhs=f2(fp))], R=[GD, CONST], W=[PSD[1]])
            P.mm([dict(out=ps[2][:, 0:64], lhsT=ones_f, rhs=f2(fp))], R=[GD, CONST], W=[PSD[2]])
            P.op(dve, lambda e: e.tensor_tensor(out=f2(aa), in0=ps[1][:, 0:64], in1=f2(li), op=ALU.add),
                 R=[PSD[1]], W=[GD])
            P.op(act, lambda e: e.activation(out=f2(aa), in_=f2(aa), func=AF.Exp), R=[], W=[GD])
            P.mm([dict(out=ps[3][:, 0:64], lhsT=ones_f, rhs=f2(aa))], R=[GD, CONST], W=[PSD[3]])
            P.op(act, lambda e: e.activation(out=f2(ebL), in_=ps[2][:, 0:64], func=AF.Exp, scale=-1.0), R=[PSD[2]],
                 W=[GD])
            P.op(dve, lambda e: e.tensor_tensor(out=f2(d1), in0=ps[3][:, 0:64], in1=f2(ebL), op=ALU.mult),
                 R=[PSD[3]], W=[GD])
            P.op(dve, lambda e: e.tensor_tensor(out=d1[:, :, 0:1], in0=d1[:, :, 0:1], in1=ebL[:, :, 0:1], op=ALU.add),
                 R=[], W=[GD])
            P.op(dve, lambda e: e.tensor_copy(out=f2(d0), in_=f2(ebL)), R=[], W=[GD])
            P.op(dve, lambda e: e.memset(d0[:, :, 0:1], 0.0), R=[], W=[GD])
            P.op(dve, lambda e: e.tensor_tensor_scan(out=f2(Em), data0=f2(d0), data1=f2(d1), initial=0.0,
                                                     op0=ALU.mult, op1=ALU.add), R=[], W=[GD])
            P.op(dve, lambda e: e.reciprocal(out=f2(rZ), in_=f2(Em)), R=[], W=[GD])
            P.op(dve, lambda e: e.tensor_tensor(out=f2(rZ), in0=f2(rZ), in1=f2(ebL), op=ALU.mult), R=[], W=[GD])
            P.op(dve, lambda e: e.memset(Emp[:, :, 0:1], 1.0), R=[], W=[GD])
            P.op(dve, lambda e: e.tensor_copy(out=Emp[:, :, 1:NB], in_=Em[:, :, 0:NB - 1]), R=[], W=[GD])
            P.op(dve, lambda e: e.memset(dcy[:, :, NB:NB + 1], 1.0), R=[], W=[GD])
            P.op(dve, lambda e: e.tensor_tensor(out=dcy[:, :, 0:NB], in0=Emp[:, :, :], in1=rZ[:, :, :], op=ALU.mult),
                 R=[], W=[GD])
            P.op(dve, lambda e: e.scalar_tensor_tensor(out=f2(esk), in0=f2(aa), scalar=128.0 ** -0.5, in1=f2(rZ),
                                                       op0=ALU.mult, op1=ALU.mult), R=[], W=[GD])
            P.op(act, lambda e: e.activation(out=f2(bnd), in_=ps[1][:, 0:64], func=AF.Exp), R=[PSD[1]], W=[GD])
            P.op(dve, lambda e: e.tensor_tensor(out=f2(bnd), in0=f2(bnd), in1=f2(rZ), op=ALU.mult), R=[], W=[GD])

            qT = cv.take([128, S], BF16)
            kT = cv.take([128, S], BF16)
            vaug = cv.take([128, NB, 130], BF16)
            sgoT = cv.take([128, S], BF16)
            u = [cv.take([128, TT]) for _ in range(3)]
            halo = [cv.take([128, 4, 3]) for _ in range(2)]
            MLH = [[D("mlhalo", w_, i_) for i_ in range(4)] for w_ in range(2)]
            WT = [cv.take([128, 128], BF16) for _ in range(2)]
            ktok = [cv.take([128, 128], BF16) for _ in range(2)]
            Cst = cv.take([128, 130])
            Cs = [cv.take([128, 130], BF16) for _ in range(2)]
            E4 = [cv.take([128, 4, 129]) for _ in range(2)]
            hn = cv.take([128, 4, 128])
            sq4 = cv.take([128, 4, 128])
            yb4 = [cv.take([128, 4, 128], BF16) for _ in range(2)]
            a4 = cv.take([128, 4])
            ss4 = cv.take([128, 4])
            QD = [D("mlq", t) for t in range(NTT)]
            KD = [D("mlk", t) for t in range(NTT)]
            VD = [D("mlv", g) for g in range(4)]
            OD = [D("mlo", g) for g in range(4)]
            P.op(dve, lambda e: e.memset(vaug[:, :, 128:129], 1.0), W=VD)
            ctr = {"u": 0, "c": 0}
            epi_parts = []
            for h in range(4):
                wq, wqd, _ = wnext("in")
                wk, wkd, _ = wnext("in")
                wv, wvd, _ = wnext("in")
                wo, wod, woi = wnext("in")
                w8 = [x.rearrange("p (k c) -> p k c", k=8) for x in (wq, wk, wv, wo)]
                punits = [(which, t) for which in range(3) for t in range(NTT)]
                pinfo = [(wqd, qT, QD), (wkd, kT, KD), (wod, sgoT, OD)]
                widx = [0, 1, 3]

                def pA(ui):
                    which, t = punits[ui]
                    wd_, dst, dstD = pinfo[which]
                    ch = which * 4 + h
                    tsl = slice(t * TT, (t + 1) * TT)
                    n_ = ctr["u"] + ui
                    k, pb = n_ % 3, n_ % 2
                    P.mm([dict(out=ps[pb][:], lhsT=w8[widx[which]][:, kc, :], rhs=hT[:, kc, tsl],
                               start=(kc == 0), stop=(kc == 7)) for kc in range(8)],
                         R=[wd_] + [HD[kc][t] for kc in range(8)], W=[PSD[pb]])
                    if which < 2:
                        conv_A(ps[pb][:], PSD[pb], u[k], D("mlu", k), halo[which], MLH[which], t,
                               lambda j, ch=ch: V(l, "mlcw", ch * 4 + j), V(l, "mlcb", ch), 4)

                def pB(ui):
                    which, t = punits[ui]
                    wd_, dst, dstD = pinfo[which]
                    ch = which * 4 + h
                    tsl = slice(t * TT, (t + 1) * TT)
                    n_ = ctr["u"] + ui
                    k, pb = n_ % 3, n_ % 2
                    if which == 2:
                        P.op(act, lambda e, dst=dst, pb=pb: e.activation(out=dst[:, tsl], in_=ps[pb][:],
                                                                         func=AF.Sigmoid),
                             R=[PSD[pb]], W=[dstD[t]])
                        return
                    conv_B(ps[pb][:], PSD[pb], u[k], D("mlu", k), halo[which], MLH[which], t,
                           lambda j, ch=ch: V(l, "mlcw", ch * 4 + j), 4)
                    P.op(act, lambda e, k=k, dst=dst: e.activation(out=dst[:, tsl], in_=u[k], func=AF.Silu),
                         R=[D("mlu", k)], W=[dstD[t]])

                pA(0)
                for ui in range(len(punits)):
                    if ui + 1 < len(punits):
                        pA(ui + 1)
                    pB(ui)
                    if epi_parts:
                        epi_parts.pop(0)()
                ctr["u"] += len(punits)
                for g4 in range(4):
                    vb = 2 + (g4 % 2)
                    mms = []
                    for i_ in range(4):
                        tb = g4 * 4 + i_
                        for kc in range(8):
                            mms.append(dict(out=ps[vb][:, i_ * 128:(i_ + 1) * 128],
                                            lhsT=hT[:, kc, tb * 128:(tb + 1) * 128], rhs=w8[2][:, kc, :],
                                            start=(kc == 0), stop=(kc == 7)))
                    P.mm(mms, R=[wvd] + [HD[kc][g4] for kc in range(8)], W=[PSD[vb]])
                    P.op(act, lambda e, g4=g4, vb=vb: e.activation(
                        out=vaug[:, g4 * 4:(g4 + 1) * 4, 0:128],
                        in_=ps[vb][:].rearrange("p (a b) -> p a b", a=4), func=AF.Copy), R=[PSD[vb]], W=[VD[g4]])
                wdone(woi)
                CD = D("mlC")
                P.op(dve, lambda e: e.memset(Cst, 0.0), R=[], W=[CD])

                def stageA(c):
                    csl = slice(c * 128, (c + 1) * 128)
                    k = c % 2
                    eskc = f2(esk)[:, h * NB + c:h * NB + c + 1]
                    sb_ = 4 + k
                    P.mm([dict(out=ps[sb_][:, 0:128], lhsT=kT[:, csl], rhs=qT[:, csl])], R=[KD[c // 4], QD[c // 4]],
                         W=[PSD[sb_]])
                    P.op(dve, lambda e: e.scalar_tensor_tensor(out=WT[k], in0=ps[sb_][:, 0:128], scalar=eskc,
                                                               in1=tri_f, op0=ALU.mult, op1=ALU.mult),
                         R=[PSD[sb_], GD, CONST], W=[D("mlWT", k)])
                    P.mm([dict(tr=True, out=psT[:, k * 128:(k + 1) * 128], in_=kT[:, csl], identity=ident_b)],
                         R=[KD[c // 4], CONST], W=[PSTD])
                    P.op(act, lambda e: e.activation(out=ktok[k], in_=psT[:, k * 128:(k + 1) * 128], func=AF.Copy,
                                                     scale=eskc), R=[PSTD, GD], W=[D("mlktok", k)])
                    if c < NB - 1:
                        P.mm([dict(out=ps[2 + k][:, 0:129], lhsT=ktok[k], rhs=vaug[:, c, 0:129])],
                             R=[D("mlktok", k), VD[c // 4]], W=[PSD[2 + k]])

                stageA(0)
                for c in range(NB):
                    csl = slice(c * 128, (c + 1) * 128)
                    k = c % 2
                    eb = (c // 4) % 2
                    if c + 1 < NB:
                        stageA(c + 1)
                    mms = []
                    if c > 0:
                        mms.append(dict(out=ps[6][:, 0:129], lhsT=qT[:, csl], rhs=Cs[(c - 1) % 2][:, 0:129],
                                        start=True, stop=False))
                    mms.append(dict(out=ps[6][:, 0:129], lhsT=WT[k], rhs=vaug[:, c, 0:129], start=(c == 0),
                                    stop=True))
                    P.mm(mms, R=[QD[c // 4], D("mlWT", k), VD[c // 4]] + ([D("mlCs", (c - 1) % 2)] if c > 0 else []),
                         W=[PSD[6]])
                    P.op(act, lambda e, c=c, eb=eb: e.activation(out=E4[eb][:, c % 4, :], in_=ps[6][:, 0:129],
                                                                 func=AF.Copy), R=[PSD[6]], W=[D("mlE4", eb)])
                    if c < NB - 1:
                        dc_ = dcy[:, h, c:c + 1]
                        dn_ = dcy[:, h, c + 1:c + 2]
                        P.op(dve, lambda e, dc_=dc_, k=k: e.scalar_tensor_tensor(
                            out=Cst[:, 0:129], in0=Cst[:, 0:129], scalar=dc_, in1=ps[2 + k][:, 0:129],
                            op0=ALU.mult, op1=ALU.add), R=[PSD[2 + k], GD], W=[CD])
                        P.op(dve, lambda e, dn_=dn_, c=c: e.tensor_scalar(out=Cs[c % 2][:, 0:129], in0=Cst[:, 0:129],
                                                                          scalar1=dn_, scalar2=None, op0=ALU.mult),
                             R=[GD], W=[D("mlCs", c % 2)])
                    if epi_parts:
                        epi_parts.pop(0)()
                    if c % 4 == 3:
                        c0 = c - 3
                        E_ = E4[eb]
                        eD, aD, hD, qD, sD, yD = (D("mlE4", eb), D("mla4"), D("mlhn"), D("mlsq4"), D("mlss4"),
                                                  D("mlyb4", eb))

                        def p1(E_=E_, eD=eD, aD=aD, hD=hD, qD=qD, c0=c0, h=h):
                            dn4 = E_[:, :, 128]
                            P.op(dve, lambda e: e.scalar_tensor_tensor(out=a4, in0=dn4, scalar=-1.0, in1=dn4,
                                                                       op0=ALU.mult, op1=ALU.max), R=[eD], W=[aD])
                            P.op(dve, lambda e: e.tensor_tensor(out=a4, in0=a4, in1=bnd[:, h, c0:c0 + 4],
                                                                op=ALU.max), R=[GD], W=[aD])
                            P.op(dve, lambda e: e.reciprocal(out=a4, in_=a4), R=[], W=[aD])
                            P.op(dve, lambda e: e.tensor_tensor(
                                out=hn, in0=E_[:, :, 0:128], in1=a4.unsqueeze(2).to_broadcast([128, 4, 128]),
                                op=ALU.mult), R=[eD, aD], W=[hD])
                            P.op(pool, lambda e: e.tensor_tensor(out=sq4, in0=hn, in1=hn, op=ALU.mult), R=[hD],
                                 W=[qD])

                        def p2(qD=qD, sD=sD):
                            P.op(dve, lambda e: e.tensor_reduce(out=ss4, in_=sq4, axis=mybir.AxisListType.X,
                                                                op=ALU.add), R=[qD], W=[sD])
                            P.op(act, lambda e: e.activation(out=ss4, in_=ss4, func=AF.Ln, scale=1.0 / 128.0,
                                                             bias=EPS_AP), R=CONSTS, W=[sD])
                            P.op(act, lambda e: e.activation(out=ss4, in_=ss4, func=AF.Exp, scale=-0.5), R=[],
                                 W=[sD])

                        def p3(sD=sD, qD=qD, hD=hD, yD=yD, eb=eb):
                            P.op(dve, lambda e: e.tensor_tensor(
                                out=yb4[eb], in0=hn, in1=ss4.unsqueeze(2).to_broadcast([128, 4, 128]), op=ALU.mult),
                                R=[sD, qD, hD], W=[yD])

                        def p4(yD=yD, c0=c0, eb=eb, c=c, h=h):
                            P.mm([dict(tr=True, out=psT[:, 512 + ii * 128:512 + (ii + 1) * 128],
                                       in_=yb4[eb][:, ii, :], identity=ident_b) for ii in range(4)],
                                 R=[yD, CONST], W=[PSTD])
                            P.op(dve, lambda e: e.scalar_tensor_tensor(
                                out=yT[:, h, c0 * 128:(c0 + 4) * 128], in0=psT[:, 512:1024],
                                scalar=V(l, "mlnorm", h), in1=sgoT[:, c0 * 128:(c0 + 4) * 128], op0=ALU.mult,
                                op1=ALU.mult), R=[PSTD, OD[c // 4]] + CONSTS, W=[YD[h][c // 4]])

                        while epi_parts:
                            epi_parts.pop(0)()
                        epi_parts.extend([p1, p2, p3, p4])
                if h == 3:
                    while epi_parts:
                        epi_parts.pop(0)()
            P.barrier()

        def ffn(l):
            cv.reset()
            aT = cv.take([128, 6, S], BF16)
            NBUF = 3
            ug = [cv.take([128, TT]) for _ in range(NBUF)]
            uv = [cv.take([128, TT]) for _ in range(NBUF)]
            sg = [cv.take([128, TT]) for _ in range(NBUF)]
            hg = cv.take([128, 4, 2])
            hv = cv.take([128, 4, 2])
            HG = [D("fhg", i_) for i_ in range(4)]
            HV = [D("fhv", i_) for i_ in range(4)]
            it = 0
            for G in FFN_GROUPS:
                units = []
                for gi, j in enumerate(G):
                    wgk, wgd, _ = wnext("up")
                    wvk, wvd, wvi = wnext("up")
                    wg8 = wgk.rearrange("p (k c) -> p k c", k=8)
                    wv8 = wvk.rearrange("p (k c) -> p k c", k=8)
                    for t in range(NTT):
                        units.append((gi, j, t, wg8, wgd, wv8, wvd, wvi))

                def stA(u_, it_):
                    gi, j, t, wg8, wgd, wv8, wvd, wvi = u_
                    tsl = slice(t * TT, (t + 1) * TT)
                    k, b = it_ % NBUF, it_ % 2
                    gbk, vbk = b, 2 + b
                    P.mm([dict(out=ps[gbk][:], lhsT=wg8[:, kc, :], rhs=hT[:, kc, tsl], start=(kc == 0),
                               stop=(kc == 7)) for kc in range(8)],
                         R=[wgd] + [HD[kc][t] for kc in range(8)], W=[PSD[gbk]])
                    P.mm([dict(out=ps[vbk][:], lhsT=wv8[:, kc, :], rhs=hT[:, kc, tsl], start=(kc == 0),
                               stop=(kc == 7)) for kc in range(8)],
                         R=[wvd] + [HD[kc][t] for kc in range(8)], W=[PSD[vbk]])
                    if t == NTT - 1:
                        wdone(wvi)
                    conv_A(ps[gbk][:], PSD[gbk], ug[k], D("fug", k), hg, HG, t,
                           lambda jj, j=j: V(l, "fcw", (2 * j) * 3 + jj), V(l, "fcb", 2 * j), 3)
                    conv_A(ps[vbk][:], PSD[vbk], uv[k], D("fuv", k), hv, HV, t,
                           lambda jj, j=j: V(l, "fcw", (2 * j + 1) * 3 + jj), V(l, "fcb", 2 * j + 1), 3)

                def stBC(u_, it_):
                    gi, j, t, wg8, wgd, wv8, wvd, wvi = u_
                    tsl = slice(t * TT, (t + 1) * TT)
                    k, b = it_ % NBUF, it_ % 2
                    gbk, vbk = b, 2 + b
                    conv_B(ps[gbk][:], PSD[gbk], ug[k], D("fug", k), hg, HG, t,
                           lambda jj, j=j: V(l, "fcw", (2 * j) * 3 + jj), 3)
                    conv_B(ps[vbk][:], PSD[vbk], uv[k], D("fuv", k), hv, HV, t,
                           lambda jj, j=j: V(l, "fcw", (2 * j + 1) * 3 + jj), 3)
                    P.op(act, lambda e, k=k: e.activation(out=sg[k], in_=ug[k], func=AF.Silu), R=[D("fug", k)],
                         W=[D("fsg", k)])
                    P.op(pool, lambda e, k=k, gi=gi: e.tensor_tensor(out=aT[:, gi, tsl], in0=sg[k], in1=uv[k],
                                                                      op=ALU.mult),
                         R=[D("fsg", k), D("fuv", k)], W=[D("faT", gi, t)])

                stA(units[0], it)
                for ui in range(len(units)):
                    if ui + 1 < len(units):
                        stA(units[ui + 1], it + ui + 1)
                    stBC(units[ui], it + ui)
                it += len(units)
                wds = [wnext("down") for _ in G]
                for dc in range(8):
                    for t in range(NTT):
                        tsl = slice(t * TT, (t + 1) * TT)
                        bk = 4 + (dc * NTT + t) % 2
                        P.mm([dict(out=ps[bk][:], lhsT=wds[gi][0][:, dc * 128:(dc + 1) * 128], rhs=aT[:, gi, tsl],
                                   start=(gi == 0), stop=(gi == len(G) - 1)) for gi in range(len(G))],
                             R=[w_[1] for w_ in wds] + [D("faT", gi, t) for gi in range(len(G))], W=[PSD[bk]])
                        P.op(dve, lambda e, dc=dc, bk=bk: e.tensor_tensor(out=xT[:, dc, tsl], in0=ps[bk][:],
                                                                           in1=xT[:, dc, tsl], op=ALU.add),
                             R=[PSD[bk]], W=[XD[dc][t]])
                wdone(wds[-1][2])
            P.barrier()

        def final_norm():
            cv.reset()
            sq = [cv.take([128, TT], BF16) for _ in range(4)]
            lnv = [cv.take([128, TT]) for _ in range(2)]
            rstd = [cv.take([128, TT]) for _ in range(2)]
            ost = [cv.take([128, 8, TT]) for _ in range(2)]
            OUTD = D("outd")
            for t in range(NTT):
                tsl = slice(t * TT, (t + 1) * TT)
                for c in range(8):
                    b = sq[c % 4]
                    bd = D("nsq", c % 4)
                    P.op(act, lambda e, b=b, c=c: e.activation(out=b, in_=xT[:, c, tsl], func=AF.Square),
                         R=[XD[c][t]], W=[bd])
                    P.mm([dict(out=ps[0][:], lhsT=ones_b, rhs=b, start=(c == 0), stop=(c == 7))],
                         R=[bd, CONST], W=[PSD[0]])
                ld = D("nln", t % 2)
                rd = D("nrs", t % 2)
                P.op(act, lambda e: e.activation(out=lnv[t % 2], in_=ps[0][:], func=AF.Ln, scale=1.0 / D_F,
                                                 bias=EPS_AP), R=[PSD[0]] + CONSTS, W=[ld])
                P.op(act, lambda e: e.activation(out=rstd[t % 2], in_=lnv[t % 2], func=AF.Exp, scale=-0.5),
                     R=[ld], W=[rd])
                oD = D("ost", t % 2)
                for c in range(8):
                    P.op(dve, lambda e, c=c: e.scalar_tensor_tensor(out=ost[t % 2][:, c, :], in0=xT[:, c, tsl],
                                                                    scalar=gv[:, c:c + 1], in1=rstd[t % 2],
                                                                    op0=ALU.mult, op1=ALU.mult),
                         R=[XD[c][t], rd, CONST], W=[oD])
                P.dma(sp, out_d[:, :, tsl], ost[t % 2], D("outd", t % 2), R=[oD])

        for l in range(nl):
            last = (l == nl - 1)
            rmsnorm_to_hT(l, "g1")
            if stage == "big" and last:
                for rep in range(200):
                    P.mm([dict(out=ps[5][:, 0:128], lhsT=ident_b, rhs=ones_b) for _ in range(125)],
                         R=[CONST], W=[PSD[5]])
                stage = "n1"
            if stage == "n1" and last:
                dump(hT[:, :, :].rearrange("p c s -> p (c s)"), [HD[c][t] for c in range(8) for t in range(NTT)],
                     8 * S, cast=True)
                finish()
                return nc
            rg_group(l)
            if stage == "rg" and last:
                dump(yT[:, :, :].rearrange("p c s -> p (c s)"), [YD[c][t] for c in range(4) for t in range(NTT)],
                     4 * S, cast=True)
                finish()
                return nc
            wout_group(l, 0)
            if stage == "wo0" and last:
                dump(xT[:, :, :].rearrange("p c s -> p (c s)"), [XD[c][t] for c in range(8) for t in range(NTT)],
                     8 * S)
                finish()
                return nc
            if da_group(l):
                finish()
                return nc
            if stage in ("da", "da1") and last:
                dump(yT[:, :, :].rearrange("p c s -> p (c s)"), [YD[c][t] for c in range(4) for t in range(NTT)],
                     4 * S, cast=True)
                finish()
                return nc
            wout_group(l, 1)
            ml_group(l)
            if stage == "ml" and last:
                dump(yT[:, :, :].rearrange("p c s -> p (c s)"), [YD[c][t] for c in range(4) for t in range(NTT)],
                     4 * S, cast=True)
                finish()
                return nc
            wout_group(l, 2)
            if stage == "x1" and last:
                dump(xT[:, :, :].rearrange("p c s -> p (c s)"), [XD[c][t] for c in range(8) for t in range(NTT)],
                     8 * S)
                finish()
                return nc
            rmsnorm_to_hT(l, "g2")
            ffn(l)
            if stage == "x2" and last:
                dump(xT[:, :, :].rearrange("p c s -> p (c s)"), [XD[c][t] for c in range(8) for t in range(NTT)],
                     8 * S)
                finish()
                return nc
        final_norm()
        finish()
    return nc


_CACHE = {}


def make_in_maps(inputs, nl=NL, ncores=8):
    inp = {k: np.asarray(v) for k, v in inputs.items()}
    cm = const_mats()
    gvv = pack_gvec(inp)
    vecs = np.stack([pack_vec(inp, l) for l in range(nl)], axis=0)
    wblk = np.concatenate([pack_blocks(inp, l) for l in range(nl)], axis=0)
    maps = []
    for b in range(ncores):
        xb = np.ascontiguousarray(inp["x"][b].T.reshape(8, 128, S).transpose(1, 0, 2))
        posb = np.ascontiguousarray(np.broadcast_to(inp["positions"][b][None, :].astype(np.int32), (128, S)))
        maps.append({"xT": xb, "pos": posb, "cmat": cm, "gvec": gvv, "vec": vecs, "wblk": wblk})
    return maps


def kernel(**inputs):
    if "nc" not in _CACHE:
        _CACHE["nc"] = build()
    nc = _CACHE["nc"]
    maps = make_in_maps(inputs)
    res = run_bass_kernel_spmd(nc, maps, core_ids=list(range(8)))
    outs = []
    for b in range(8):
        o = res.results[b]["outT"]
        outs.append(o.transpose(2, 1, 0).reshape(S, 1024))
    return np.stack(outs, axis=0).astype(np.float32)
```

```python
import math
import numpy as np
from contextlib import ExitStack
import concourse.bass as bass
import concourse.mybir as mybir
from concourse.bass_utils import run_bass_kernel_spmd

F32 = mybir.dt.float32
BF16 = mybir.dt.bfloat16
I32 = mybir.dt.int32
AF = mybir.ActivationFunctionType
ALU = mybir.AluOpType

S = 2048
D = 1024
NL = 2
NTT = 4
TT = 512
NB = 16
EPS = 1e-6
DFF = 2816
NJ = 22
FFN_GROUPS = [list(range(0, 6)), list(range(6, 12)), list(range(12, 17)), list(range(17, 22))]
NSLOT = 12
RG_C = 8.0

_VEC = [("g1", 8), ("g2", 8), ("rgcw", 16), ("rgcb", 4), ("rgba", 4), ("rgbx", 4), ("rglam", 4), ("rgnorm", 4),
        ("mlcw", 32), ("mlcb", 8), ("fcw", 132), ("fcb", 44), ("danorm", 1), ("mlnorm", 4),
        ("ibias", 4), ("fbias", 4), ("dalam", 256)]
VOFF = {}
_o = 0
for _n, _w in _VEC:
    VOFF[_n] = _o
    _o += _w
NV = _o
GOFF = {"fn": 0, "invf": 8, "sgn": 9}
NG = 10

OFF_IN = dict(rg_x=0, rg_g=512, da_q=1024, da_k=1536, da_v=2048, ml_q=2560, ml_k=3072, ml_v=3584, ml_o=4096,
              ml_i=4608, ml_f=4612)


def block_plan():
    plan = []
    for n in range(4):
        plan.append(("in", OFF_IN["rg_x"] + n * 128))
    for n in range(4):
        plan.append(("in", OFF_IN["rg_g"] + n * 128))
    plan.append(("gatew",))
    for pr in range(4):
        plan.append(("out", 0, pr))
    for h in range(4):
        plan.append(("in", OFF_IN["da_q"] + h * 128))
        plan.append(("in", OFF_IN["da_k"] + h * 128))
        plan.append(("in", OFF_IN["da_v"] + h * 128))
    for pr in range(4):
        plan.append(("out", 1, pr))
    plan.append(("gates",))
    for h in range(4):
        plan.append(("in", OFF_IN["ml_q"] + h * 128))
        plan.append(("in", OFF_IN["ml_k"] + h * 128))
        plan.append(("in", OFF_IN["ml_v"] + h * 128))
        plan.append(("in", OFF_IN["ml_o"] + h * 128))
    for pr in range(4):
        plan.append(("out", 2, pr))
    for G in FFN_GROUPS:
        for j in G:
            plan.append(("up", j))
            plan.append(("up", DFF // 128 * 0 + j + 1000))
        for j in G:
            plan.append(("down", j))
    return plan


PLAN = block_plan()
NBLK = len(PLAN)


def pack_blocks(inp, l):
    w_in = inp["w_in"][l]
    w_out = inp["w_out"][l]
    w_up = inp["w_up"][l]
    w_down = inp["w_down"][l]
    out = np.zeros((NBLK, 128, 1024), np.float32)

    def fm(w, c0, ncols=128):
        blk = np.zeros((8, 128, 128), np.float32)
        blk[:, :, :ncols] = w[:, c0:c0 + ncols].reshape(8, 128, ncols)
        return blk.transpose(1, 0, 2).reshape(128, 1024)

    for i, b in enumerate(PLAN):
        k = b[0]
        if k == "in":
            out[i] = fm(w_in, b[1])
        elif k == "gates":
            out[i] = fm(w_in, OFF_IN["ml_i"], 8)
        elif k == "gatew":
            wa = inp["rg_wa"][l].transpose(1, 0, 2)
            wx = inp["rg_wx"][l].transpose(1, 0, 2)
            out[i] = np.stack([wa, wx], axis=1).reshape(128, 1024)
        elif k == "out":
            g, pr = b[1], b[2]
            sub = w_out[g * 512:(g + 1) * 512, pr * 256:(pr + 1) * 256]
            out[i] = sub.reshape(4, 128, 2, 128).transpose(1, 2, 0, 3).reshape(128, 1024)
        elif k == "up":
            j = b[1]
            c0 = j * 128 if j < 1000 else DFF + (j - 1000) * 128
            out[i] = fm(w_up, c0)
        elif k == "down":
            j = b[1]
            out[i] = w_down[j * 128:(j + 1) * 128, :]
    return out


def pack_vec(inp, l):
    v = np.zeros((128, NV), np.float32)

    def put(name, arr):
        arr = np.asarray(arr, np.float32)
        v[:, VOFF[name]:VOFF[name] + arr.shape[1]] = arr

    put("g1", inp["attn_norm"][l].reshape(8, 128).T)
    put("g2", inp["mlp_norm"][l].reshape(8, 128).T)
    put("rgcw", inp["rg_conv_w"][l].reshape(4, 4, 128).transpose(2, 1, 0).reshape(128, 16))
    put("rgcb", inp["rg_conv_b"][l].reshape(4, 128).T)
    put("rgba", inp["rg_ba"][l].reshape(4, 128).T)
    put("rgbx", inp["rg_bx"][l].reshape(4, 128).T)
    put("rglam", inp["rg_lambda"][l].reshape(4, 128).T)
    put("rgnorm", inp["rg_norm"][l].reshape(4, 128).T)
    put("mlcw", inp["ml_conv_w"][l].reshape(4, 8, 128).transpose(2, 1, 0).reshape(128, 32))
    put("mlcb", inp["ml_conv_b"][l].reshape(8, 128).T)
    fw = inp["ffn_conv_w"][l]
    fb = inp["ffn_conv_b"][l]
    fcw = np.zeros((128, 44, 3), np.float32)
    fcb = np.zeros((128, 44), np.float32)
    for j in range(NJ):
        fcw[:, 2 * j, :] = fw[:, j * 128:(j + 1) * 128].T
        fcw[:, 2 * j + 1, :] = fw[:, DFF + j * 128:DFF + (j + 1) * 128].T
        fcb[:, 2 * j] = fb[j * 128:(j + 1) * 128]
        fcb[:, 2 * j + 1] = fb[DFF + j * 128:DFF + (j + 1) * 128]
    put("fcw", fcw.reshape(128, 132))
    put("fcb", fcb)
    put("danorm", inp["da_norm"][l].reshape(128, 1))
    put("mlnorm", inp["ml_norm"][l].reshape(4, 128).T)
    put("ibias", np.broadcast_to(inp["ml_i_bias"][l][None, :], (128, 4)))
    put("fbias", np.broadcast_to(inp["ml_f_bias"][l][None, :], (128, 4)))
    put("dalam", np.broadcast_to(inp["da_lambda"][l].reshape(1, 256), (128, 256)))
    return v


def const_mats():
    ident = np.eye(128, dtype=np.float32)
    perm = np.zeros((128, 128), np.float32)
    for base in (0, 64):
        for r in range(8):
            perm[base + r + 8, base + r] = 1.0
            perm[base + r, base + r + 8] = 1.0
    kk = np.arange(128)[:, None]
    qq = np.arange(128)[None, :]
    maskneg = np.where(kk > qq, -1e30, 0.0).astype(np.float32)
    tri = (qq >= kk).astype(np.float32)
    ones = np.ones((128, 128), np.float32)
    return np.stack([ident, perm, maskneg, tri, ones], axis=1)


def pack_gvec(inp):
    g = np.zeros((128, NG), np.float32)
    g[:, 0:8] = inp["final_norm"].reshape(8, 128).T
    inv = (500000.0 ** (-np.arange(0, 16, 2, dtype=np.float32) / 16.0)).astype(np.float32)
    for base in (0, 64):
        for r in range(8):
            g[base + r, 8] = inv[r]
            g[base + r + 8, 8] = inv[r]
            g[base + r, 9] = -1.0
            g[base + r + 8, 9] = 1.0
    return g


class Dep:
    __slots__ = ("w", "r", "sem", "nd", "x", "e")

    def __init__(self):
        self.x = False
        self.e = []
        self.w = None
        self.r = {}
        self.sem = None
        self.nd = 0


class Queue:
    def __init__(self, name, eng):
        self.name = name
        self.eng = eng
        self.sem = None
        self.cnt = 0
        self.seen = {}
        self.nsem = 0


class Prog:
    MAXC = 20000

    def __init__(self, nc, es):
        self.nc = nc
        self.es = es
        self.deps = {}
        self.pe = Queue("pe", nc.tensor)
        self.act = Queue("act", nc.scalar)
        self.dve = Queue("dve", nc.vector)
        self.pool = Queue("pool", nc.gpsimd)
        self.sp = Queue("sp", nc.sync)
        self.nsems = 0
        self.semkeep = []

    def newsem(self, name):
        self.nsems += 1
        h = self.es.enter_context(self.nc.semaphore(name))
        self.semkeep.append(h)
        return h

    def D(self, *key):
        d = self.deps.get(key)
        if d is None:
            d = Dep()
            self.deps[key] = d
        return d

    def _wait(self, q, toks):
        need = {}
        for t in toks:
            if t is None:
                continue
            sem, val = t
            k = id(sem)
            if q.seen.get(k, 0) >= val:
                continue
            if k not in need or need[k][1] < val:
                need[k] = (sem, val)
        for k, (sem, val) in need.items():
            q.eng.wait_ge(sem, val)
            q.seen[k] = val

    def _collect(self, R, W):
        toks = []
        for d in R:
            toks.append(d.w)
            toks.extend(d.e)
            if d.x:
                toks.extend(d.r.values())
        for d in W:
            toks.append(d.w)
            toks.extend(d.r.values())
        return toks

    def _mark(self, tok, R, W):
        k = id(tok[0])
        for d in R:
            d.r[k] = tok
        for d in W:
            d.w = tok
            d.r = {}

    def _signal(self, q, ins):
        if q.sem is None or q.cnt >= self.MAXC:
            q.nsem += 1
            q.sem = self.newsem(f"{q.name}{q.nsem}")
            q.cnt = 0
        q.cnt += 1
        ins.then_inc(q.sem, 1)
        return (q.sem, q.cnt)

    def op(self, q, fn, R=(), W=()):
        self._wait(q, self._collect(R, W))
        ins = fn(q.eng)
        tok = self._signal(q, ins)
        self._mark(tok, R, W)
        return tok

    def mm(self, mms, R=(), W=()):
        q = self.pe
        self._wait(q, self._collect(R, W))
        ins = None
        for kw in mms:
            if kw.pop("tr", False):
                ins = q.eng.transpose(kw["out"], kw["in_"], kw["identity"])
            else:
                ins = q.eng.matmul(kw["out"], lhsT=kw["lhsT"], rhs=kw["rhs"], start=kw.get("start", True),
                                   stop=kw.get("stop", True), skip_group_check=kw.get("sgc", False))
        tok = self._signal(q, ins)
        self._mark(tok, R, W)
        return tok

    def barrier(self):
        toks = []
        for key, d in self.deps.items():
            if key[0] == "wslot":
                continue
            toks.append(d.w)
            toks.extend(d.r.values())
        self._wait(self.dve, toks)
        ins = self.dve.eng.memset(self.bar_ap, 0.0)
        tok = self._signal(self.dve, ins)
        for q in (self.pe, self.act, self.pool, self.sp):
            self._wait(q, [tok])

    def dma(self, q, out, in_, semdep, R=(), W=(), **kw):
        self._wait(q, self._collect(R, W))
        ins = q.eng.dma_start(out=out, in_=in_, **kw)
        if semdep.sem is None:
            semdep.sem = self.newsem(f"dma{self.nsems}")
        semdep.nd += 1
        ins.then_inc(semdep.sem, 16)
        tok = (semdep.sem, 16 * semdep.nd)
        self._mark(tok, R, W)
        return tok


def build(nl=NL, stage="all", dbg_cols=0):
    nc = bass.Bass("TRN2", target_bir_lowering=False)
    xT_d = nc.dram_tensor("xT", [128, 8, S], F32, kind="ExternalInput").ap()
    pos_d = nc.dram_tensor("pos", [128, S], I32, kind="ExternalInput").ap()
    cm_d = nc.dram_tensor("cmat", [128, 5, 128], F32, kind="ExternalInput").ap()
    gv_d = nc.dram_tensor("gvec", [128, NG], F32, kind="ExternalInput").ap()
    vec_d = nc.dram_tensor("vec", [nl, 128, NV], F32, kind="ExternalInput").ap()
    wb_d = nc.dram_tensor("wblk", [nl * NBLK, 128, 1024], F32, kind="ExternalInput").ap()
    out_d = nc.dram_tensor("outT", [128, 8, S], F32, kind="ExternalOutput").ap()
    dbg_d = None
    if dbg_cols:
        dbg_d = nc.dram_tensor("dbg", [128, dbg_cols], F32, kind="ExternalOutput").ap()

    with ExitStack() as es:
        P = Prog(nc, es)
        D = P.D
        pe, act, dve, pool, sp = P.pe, P.act, P.dve, P.pool, P.sp

        def sb(name, shape, dt=F32):
            return es.enter_context(nc.sbuf_tensor("s_" + name, shape, dt))

        xT = sb("xT", [128, 8, S])
        hT = sb("hT", [128, 8, S], BF16)
        yT = sb("yT", [128, 4, S], BF16)
        ropeC = sb("ropeC", [128, S])
        ropeS = sb("ropeS", [128, S])
        cmb = sb("cmb", [128, 5, 128], BF16)
        cmf = sb("cmf", [128, 2, 128], F32)
        gv = sb("gv", [128, NG])
        vec = sb("vec", [128, nl, NV])
        wst = sb("wst", [128, NSLOT, 1024], BF16)
        SCR = 46 * 1024
        scr = sb("scr", [128, SCR // 4])
        ps = [es.enter_context(nc.psum_tensor(f"ps{i}", [128, 512], F32)) for i in range(7)]
        psT = es.enter_context(nc.psum_tensor("psT", [128, 1024], BF16))
        PSD = [D("ps", i) for i in range(7)]
        PSTD = D("psT")
        for d_ in PSD + [PSTD]:
            d_.x = True

        ident_b = cmb[:, 0, :]
        perm_b = cmb[:, 1, :]
        maskneg_b = cmb[:, 2, :]
        tri_b = cmb[:, 3, :]
        ones_b = cmb[:, 4, :]
        tri_f = cmf[:, 0, :]
        ones_f = cmf[:, 1, :]

        class Carver:
            def __init__(self):
                self.off = 0

            def reset(self):
                self.off = 0

            def take(self, shape, dt=F32):
                n = 1
                for s_ in shape[1:]:
                    n *= s_
                words = n if dt == F32 or dt == I32 else (n + 1) // 2
                assert self.off + words <= SCR // 4, (self.off, words)
                v = scr[:, self.off:self.off + words]
                self.off += words
                if dt == BF16:
                    v = v.bitcast(BF16)[:, 0:n]
                elif dt == I32:
                    v = v.bitcast(I32)
                if len(shape) == 3:
                    v = v.rearrange("p (a b) -> p a b", a=shape[1])
                elif len(shape) == 4:
                    v = v.rearrange("p (a b c) -> p a b c", a=shape[1], b=shape[2])
                return v

        cv = Carver()
        SCRD = D("scr")

        wstate = {"next_load": 0, "next_use": 0}
        total_blocks = nl * NBLK

        def wslotD(i):
            return D("wslot", i % NSLOT)

        def prefetch(upto):
            upto = min(upto, total_blocks)
            while wstate["next_load"] < upto:
                i = wstate["next_load"]
                P.dma(pool, wst[:, i % NSLOT, :], wb_d[i], wslotD(i), W=[wslotD(i)], max_dma_last_dim=4096)
                wstate["next_load"] += 1

        def wnext(expect=None):
            i = wstate["next_use"]
            if expect is not None:
                assert PLAN[i % NBLK][0] == expect, (PLAN[i % NBLK], expect)
            prefetch(i + 1)
            wstate["next_use"] += 1
            return wst[:, i % NSLOT, :], wslotD(i), i

        def wdone(i):
            prefetch(i + NSLOT + 1)

        dbgstate = {"off": 0}

        def dump(ap, deps, ncols, cast=False):
            o = dbgstate["off"]
            q = pool if cast else sp
            P.dma(q, dbg_d[:, o:o + ncols], ap, D("dbgout"), R=deps)
            dbgstate["off"] += ncols

        def finish():
            toks = [(d.sem, 16 * d.nd) for d in P.deps.values() if d.sem is not None]
            for q in (sp, pool):
                P._wait(q, toks)

        XD = [[D("xT", c, t) for t in range(NTT)] for c in range(8)]
        HD = [[D("hT", c, t) for t in range(NTT)] for c in range(8)]
        YD = [[D("yT", c, t) for t in range(NTT)] for c in range(4)]
        CONST = D("const")
        for c in range(8):
            xtok = P.dma(sp, xT[:, c, :], xT_d[:, c, :], D("xload"), W=[XD[c][t] for t in range(NTT)])
        for c in range(8):
            for t in range(NTT):
                XD[c][t].w = xtok
        P.dma(sp, cmf[:], cm_d[:, 3:5, :], CONST, W=[CONST])
        P.dma(sp, gv[:], gv_d, CONST, W=[CONST])
        for l in range(nl):
            P.dma(sp, vec[:, l, :], vec_d[l], CONST, W=[CONST])
        CONST.e.append(P.dma(pool, cmb[:], cm_d, D("constb")))
        prefetch(NSLOT)

        D_F = 1024.0
        epst = sb("epst", [128, 4])
        P.op(dve, lambda e: e.memset(epst[:, 0:1], EPS), W=[D("epst")])
        P.op(dve, lambda e: e.memset(epst[:, 1:2], 1.0), W=[D("epst")])
        P.op(dve, lambda e: e.memset(epst[:, 2:3], 0.0), W=[D("epst")])
        P.bar_ap = epst[:, 3:4]
        EPS_AP = epst[:, 0:1]
        ONE_AP = epst[:, 1:2]
        CONSTS = [CONST, D("epst")]
        ROPE = D("rope")
        cv.reset()
        posi = cv.take([128, S], I32)
        tA = cv.take([128, S])
        tB = cv.take([128, S])
        tK = cv.take([128, S], I32)
        P.dma(sp, posi, pos_d, D("posload"), W=[D("posi")])
        TWO_PI = 2.0 * math.pi
        C1 = 6.28125
        C2 = TWO_PI - C1
        P.op(dve, lambda e: e.tensor_copy(out=tA, in_=posi), R=[D("posi")], W=[D("tA")])
        P.op(dve, lambda e: e.tensor_scalar(out=tA, in0=tA, scalar1=gv[:, 8:9], scalar2=None, op0=ALU.mult),
             R=[CONST], W=[D("tA")])
        P.op(dve, lambda e: e.tensor_scalar(out=tK, in0=tA, scalar1=1.0 / TWO_PI, scalar2=None, op0=ALU.mult),
             R=[D("tA")], W=[D("tK")])
        P.op(dve, lambda e: e.tensor_copy(out=tB, in_=tK), R=[D("tK")], W=[D("tB")])
        P.op(dve, lambda e: e.scalar_tensor_tensor(out=tA, in0=tB, scalar=-C1, in1=tA, op0=ALU.mult, op1=ALU.add),
             R=[D("tB")], W=[D("tA")])
        P.op(dve, lambda e: e.scalar_tensor_tensor(out=tA, in0=tB, scalar=-C2, in1=tA, op0=ALU.mult, op1=ALU.add),
             R=[D("tB")], W=[D("tA")])

        def wrap(t, dname):
            P.op(dve, lambda e: e.tensor_scalar(out=tB, in0=t, scalar1=math.pi, scalar2=-TWO_PI, op0=ALU.is_gt,
                                                op1=ALU.mult), R=[D(dname)], W=[D("tB")])
            P.op(dve, lambda e: e.tensor_tensor(out=t, in0=t, in1=tB, op=ALU.add), R=[D("tB")], W=[D(dname)])
            P.op(dve, lambda e: e.tensor_scalar(out=tB, in0=t, scalar1=-math.pi, scalar2=TWO_PI, op0=ALU.is_lt,
                                                op1=ALU.mult), R=[D(dname)], W=[D("tB")])
            P.op(dve, lambda e: e.tensor_tensor(out=t, in0=t, in1=tB, op=ALU.add), R=[D("tB")], W=[D(dname)])
            P.op(dve, lambda e: e.tensor_scalar(out=t, in0=t, scalar1=3.1415925, scalar2=-3.1415925, op0=ALU.min,
                                                op1=ALU.max), R=[], W=[D(dname)])

        wrap(tA, "tA")
        P.op(act, lambda e: e.activation(out=ropeS[:], in_=tA, func=AF.Sin, scale=gv[:, 9:10]),
             R=[D("tA"), CONST], W=[ROPE])
        P.op(dve, lambda e: e.tensor_scalar(out=tA, in0=tA, scalar1=math.pi / 2, scalar2=None, op0=ALU.add),
             R=[ROPE], W=[D("tA")])
        wrap(tA, "tA")
        P.op(act, lambda e: e.activation(out=ropeC[:], in_=tA, func=AF.Sin), R=[D("tA")], W=[ROPE])
        P.barrier()

        def V(l, name, j=0, n=1):
            o = VOFF[name] + j
            return vec[:, l, o:o + n]

        def rmsnorm_to_hT(l, gname):
            cv.reset()
            sq = [cv.take([128, TT], BF16) for _ in range(4)]
            lnv = [cv.take([128, TT]) for _ in range(2)]
            rstd = [cv.take([128, TT]) for _ in range(2)]
            for t in range(NTT):
                tsl = slice(t * TT, (t + 1) * TT)
                for c in range(8):
                    b = sq[c % 4]
                    bd = D("nsq", c % 4)
                    if c % 2 == 0:
                        P.op(act, lambda e, b=b, c=c: e.activation(out=b, in_=xT[:, c, tsl], func=AF.Square),
                             R=[XD[c][t]], W=[bd])
                    else:
                        P.op(pool, lambda e, b=b, c=c: e.tensor_tensor(out=b, in0=xT[:, c, tsl], in1=xT[:, c, tsl],
                                                                          op=ALU.mult), R=[XD[c][t]], W=[bd])
                    P.mm([dict(out=ps[0][:], lhsT=ones_b, rhs=b, start=(c == 0), stop=(c == 7))],
                         R=[bd, CONST], W=[PSD[0]])
                ld = D("nln", t % 2)
                rd = D("nrs", t % 2)
                P.op(act, lambda e: e.activation(out=lnv[t % 2], in_=ps[0][:], func=AF.Ln, scale=1.0 / D_F, bias=EPS_AP),
                     R=[PSD[0]], W=[ld])
                P.op(act, lambda e: e.activation(out=rstd[t % 2], in_=lnv[t % 2], func=AF.Exp, scale=-0.5),
                     R=[ld], W=[rd])
                for c in range(8):
                    P.op(dve, lambda e, c=c: e.scalar_tensor_tensor(out=hT[:, c, tsl], in0=xT[:, c, tsl],
                                                                    scalar=V(l, gname, c), in1=rstd[t % 2],
                                                                    op0=ALU.mult, op1=ALU.mult),
                         R=[XD[c][t], rd, CONST], W=[HD[c][t]])
            P.barrier()


        def conv_A(src_ps, srcD, u, uD, halo, haloD, t, wcol, bcol, ntap):
            K1 = ntap - 1
            hm = len(haloD)
            P.op(act, lambda e: e.activation(out=u, in_=src_ps, func=AF.Identity, scale=wcol(K1), bias=bcol),
                 R=[srcD] + CONSTS, W=[uD])
            if t < NTT - 1:
                P.op(act, lambda e: e.activation(out=halo[:, t % hm, :], in_=src_ps[:, TT - K1:TT], func=AF.Copy),
                     R=[srcD], W=[haloD[t % hm]])

        def conv_B(src_ps, srcD, u, uD, halo, haloD, t, wcol, ntap):
            K1 = ntap - 1
            hm = len(haloD)
            for j in range(K1):
                sh = K1 - j
                P.op(dve, lambda e, j=j, sh=sh: e.scalar_tensor_tensor(out=u[:, sh:TT], in0=src_ps[:, 0:TT - sh],
                                                                       scalar=wcol(j), in1=u[:, sh:TT],
                                                                       op0=ALU.mult, op1=ALU.add),
                     R=[srcD] + CONSTS, W=[uD])
                if t > 0:
                    hp = halo[:, (t - 1) % hm, :]
                    P.op(dve, lambda e, j=j, sh=sh, hp=hp: e.scalar_tensor_tensor(
                        out=u[:, 0:sh], in0=hp[:, K1 - sh:K1], scalar=wcol(j), in1=u[:, 0:sh], op0=ALU.mult,
                        op1=ALU.add), R=[haloD[(t - 1) % hm]] + CONSTS, W=[uD])

        def conv_taps(src_ps, srcD, u, uD, halo, haloD, t, wcol, bcol, ntap, l):
            conv_A(src_ps, srcD, u, uD, halo, haloD, t, wcol, bcol, ntap)
            conv_B(src_ps, srcD, u, uD, halo, haloD, t, wcol, ntap)

        def wout_group(l, g):
            for pr in range(4):
                wsl, wd, wi = wnext("out")
                w4 = wsl.rearrange("p (d f c) -> p d f c", d=2, f=4)
                for d2 in range(2):
                    dc = pr * 2 + d2
                    for t in range(NTT):
                        tsl = slice(t * TT, (t + 1) * TT)
                        bk = (d2 * NTT + t) % 2
                        P.mm([dict(out=ps[bk][:], lhsT=w4[:, d2, f, :], rhs=yT[:, f, tsl], start=(f == 0),
                                   stop=(f == 3)) for f in range(4)],
                             R=[wd] + [YD[f][t] for f in range(4)], W=[PSD[bk]])
                        P.op(dve, lambda e, dc=dc, bk=bk: e.tensor_tensor(out=xT[:, dc, tsl], in0=ps[bk][:],
                                                                           in1=xT[:, dc, tsl], op=ALU.add),
                             R=[PSD[bk]], W=[XD[dc][t]])
                wdone(wi)

        def rg_group(l):
            cv.reset()
            ws = [wnext("in") for _ in range(8)]
            gw, gwd, gwi = wnext("gatew")
            gw4 = gw.rearrange("p (a n j) -> p a n j", a=2, n=4)
            nls = cv.take([128, 4])
            tmp4 = cv.take([128, 4])
            P.op(act, lambda e: e.activation(out=tmp4, in_=V(l, "rglam", 0, 4), func=AF.Exp, scale=-1.0),
                 R=CONSTS, W=[D("rgtmp4")])
            P.op(act, lambda e: e.activation(out=tmp4, in_=tmp4, func=AF.Ln, bias=ONE_AP), R=CONSTS,
                 W=[D("rgtmp4")])
            P.op(dve, lambda e: e.tensor_scalar(out=nls, in0=tmp4, scalar1=-RG_C, scalar2=None, op0=ALU.mult),
                 R=[D("rgtmp4")], W=[D("rgnls")])
            nls2 = cv.take([128, 4])
            P.op(dve, lambda e: e.tensor_scalar(out=nls2, in0=tmp4, scalar1=-2.0 * RG_C, scalar2=None,
                                                op0=ALU.mult), R=[D("rgtmp4")], W=[D("rgnls")])
            halo = [cv.take([128, 4, 3]) for _ in range(4)]
            HAL = [[D("rghalo", n, i_) for i_ in range(4)] for n in range(4)]
            hst = cv.take([128, 4, 2])
            u = [cv.take([128, TT]) for _ in range(3)]
            gg = [cv.take([128, TT]) for _ in range(3)]
            ub = [cv.take([128, TT], BF16) for _ in range(2)]
            rr = [cv.take([128, TT]) for _ in range(2)]
            ig = [cv.take([128, TT]) for _ in range(2)]
            aa = [cv.take([128, TT]) for _ in range(2)]
            hh = [cv.take([128, TT]) for _ in range(2)]
            bt = cv.take([128, TT])
            ypre = cv.take([128, 4, TT])
            ysq = [cv.take([128, TT], BF16) for _ in range(2)]
            lnv = cv.take([128, TT])
            units = [(t, n) for t in range(NTT) for n in range(4)]
            NU = len(units)

            def S1(i):
                t, n = units[i]
                tsl = slice(t * TT, (t + 1) * TT)
                k3, b = i % 3, i % 2
                xb, gb = b, 2 + b
                wx_, wxd, _ = ws[n]
                wg_, wgd, _ = ws[4 + n]
                w8x = wx_.rearrange("p (k c) -> p k c", k=8)
                w8g = wg_.rearrange("p (k c) -> p k c", k=8)
                P.mm([dict(out=ps[xb][:], lhsT=w8x[:, kc, :], rhs=hT[:, kc, tsl], start=(kc == 0), stop=(kc == 7))
                      for kc in range(8)], R=[wxd] + [HD[kc][t] for kc in range(8)], W=[PSD[xb]])
                P.mm([dict(out=ps[gb][:], lhsT=w8g[:, kc, :], rhs=hT[:, kc, tsl], start=(kc == 0), stop=(kc == 7))
                      for kc in range(8)], R=[wgd] + [HD[kc][t] for kc in range(8)], W=[PSD[gb]])
                conv_A(ps[xb][:], PSD[xb], u[k3], D("rgu", k3), halo[n], HAL[n], t,
                       lambda j, n=n: V(l, "rgcw", n * 4 + j), V(l, "rgcb", n), 4)
                P.op(act, lambda e: e.activation(out=gg[k3], in_=ps[gb][:], func=AF.Square), R=[PSD[gb]],
                     W=[D("rgg", k3)])

            def S2(i):
                t, n = units[i]
                k3, b = i % 3, i % 2
                xb, gb, rb, ib = b, 2 + b, 4, 5
                uD, gD = D("rgu", k3), D("rgg", k3)
                conv_B(ps[xb][:], PSD[xb], u[k3], uD, halo[n], HAL[n], t, lambda j, n=n: V(l, "rgcw", n * 4 + j), 4)
                P.op(dve, lambda e: e.tensor_copy(out=ub[b], in_=u[k3]), R=[uD], W=[D("rgub", b)])
                P.op(dve, lambda e: e.tensor_scalar(out=gg[k3], in0=gg[k3], scalar1=0.044715, scalar2=1.0,
                                                    op0=ALU.mult, op1=ALU.add), R=[], W=[gD])
                P.op(dve, lambda e: e.tensor_tensor(out=gg[k3], in0=ps[gb][:], in1=gg[k3], op=ALU.mult),
                     R=[PSD[gb]], W=[gD])
                P.mm([dict(out=ps[rb][:], lhsT=gw4[:, 0, n, :], rhs=ub[b])], R=[gwd, D("rgub", b)], W=[PSD[rb]])
                P.mm([dict(out=ps[ib][:], lhsT=gw4[:, 1, n, :], rhs=ub[b])], R=[gwd, D("rgub", b)], W=[PSD[ib]])
                P.op(act, lambda e: e.activation(out=gg[k3], in_=gg[k3], func=AF.Sigmoid, scale=1.5957691216057308),
                     R=[], W=[gD])
                P.op(act, lambda e: e.activation(out=rr[b], in_=ps[rb][:], func=AF.Sigmoid, bias=V(l, "rgba", n)),
                     R=[PSD[rb]] + CONSTS, W=[D("rgr", b)])
                P.op(act, lambda e: e.activation(out=ig[b], in_=ps[ib][:], func=AF.Sigmoid, bias=V(l, "rgbx", n)),
                     R=[PSD[ib]] + CONSTS, W=[D("rgi", b)])
                P.op(dve, lambda e: e.tensor_tensor(out=gg[k3], in0=ps[gb][:], in1=gg[k3], op=ALU.mult),
                     R=[PSD[gb]], W=[gD])
                P.op(pool, lambda e: e.tensor_tensor(out=ig[b], in0=ig[b], in1=u[k3], op=ALU.mult), R=[uD],
                     W=[D("rgi", b)])

            def S3(i):
                t, n = units[i]
                tsl = slice(t * TT, (t + 1) * TT)
                k3, b = i % 3, i % 2
                uD, gD = D("rgu", k3), D("rgg", k3)
                aD, bD, hD = D("rga", b), D("rgbt"), D("rgh", b)
                P.op(act, lambda e: e.activation(out=aa[b], in_=rr[b], func=AF.Exp, scale=nls[:, n:n + 1]),
                     R=[D("rgr", b), D("rgnls")], W=[aD])
                P.op(act, lambda e: e.activation(out=bt, in_=rr[b], func=AF.Exp, scale=nls2[:, n:n + 1]),
                     R=[D("rgr", b), D("rgnls")], W=[bD])
                P.op(act, lambda e: e.activation(out=bt, in_=bt, func=AF.Ln, scale=-1.0, bias=ONE_AP), R=CONSTS,
                     W=[bD])
                P.op(act, lambda e: e.activation(out=bt, in_=bt, func=AF.Exp, scale=0.5), R=[], W=[bD])

            def S3b(i):
                t, n = units[i]
                tsl = slice(t * TT, (t + 1) * TT)
                k3, b = i % 3, i % 2
                uD, gD = D("rgu", k3), D("rgg", k3)
                aD, bD, hD = D("rga", b), D("rgbt"), D("rgh", b)
                P.op(dve, lambda e: e.tensor_tensor(out=bt, in0=bt, in1=ig[b], op=ALU.mult), R=[D("rgi", b)],
                     W=[bD])
                if t == 0:
                    init, initR = 0.0, []
                else:
                    init = hst[:, n, (t - 1) % 2:(t - 1) % 2 + 1]
                    initR = [D("rghst", n, (t - 1) % 2)]
                P.op(dve, lambda e: e.tensor_tensor_scan(out=hh[b], data0=aa[b], data1=bt, initial=init,
                                                         op0=ALU.mult, op1=ALU.add), R=[aD, bD] + initR, W=[hD])
                if t < NTT - 1:
                    P.op(pool, lambda e: e.tensor_copy(out=hst[:, n, t % 2:t % 2 + 1], in_=hh[b][:, TT - 1:TT]),
                         R=[hD], W=[D("rghst", n, t % 2)])
                yD = D("rgy", n)
                P.op(dve, lambda e: e.tensor_tensor(out=ypre[:, n, :], in0=gg[k3], in1=hh[b], op=ALU.mult),
                     R=[gD, hD], W=[yD])
                sD = D("rgysq", b)
                P.op(pool, lambda e: e.tensor_tensor(out=ysq[b], in0=ypre[:, n, :], in1=ypre[:, n, :], op=ALU.mult),
                     R=[yD], W=[sD])

            def SS(i):
                t, n = units[i]
                tsl = slice(t * TT, (t + 1) * TT)
                b = i % 2
                P.mm([dict(out=ps[6][:], lhsT=ones_b, rhs=ysq[b], start=(n == 0), stop=(n == 3))],
                     R=[D("rgysq", b), CONST], W=[PSD[6]])
                if n == 3:
                    P.op(act, lambda e: e.activation(out=lnv, in_=ps[6][:], func=AF.Ln, scale=1.0 / 512.0,
                                                     bias=EPS_AP), R=[PSD[6]] + CONSTS, W=[D("rgln")])
                    P.op(act, lambda e: e.activation(out=lnv, in_=lnv, func=AF.Exp, scale=-0.5), R=[],
                         W=[D("rgln")])
                    for n2 in range(4):
                        P.op(dve, lambda e, n2=n2: e.scalar_tensor_tensor(out=yT[:, n2, tsl], in0=ypre[:, n2, :],
                                                                          scalar=V(l, "rgnorm", n2), in1=lnv,
                                                                          op0=ALU.mult, op1=ALU.mult),
                             R=[D("rgy", n2), D("rgln")] + CONSTS, W=[YD[n2][t]])

            S1(0)
            S1(1)
            S2(0)
            for i in range(NU):
                if i + 2 < NU:
                    S1(i + 2)
                S3(i)
                if i + 1 < NU:
                    S2(i + 1)
                if i > 0:
                    SS(i - 1)
                S3b(i)
            SS(NU - 1)
            wdone(gwi)
            P.barrier()

        def da_group(l):
            lambda_init = 0.8 - 0.6 * math.exp(-0.3 * l)
            cv.reset()
            junk = cv.take([128, 128])
            s12 = cv.take([128, 2])
            nlam = cv.take([128, 1])
            dl = V(l, "dalam", 0, 256)
            LD = D("dalam_t")
            for i_ in range(2):
                P.op(dve, lambda e, i_=i_: e.tensor_tensor(out=junk[:, 0:64], in0=dl[:, i_ * 128:i_ * 128 + 64],
                                                           in1=dl[:, i_ * 128 + 64:i_ * 128 + 128], op=ALU.mult),
                     R=CONSTS, W=[LD])
                P.op(dve, lambda e, i_=i_: e.tensor_scalar(out=junk[:, 64:128], in0=junk[:, 0:64], scalar1=1.0,
                                                           scalar2=None, op0=ALU.mult, op1=ALU.add,
                                                           accum_out=s12[:, i_:i_ + 1]), R=[], W=[LD])
            P.op(act, lambda e: e.activation(out=s12, in_=s12, func=AF.Exp), R=[], W=[LD])
            P.op(dve, lambda e: e.tensor_tensor(out=nlam, in0=s12[:, 1:2], in1=s12[:, 0:1], op=ALU.subtract), R=[],
                 W=[LD])
            P.op(dve, lambda e: e.tensor_scalar(out=nlam, in0=nlam, scalar1=-lambda_init, scalar2=None, op0=ALU.add),
                 R=[], W=[LD])
            if stage == "dalam":
                dump(nlam, [LD], 1)
                return True
            qT = cv.take([128, S], BF16)
            kTz = [cv.take([128, S], BF16) for _ in range(2)]
            kT = kTz[0]
            vaug = cv.take([128, NB, 130], BF16)
            qb = [cv.take([128, TT], BF16) for _ in range(2)]
            t1 = [cv.take([128, TT]) for _ in range(2)]
            t2 = [cv.take([128, TT]) for _ in range(2)]
            PT = [[cv.take([128, TT], BF16) for _ in range(2)] for _ in range(2)]
            rrA = cv.take([128, TT])
            rrB = cv.take([128, TT])
            o1 = cv.take([128, TT])
            o2 = cv.take([128, TT])
            sqb = cv.take([128, TT], BF16)
            lnv = cv.take([128, TT])
            lbias = cv.take([128, 1])
            P.op(dve, lambda e: e.memset(lbias, math.log(1.0 - lambda_init)), W=[D("dalb")])
            psTf = psT[:].bitcast(F32)
            pending = []
            QD = [D("daq", t) for t in range(NTT)]
            KD = [D("dak", t) for t in range(NTT)]
            VD = [D("dav", g) for g in range(4)]
            P.op(dve, lambda e: e.memset(vaug[:, :, 128:129], 1.0), W=VD)
            P.op(dve, lambda e: e.memset(kTz[0][64:128, :], 0.0), W=[D("dakz")])
            P.op(dve, lambda e: e.memset(kTz[1][0:64, :], 0.0), W=[D("dakz")])
            ctr = {"rp": 0, "st": 0, "ep": 0, "tp": 0}
            for h in range(4):
                wq, wqd, _ = wnext("in")
                wk, wkd, _ = wnext("in")
                wv, wvd, wvi = wnext("in")
                wq8 = wq.rearrange("p (k c) -> p k c", k=8)
                wk8 = wk.rearrange("p (k c) -> p k c", k=8)
                wv8 = wv.rearrange("p (k c) -> p k c", k=8)
                punits = [(t, which) for t in range(NTT) for which in range(2)]
                pinfo = [(wq8, wqd, qT, QD, 0.125), (wk8, wkd, kT, KD, 1.0)]
                base = ctr["rp"]

                def prA(ui):
                    t, which = punits[ui]
                    w8, wd_, dst, dstD, scl = pinfo[which]
                    tsl = slice(t * TT, (t + 1) * TT)
                    k = (base + ui) % 2
                    pb = k
                    P.mm([dict(out=ps[pb][:], lhsT=w8[:, kc, :], rhs=hT[:, kc, tsl], start=(kc == 0),
                               stop=(kc == 7)) for kc in range(8)],
                         R=[wd_] + [HD[kc][t] for kc in range(8)], W=[PSD[pb]])
                    P.op(act, lambda e: e.activation(out=qb[k], in_=ps[pb][:], func=AF.Copy), R=[PSD[pb]],
                         W=[D("daqb", k)])

                def prB(ui):
                    t, which = punits[ui]
                    w8, wd_, dst, dstD, scl = pinfo[which]
                    tsl = slice(t * TT, (t + 1) * TT)
                    k = (base + ui) % 2
                    pb, sbk = k, 2 + k
                    P.mm([dict(out=ps[sbk][:], lhsT=perm_b, rhs=qb[k])], R=[D("daqb", k), CONST], W=[PSD[sbk]])
                    P.op(dve, lambda e: e.scalar_tensor_tensor(out=t1[k], in0=ps[pb][:], scalar=scl,
                                                               in1=ropeC[:, tsl], op0=ALU.mult, op1=ALU.mult),
                         R=[PSD[pb], ROPE], W=[D("dat1", k)])
                    P.op(dve, lambda e: e.scalar_tensor_tensor(out=t2[k], in0=ps[sbk][:], scalar=scl,
                                                               in1=ropeS[:, tsl], op0=ALU.mult, op1=ALU.mult),
                         R=[PSD[sbk], ROPE], W=[D("dat2", k)])
                    if which == 0:
                        P.op(pool, lambda e: e.tensor_tensor(out=dst[:, tsl], in0=t1[k], in1=t2[k], op=ALU.add),
                             R=[D("dat1", k), D("dat2", k)], W=[dstD[t]])
                    else:
                        for c_ in range(2):
                            pr_ = slice(c_ * 64, (c_ + 1) * 64)
                            P.op(pool, lambda e, c_=c_, pr_=pr_: e.tensor_tensor(
                                out=kTz[c_][pr_, tsl], in0=t1[k][pr_, :], in1=t2[k][pr_, :], op=ALU.add),
                                R=[D("dat1", k), D("dat2", k), D("dakz")], W=[dstD[t]])

                prA(0)
                for ui in range(len(punits)):
                    if ui + 1 < len(punits):
                        prA(ui + 1)
                    prB(ui)
                ctr["rp"] += len(punits)
                for g4 in range(4):
                    vb = 4 + (g4 % 2)
                    mms = []
                    for i in range(4):
                        tb = g4 * 4 + i
                        for kc in range(8):
                            mms.append(dict(out=ps[vb][:, i * 128:(i + 1) * 128],
                                            lhsT=hT[:, kc, tb * 128:(tb + 1) * 128], rhs=wv8[:, kc, :],
                                            start=(kc == 0), stop=(kc == 7)))
                    P.mm(mms, R=[wvd] + [HD[kc][g4] for kc in range(8)], W=[PSD[vb]])
                    P.op(act, lambda e, g4=g4, vb=vb: e.activation(
                        out=vaug[:, g4 * 4:(g4 + 1) * 4, 0:128],
                        in_=ps[vb][:].rearrange("p (a b) -> p a b", a=4), func=AF.Copy), R=[PSD[vb]], W=[VD[g4]])
                wdone(wvi)
                if stage == "daq":
                    dump(qT, QD, S, cast=True)
                    dump(kTz[0], KD, S, cast=True)
                    dump(vaug.rearrange("p a b -> p (a b)"), VD, NB * 130, cast=True)
                    return True
                for qg in range(4 if stage != "da1" else 1):
                    qsl = slice(qg * TT, (qg + 1) * TT)
                    UB = [ps[4], ps[5]]
                    RB = [ps[6], psTf]
                    UD_ = [PSD[4], PSD[5]]
                    RD_ = [PSD[6], PSTD]
                    nj = 4 * qg + 4
                    deferred = []
                    for j in range(nj):
                        r = max(0, j - 4 * qg)
                        c0 = r * 128
                        par = j % 2
                        cur = []
                        for c in range(2):
                            sbank = c * 2 + par
                            prow = slice(c * 64, (c + 1) * 64)
                            mms = [dict(out=ps[sbank][:, c0:TT], lhsT=kTz[c][:, j * 128:(j + 1) * 128],
                                        rhs=qT[:, qg * TT + c0:(qg + 1) * TT], start=True, stop=(j < 4 * qg),
                                        sgc=True)]
                            if j >= 4 * qg:
                                mms.append(dict(out=ps[sbank][:, c0:c0 + 128], lhsT=ident_b, rhs=maskneg_b,
                                                start=False, stop=True, sgc=True))
                            P.mm(mms, R=[KD[j // 4], QD[qg], CONST], W=[PSD[sbank]])
                        for c in range(2):
                            sbank = c * 2 + par
                            ptD = D("dapt", c, par)
                            P.op(act, lambda e, c=c, par=par, sbank=sbank, c0=c0: e.activation(
                                out=PT[c][par][:, c0:TT], in_=ps[sbank][:, c0:TT], func=AF.Exp),
                                R=[PSD[sbank]], W=[ptD])

                            def acc(c=c, par=par, c0=c0, j=j, ptD=ptD):
                                P.mm([dict(out=UB[c][:, c0:TT], lhsT=vaug[:, j, 0:128], rhs=PT[c][par][:, c0:TT],
                                           start=(j == 0), stop=(j == nj - 1), sgc=True)],
                                     R=[ptD, VD[j // 4]], W=[UD_[c]])
                                P.mm([dict(out=RB[c][:, c0:TT], lhsT=ones_b, rhs=PT[c][par][:, c0:TT],
                                           start=(j == 0), stop=(j == nj - 1), sgc=True)],
                                     R=[ptD, CONST], W=[RD_[c]])

                            cur.append(acc)
                        if j == 1 and pending:
                            pending.pop(0)()
                        for f_ in deferred:
                            f_()
                        deferred = cur
                    for f_ in deferred:
                        f_()

                    aD, bD, o1D, o2D = D("darrA"), D("darrB"), D("dao1"), D("dao2")
                    P.op(act, lambda e: e.activation(out=rrA, in_=RB[0][:, :], func=AF.Ln), R=[RD_[0]], W=[aD])
                    P.op(act, lambda e: e.activation(out=rrB, in_=RB[1][:, :], func=AF.Ln), R=[RD_[1]], W=[bD])
                    P.op(act, lambda e: e.activation(out=rrA, in_=rrA, func=AF.Exp, scale=-1.0), R=[], W=[aD])
                    P.op(act, lambda e: e.activation(out=rrB, in_=rrB, func=AF.Exp, scale=-1.0), R=[], W=[bD])
                    P.op(dve, lambda e: e.tensor_tensor(out=o1, in0=UB[0][:, :], in1=rrA, op=ALU.mult),
                         R=[UD_[0], aD], W=[o1D])
                    P.op(dve, lambda e: e.scalar_tensor_tensor(out=o2, in0=UB[1][:, :], scalar=nlam[:, 0:1], in1=rrB,
                                                               op0=ALU.mult, op1=ALU.mult),
                         R=[UD_[1], bD, LD], W=[o2D])

                    def tail(qg=qg, h=h, qsl=qsl):
                        o1D, o2D, sD, lD = D("dao1"), D("dao2"), D("dasq"), D("daln")
                        P.op(pool, lambda e: e.tensor_tensor(out=o1, in0=o1, in1=o2, op=ALU.add), R=[o2D], W=[o1D])
                        P.op(pool, lambda e: e.tensor_tensor(out=sqb, in0=o1, in1=o1, op=ALU.mult), R=[o1D], W=[sD])
                        P.mm([dict(out=ps[6][:, :], lhsT=ones_b, rhs=sqb)], R=[sD, CONST], W=[PSD[6]])
                        P.op(act, lambda e: e.activation(out=lnv, in_=ps[6][:, :], func=AF.Ln, scale=1.0 / 128.0,
                                                         bias=EPS_AP), R=[PSD[6]] + CONSTS, W=[lD])
                        P.op(act, lambda e: e.activation(out=lnv, in_=lnv, func=AF.Exp, scale=-0.5, bias=lbias),
                             R=[D("dalb")], W=[lD])
                        P.op(dve, lambda e: e.scalar_tensor_tensor(out=yT[:, h, qsl], in0=o1,
                                                                   scalar=V(l, "danorm", 0), in1=lnv, op0=ALU.mult,
                                                                   op1=ALU.mult),
                             R=[o1D, lD] + CONSTS, W=[YD[h][qg]])

                    pending.append(tail)
                while pending:
                    pending.pop(0)()
            P.barrier()

        def ml_group(l):
            cv.reset()
            wg, wgd, wgi = wnext("gates")
            wg8 = wg.rearrange("p (k c) -> p k c", k=8)
            mms = []
            for c in range(NB):
                for kc in range(8):
                    mms.append(dict(out=ps[0][:, c * 8:(c + 1) * 8], lhsT=hT[:, kc, c * 128:(c + 1) * 128],
                                    rhs=wg8[:, kc, 0:8], start=(kc == 0), stop=(kc == 7)))
            P.mm(mms, R=[wgd] + [HD[kc][t] for kc in range(8) for t in range(NTT)], W=[PSD[0]])
            wdone(wgi)
            gview = ps[0][:, 0:128].rearrange("p (c g) -> p g c", g=8)
            li = cv.take([128, 4, NB])
            fp = cv.take([128, 4, NB])
            aa = cv.take([128, 4, NB])
            ebL = cv.take([128, 4, NB])
            d1 = cv.take([128, 4, NB])
            d0 = cv.take([128, 4, NB])
            Em = cv.take([128, 4, NB])
            rZ = cv.take([128, 4, NB])
            dcy = cv.take([128, 4, NB + 1])
            esk = cv.take([128, 4, NB])
            bnd = cv.take([128, 4, NB])
            Emp = cv.take([128, 4, NB])
            GD = D("mlg")

            def f2(x):
                return x.rearrange("p a b -> p (a b)")

            for h in range(4):
                P.op(dve, lambda e, h=h: e.tensor_scalar(out=li[:, h, :], in0=gview[:, h, :],
                                                         scalar1=V(l, "ibias", h), scalar2=None, op0=ALU.add),
                     R=[PSD[0]] + CONSTS, W=[GD])
                P.op(dve, lambda e, h=h: e.tensor_scalar(out=fp[:, h, :], in0=gview[:, 4 + h, :],
                                                         scalar1=V(l, "fbias", h), scalar2=None, op0=ALU.add),
                     R=[PSD[0]] + CONSTS, W=[GD])
            P.op(act, lambda e: e.activation(out=f2(fp), in_=f2(fp), func=AF.Exp, scale=-1.0), R=[], W=[GD])
            P.op(act, lambda e: e.activation(out=f2(fp), in_=f2(fp), func=AF.Ln, bias=ONE_AP), R=CONSTS, W=[GD])
            P.mm([dict(out=ps[1][:, 0:64], lhsT=tri_f, rhs=f2(fp))], R=[GD, CONST], W=[PSD[1]])
            P.mm([dict(out=ps[2][:, 0:64], lhsT=ones_f, rhs=f2(fp))], R=[GD, CONST], W=[PSD[2]])
            P.op(dve, lambda e: e.tensor_tensor(out=f2(aa), in0=ps[1][:, 0:64], in1=f2(li), op=ALU.add),
                 R=[PSD[1]], W=[GD])
            P.op(act, lambda e: e.activation(out=f2(aa), in_=f2(aa), func=AF.Exp), R=[], W=[GD])
            P.mm([dict(out=ps[3][:, 0:64], lhsT=ones_f, rhs=f2(aa))], R=[GD, CONST], W=[PSD[3]])
            P.op(act, lambda e: e.activation(out=f2(ebL), in_=ps[2][:, 0:64], func=AF.Exp, scale=-1.0), R=[PSD[2]],
                 W=[GD])
            P.op(dve, lambda e: e.tensor_tensor(out=f2(d1), in0=ps[3][:, 0:64], in1=f2(ebL), op=ALU.mult),
                 R=[PSD[3]], W=[GD])
            P.op(dve, lambda e: e.tensor_tensor(out=d1[:, :, 0:1], in0=d1[:, :, 0:1], in1=ebL[:, :, 0:1], op=ALU.add),
                 R=[], W=[GD])
            P.op(dve, lambda e: e.tensor_copy(out=f2(d0), in_=f2(ebL)), R=[], W=[GD])
            P.op(dve, lambda e: e.memset(d0[:, :, 0:1], 0.0), R=[], W=[GD])
            P.op(dve, lambda e: e.tensor_tensor_scan(out=f2(Em), data0=f2(d0), data1=f2(d1), initial=0.0,
                                                     op0=ALU.mult, op1=ALU.add), R=[], W=[GD])
            P.op(dve, lambda e: e.reciprocal(out=f2(rZ), in_=f2(Em)), R=[], W=[GD])
            P.op(dve, lambda e: e.tensor_tensor(out=f2(rZ), in0=f2(rZ), in1=f2(ebL), op=ALU.mult), R=[], W=[GD])
            P.op(dve, lambda e: e.memset(Emp[:, :, 0:1], 1.0), R=[], W=[GD])
            P.op(dve, lambda e: e.tensor_copy(out=Emp[:, :, 1:NB], in_=Em[:, :, 0:NB - 1]), R=[], W=[GD])
            P.op(dve, lambda e: e.memset(dcy[:, :, NB:NB + 1], 1.0), R=[], W=[GD])
            P.op(dve, lambda e: e.tensor_tensor(out=dcy[:, :, 0:NB], in0=Emp[:, :, :], in1=rZ[:, :, :], op=ALU.mult),
                 R=[], W=[GD])
            P.op(dve, lambda e: e.scalar_tensor_tensor(out=f2(esk), in0=f2(aa), scalar=128.0 ** -0.5, in1=f2(rZ),
                                                       op0=ALU.mult, op1=ALU.mult), R=[], W=[GD])
            P.op(act, lambda e: e.activation(out=f2(bnd), in_=ps[1][:, 0:64], func=AF.Exp), R=[PSD[1]], W=[GD])
            P.op(dve, lambda e: e.tensor_tensor(out=f2(bnd), in0=f2(bnd), in1=f2(rZ), op=ALU.mult), R=[], W=[GD])

            qT = cv.take([128, S], BF16)
            kT = cv.take([128, S], BF16)
            vaug = cv.take([128, NB, 130], BF16)
            sgoT = cv.take([128, S], BF16)
            u = [cv.take([128, TT]) for _ in range(3)]
            halo = [cv.take([128, 4, 3]) for _ in range(2)]
            MLH = [[D("mlhalo", w_, i_) for i_ in range(4)] for w_ in range(2)]
            WT = [cv.take([128, 128], BF16) for _ in range(2)]
            ktok = [cv.take([128, 128], BF16) for _ in range(2)]
            Cst = cv.take([128, 130])
            Cs = [cv.take([128, 130], BF16) for _ in range(2)]
            E4 = [cv.take([128, 4, 129]) for _ in range(2)]
            hn = cv.take([128, 4, 128])
            sq4 = cv.take([128, 4, 128])
            yb4 = [cv.take([128, 4, 128], BF16) for _ in range(2)]
            a4 = cv.take([128, 4])
            ss4 = cv.take([128, 4])
            QD = [D("mlq", t) for t in range(NTT)]
            KD = [D("mlk", t) for t in range(NTT)]
            VD = [D("mlv", g) for g in range(4)]
            OD = [D("mlo", g) for g in range(4)]
            P.op(dve, lambda e: e.memset(vaug[:, :, 128:129], 1.0), W=VD)
            ctr = {"u": 0, "c": 0}
            epi_parts = []
            for h in range(4):
                wq, wqd, _ = wnext("in")
                wk, wkd, _ = wnext("in")
                wv, wvd, _ = wnext("in")
                wo, wod, woi = wnext("in")
                w8 = [x.rearrange("p (k c) -> p k c", k=8) for x in (wq, wk, wv, wo)]
                punits = [(which, t) for which in range(3) for t in range(NTT)]
                pinfo = [(wqd, qT, QD), (wkd, kT, KD), (wod, sgoT, OD)]
                widx = [0, 1, 3]

                def pA(ui):
                    which, t = punits[ui]
                    wd_, dst, dstD = pinfo[which]
                    ch = which * 4 + h
                    tsl = slice(t * TT, (t + 1) * TT)
                    n_ = ctr["u"] + ui
                    k, pb = n_ % 3, n_ % 2
                    P.mm([dict(out=ps[pb][:], lhsT=w8[widx[which]][:, kc, :], rhs=hT[:, kc, tsl],
                               start=(kc == 0), stop=(kc == 7)) for kc in range(8)],
                         R=[wd_] + [HD[kc][t] for kc in range(8)], W=[PSD[pb]])
                    if which < 2:
                        conv_A(ps[pb][:], PSD[pb], u[k], D("mlu", k), halo[which], MLH[which], t,
                               lambda j, ch=ch: V(l, "mlcw", ch * 4 + j), V(l, "mlcb", ch), 4)

                def pB(ui):
                    which, t = punits[ui]
                    wd_, dst, dstD = pinfo[which]
                    ch = which * 4 + h
                    tsl = slice(t * TT, (t + 1) * TT)
                    n_ = ctr["u"] + ui
                    k, pb = n_ % 3, n_ % 2
                    if which == 2:
                        P.op(act, lambda e, dst=dst, pb=pb: e.activation(out=dst[:, tsl], in_=ps[pb][:],
                                                                         func=AF.Sigmoid),
                             R=[PSD[pb]], W=[dstD[t]])
                        return
                    conv_B(ps[pb][:], PSD[pb], u[k], D("mlu", k), halo[which], MLH[which], t,
                           lambda j, ch=ch: V(l, "mlcw", ch * 4 + j), 4)
                    P.op(act, lambda e, k=k, dst=dst: e.activation(out=dst[:, tsl], in_=u[k], func=AF.Silu),
                         R=[D("mlu", k)], W=[dstD[t]])

                pA(0)
                for ui in range(len(punits)):
                    if ui + 1 < len(punits):
                        pA(ui + 1)
                    pB(ui)
                    if epi_parts:
                        epi_parts.pop(0)()
                ctr["u"] += len(punits)
                for g4 in range(4):
                    vb = 2 + (g4 % 2)
                    mms = []
                    for i_ in range(4):
                        tb = g4 * 4 + i_
                        for kc in range(8):
                            mms.append(dict(out=ps[vb][:, i_ * 128:(i_ + 1) * 128],
                                            lhsT=hT[:, kc, tb * 128:(tb + 1) * 128], rhs=w8[2][:, kc, :],
                                            start=(kc == 0), stop=(kc == 7)))
                    P.mm(mms, R=[wvd] + [HD[kc][g4] for kc in range(8)], W=[PSD[vb]])
                    P.op(act, lambda e, g4=g4, vb=vb: e.activation(
                        out=vaug[:, g4 * 4:(g4 + 1) * 4, 0:128],
                        in_=ps[vb][:].rearrange("p (a b) -> p a b", a=4), func=AF.Copy), R=[PSD[vb]], W=[VD[g4]])
                wdone(woi)
                CD = D("mlC")
                P.op(dve, lambda e: e.memset(Cst, 0.0), R=[], W=[CD])

                def stageA(c):
                    csl = slice(c * 128, (c + 1) * 128)
                    k = c % 2
                    eskc = f2(esk)[:, h * NB + c:h * NB + c + 1]
                    sb_ = 4 + k
                    P.mm([dict(out=ps[sb_][:, 0:128], lhsT=kT[:, csl], rhs=qT[:, csl])], R=[KD[c // 4], QD[c // 4]],
                         W=[PSD[sb_]])
                    P.op(dve, lambda e: e.scalar_tensor_tensor(out=WT[k], in0=ps[sb_][:, 0:128], scalar=eskc,
                                                               in1=tri_f, op0=ALU.mult, op1=ALU.mult),
                         R=[PSD[sb_], GD, CONST], W=[D("mlWT", k)])
                    P.mm([dict(tr=True, out=psT[:, k * 128:(k + 1) * 128], in_=kT[:, csl], identity=ident_b)],
                         R=[KD[c // 4], CONST], W=[PSTD])
                    P.op(act, lambda e: e.activation(out=ktok[k], in_=psT[:, k * 128:(k + 1) * 128], func=AF.Copy,
                                                     scale=eskc), R=[PSTD, GD], W=[D("mlktok", k)])
                    if c < NB - 1:
                        P.mm([dict(out=ps[2 + k][:, 0:129], lhsT=ktok[k], rhs=vaug[:, c, 0:129])],
                             R=[D("mlktok", k), VD[c // 4]], W=[PSD[2 + k]])

                stageA(0)
                for c in range(NB):
                    csl = slice(c * 128, (c + 1) * 128)
                    k = c % 2
                    eb = (c // 4) % 2
                    if c + 1 < NB:
                        stageA(c + 1)
                    mms = []
                    if c > 0:
                        mms.append(dict(out=ps[6][:, 0:129], lhsT=qT[:, csl], rhs=Cs[(c - 1) % 2][:, 0:129],
                                        start=True, stop=False))
                    mms.append(dict(out=ps[6][:, 0:129], lhsT=WT[k], rhs=vaug[:, c, 0:129], start=(c == 0),
                                    stop=True))
                    P.mm(mms, R=[QD[c // 4], D("mlWT", k), VD[c // 4]] + ([D("mlCs", (c - 1) % 2)] if c > 0 else []),
                         W=[PSD[6]])
                    P.op(act, lambda e, c=c, eb=eb: e.activation(out=E4[eb][:, c % 4, :], in_=ps[6][:, 0:129],
                                                                 func=AF.Copy), R=[PSD[6]], W=[D("mlE4", eb)])
                    if c < NB - 1:
                        dc_ = dcy[:, h, c:c + 1]
                        dn_ = dcy[:, h, c + 1:c + 2]
                        P.op(dve, lambda e, dc_=dc_, k=k: e.scalar_tensor_tensor(
                            out=Cst[:, 0:129], in0=Cst[:, 0:129], scalar=dc_, in1=ps[2 + k][:, 0:129],
                            op0=ALU.mult, op1=ALU.add), R=[PSD[2 + k], GD], W=[CD])
                        P.op(dve, lambda e, dn_=dn_, c=c: e.tensor_scalar(out=Cs[c % 2][:, 0:129], in0=Cst[:, 0:129],
                                                                          scalar1=dn_, scalar2=None, op0=ALU.mult),
                             R=[GD], W=[D("mlCs", c % 2)])
                    if epi_parts:
                        epi_parts.pop(0)()
                    if c % 4 == 3:
                        c0 = c - 3
                        E_ = E4[eb]
                        eD, aD, hD, qD, sD, yD = (D("mlE4", eb), D("mla4"), D("mlhn"), D("mlsq4"), D("mlss4"),
                                                  D("mlyb4", eb))

                        def p1(E_=E_, eD=eD, aD=aD, hD=hD, qD=qD, c0=c0, h=h):
                            dn4 = E_[:, :, 128]
                            P.op(dve, lambda e: e.scalar_tensor_tensor(out=a4, in0=dn4, scalar=-1.0, in1=dn4,
                                                                       op0=ALU.mult, op1=ALU.max), R=[eD], W=[aD])
                            P.op(dve, lambda e: e.tensor_tensor(out=a4, in0=a4, in1=bnd[:, h, c0:c0 + 4],
                                                                op=ALU.max), R=[GD], W=[aD])
                            P.op(dve, lambda e: e.reciprocal(out=a4, in_=a4), R=[], W=[aD])
                            P.op(dve, lambda e: e.tensor_tensor(
                                out=hn, in0=E_[:, :, 0:128], in1=a4.unsqueeze(2).to_broadcast([128, 4, 128]),
                                op=ALU.mult), R=[eD, aD], W=[hD])
                            P.op(pool, lambda e: e.tensor_tensor(out=sq4, in0=hn, in1=hn, op=ALU.mult), R=[hD],
                                 W=[qD])

                        def p2(qD=qD, sD=sD):
                            P.op(dve, lambda e: e.tensor_reduce(out=ss4, in_=sq4, axis=mybir.AxisListType.X,
                                                                op=ALU.add), R=[qD], W=[sD])
                            P.op(act, lambda e: e.activation(out=ss4, in_=ss4, func=AF.Ln, scale=1.0 / 128.0,
                                                             bias=EPS_AP), R=CONSTS, W=[sD])
                            P.op(act, lambda e: e.activation(out=ss4, in_=ss4, func=AF.Exp, scale=-0.5), R=[],
                                 W=[sD])

                        def p3(sD=sD, qD=qD, hD=hD, yD=yD, eb=eb):
                            P.op(dve, lambda e: e.tensor_tensor(
                                out=yb4[eb], in0=hn, in1=ss4.unsqueeze(2).to_broadcast([128, 4, 128]), op=ALU.mult),
                                R=[sD, qD, hD], W=[yD])

                        def p4(yD=yD, c0=c0, eb=eb, c=c, h=h):
                            P.mm([dict(tr=True, out=psT[:, 512 + ii * 128:512 + (ii + 1) * 128],
                                       in_=yb4[eb][:, ii, :], identity=ident_b) for ii in range(4)],
                                 R=[yD, CONST], W=[PSTD])
                            P.op(dve, lambda e: e.scalar_tensor_tensor(
                                out=yT[:, h, c0 * 128:(c0 + 4) * 128], in0=psT[:, 512:1024],
                                scalar=V(l, "mlnorm", h), in1=sgoT[:, c0 * 128:(c0 + 4) * 128], op0=ALU.mult,
                                op1=ALU.mult), R=[PSTD, OD[c // 4]] + CONSTS, W=[YD[h][c // 4]])

                        while epi_parts:
                            epi_parts.pop(0)()
                        epi_parts.extend([p1, p2, p3, p4])
                if h == 3:
                    while epi_parts:
                        epi_parts.pop(0)()
            P.barrier()

        def ffn(l):
            cv.reset()
            aT = cv.take([128, 6, S], BF16)
            NBUF = 3
            ug = [cv.take([128, TT]) for _ in range(NBUF)]
            uv = [cv.take([128, TT]) for _ in range(NBUF)]
            sg = [cv.take([128, TT]) for _ in range(NBUF)]
            hg = cv.take([128, 4, 2])
            hv = cv.take([128, 4, 2])
            HG = [D("fhg", i_) for i_ in range(4)]
            HV = [D("fhv", i_) for i_ in range(4)]
            it = 0
            for G in FFN_GROUPS:
                units = []
                for gi, j in enumerate(G):
                    wgk, wgd, _ = wnext("up")
                    wvk, wvd, wvi = wnext("up")
                    wg8 = wgk.rearrange("p (k c) -> p k c", k=8)
                    wv8 = wvk.rearrange("p (k c) -> p k c", k=8)
                    for t in range(NTT):
                        units.append((gi, j, t, wg8, wgd, wv8, wvd, wvi))

                def stA(u_, it_):
                    gi, j, t, wg8, wgd, wv8, wvd, wvi = u_
                    tsl = slice(t * TT, (t + 1) * TT)
                    k, b = it_ % NBUF, it_ % 2
                    gbk, vbk = b, 2 + b
                    P.mm([dict(out=ps[gbk][:], lhsT=wg8[:, kc, :], rhs=hT[:, kc, tsl], start=(kc == 0),
                               stop=(kc == 7)) for kc in range(8)],
                         R=[wgd] + [HD[kc][t] for kc in range(8)], W=[PSD[gbk]])
                    P.mm([dict(out=ps[vbk][:], lhsT=wv8[:, kc, :], rhs=hT[:, kc, tsl], start=(kc == 0),
                               stop=(kc == 7)) for kc in range(8)],
                         R=[wvd] + [HD[kc][t] for kc in range(8)], W=[PSD[vbk]])
                    if t == NTT - 1:
                        wdone(wvi)
                    conv_A(ps[gbk][:], PSD[gbk], ug[k], D("fug", k), hg, HG, t,
                           lambda jj, j=j: V(l, "fcw", (2 * j) * 3 + jj), V(l, "fcb", 2 * j), 3)
                    conv_A(ps[vbk][:], PSD[vbk], uv[k], D("fuv", k), hv, HV, t,
                           lambda jj, j=j: V(l, "fcw", (2 * j + 1) * 3 + jj), V(l, "fcb", 2 * j + 1), 3)

                def stBC(u_, it_):
                    gi, j, t, wg8, wgd, wv8, wvd, wvi = u_
                    tsl = slice(t * TT, (t + 1) * TT)
                    k, b = it_ % NBUF, it_ % 2
                    gbk, vbk = b, 2 + b
                    conv_B(ps[gbk][:], PSD[gbk], ug[k], D("fug", k), hg, HG, t,
                           lambda jj, j=j: V(l, "fcw", (2 * j) * 3 + jj), 3)
                    conv_B(ps[vbk][:], PSD[vbk], uv[k], D("fuv", k), hv, HV, t,
                           lambda jj, j=j: V(l, "fcw", (2 * j + 1) * 3 + jj), 3)
                    P.op(act, lambda e, k=k: e.activation(out=sg[k], in_=ug[k], func=AF.Silu), R=[D("fug", k)],
                         W=[D("fsg", k)])
                    P.op(pool, lambda e, k=k, gi=gi: e.tensor_tensor(out=aT[:, gi, tsl], in0=sg[k], in1=uv[k],
                                                                      op=ALU.mult),
                         R=[D("fsg", k), D("fuv", k)], W=[D("faT", gi, t)])

                stA(units[0], it)
                for ui in range(len(units)):
                    if ui + 1 < len(units):
                        stA(units[ui + 1], it + ui + 1)
                    stBC(units[ui], it + ui)
                it += len(units)
                wds = [wnext("down") for _ in G]
                for dc in range(8):
                    for t in range(NTT):
                        tsl = slice(t * TT, (t + 1) * TT)
                        bk = 4 + (dc * NTT + t) % 2
                        P.mm([dict(out=ps[bk][:], lhsT=wds[gi][0][:, dc * 128:(dc + 1) * 128], rhs=aT[:, gi, tsl],
                                   start=(gi == 0), stop=(gi == len(G) - 1)) for gi in range(len(G))],
                             R=[w_[1] for w_ in wds] + [D("faT", gi, t) for gi in range(len(G))], W=[PSD[bk]])
                        P.op(dve, lambda e, dc=dc, bk=bk: e.tensor_tensor(out=xT[:, dc, tsl], in0=ps[bk][:],
                                                                           in1=xT[:, dc, tsl], op=ALU.add),
                             R=[PSD[bk]], W=[XD[dc][t]])
                wdone(wds[-1][2])
            P.barrier()

        def final_norm():
            cv.reset()
            sq = [cv.take([128, TT], BF16) for _ in range(4)]
            lnv = [cv.take([128, TT]) for _ in range(2)]
            rstd = [cv.take([128, TT]) for _ in range(2)]
            ost = [cv.take([128, 8, TT]) for _ in range(2)]
            OUTD = D("outd")
            for t in range(NTT):
                tsl = slice(t * TT, (t + 1) * TT)
                for c in range(8):
                    b = sq[c % 4]
                    bd = D("nsq", c % 4)
                    P.op(act, lambda e, b=b, c=c: e.activation(out=b, in_=xT[:, c, tsl], func=AF.Square),
                         R=[XD[c][t]], W=[bd])
                    P.mm([dict(out=ps[0][:], lhsT=ones_b, rhs=b, start=(c == 0), stop=(c == 7))],
                         R=[bd, CONST], W=[PSD[0]])
                ld = D("nln", t % 2)
                rd = D("nrs", t % 2)
                P.op(act, lambda e: e.activation(out=lnv[t % 2], in_=ps[0][:], func=AF.Ln, scale=1.0 / D_F,
                                                 bias=EPS_AP), R=[PSD[0]] + CONSTS, W=[ld])
                P.op(act, lambda e: e.activation(out=rstd[t % 2], in_=lnv[t % 2], func=AF.Exp, scale=-0.5),
                     R=[ld], W=[rd])
                oD = D("ost", t % 2)
                for c in range(8):
                    P.op(dve, lambda e, c=c: e.scalar_tensor_tensor(out=ost[t % 2][:, c, :], in0=xT[:, c, tsl],
                                                                    scalar=gv[:, c:c + 1], in1=rstd[t % 2],
                                                                    op0=ALU.mult, op1=ALU.mult),
                         R=[XD[c][t], rd, CONST], W=[oD])
                P.dma(sp, out_d[:, :, tsl], ost[t % 2], D("outd", t % 2), R=[oD])

        for l in range(nl):
            last = (l == nl - 1)
            rmsnorm_to_hT(l, "g1")
            if stage == "big" and last:
                for rep in range(200):
                    P.mm([dict(out=ps[5][:, 0:128], lhsT=ident_b, rhs=ones_b) for _ in range(125)],
                         R=[CONST], W=[PSD[5]])
                stage = "n1"
            if stage == "n1" and last:
                dump(hT[:, :, :].rearrange("p c s -> p (c s)"), [HD[c][t] for c in range(8) for t in range(NTT)],
                     8 * S, cast=True)
                finish()
                return nc
            rg_group(l)
            if stage == "rg" and last:
                dump(yT[:, :, :].rearrange("p c s -> p (c s)"), [YD[c][t] for c in range(4) for t in range(NTT)],
                     4 * S, cast=True)
                finish()
                return nc
            wout_group(l, 0)
            if stage == "wo0" and last:
                dump(xT[:, :, :].rearrange("p c s -> p (c s)"), [XD[c][t] for c in range(8) for t in range(NTT)],
                     8 * S)
                finish()
                return nc
            if da_group(l):
                finish()
                return nc
            if stage in ("da", "da1") and last:
                dump(yT[:, :, :].rearrange("p c s -> p (c s)"), [YD[c][t] for c in range(4) for t in range(NTT)],
                     4 * S, cast=True)
                finish()
                return nc
            wout_group(l, 1)
            ml_group(l)
            if stage == "ml" and last:
                dump(yT[:, :, :].rearrange("p c s -> p (c s)"), [YD[c][t] for c in range(4) for t in range(NTT)],
                     4 * S, cast=True)
                finish()
                return nc
            wout_group(l, 2)
            if stage == "x1" and last:
                dump(xT[:, :, :].rearrange("p c s -> p (c s)"), [XD[c][t] for c in range(8) for t in range(NTT)],
                     8 * S)
                finish()
                return nc
            rmsnorm_to_hT(l, "g2")
            ffn(l)
            if stage == "x2" and last:
                dump(xT[:, :, :].rearrange("p c s -> p (c s)"), [XD[c][t] for c in range(8) for t in range(NTT)],
                     8 * S)
                finish()
                return nc
        final_norm()
        finish()
    return nc


_CACHE = {}


def make_in_maps(inputs, nl=NL, ncores=8):
    inp = {k: np.asarray(v) for k, v in inputs.items()}
    cm = const_mats()
    gvv = pack_gvec(inp)
    vecs = np.stack([pack_vec(inp, l) for l in range(nl)], axis=0)
    wblk = np.concatenate([pack_blocks(inp, l) for l in range(nl)], axis=0)
    maps = []
    for b in range(ncores):
        xb = np.ascontiguousarray(inp["x"][b].T.reshape(8, 128, S).transpose(1, 0, 2))
        posb = np.ascontiguousarray(np.broadcast_to(inp["positions"][b][None, :].astype(np.int32), (128, S)))
        maps.append({"xT": xb, "pos": posb, "cmat": cm, "gvec": gvv, "vec": vecs, "wblk": wblk})
    return maps


def kernel(**inputs):
    if "nc" not in _CACHE:
        _CACHE["nc"] = build()
    nc = _CACHE["nc"]
    maps = make_in_maps(inputs)
    res = run_bass_kernel_spmd(nc, maps, core_ids=list(range(8)))
    outs = []
    for b in range(8):
        o = res.results[b]["outT"]
        outs.append(o.transpose(2, 1, 0).reshape(S, 1024))
    return np.stack(outs, axis=0).astype(np.float32)
```

```python
import math
import numpy as np
from contextlib import ExitStack
import concourse.bass as bass
import concourse.mybir as mybir
from concourse.bass_utils import run_bass_kernel_spmd

F32 = mybir.dt.float32
BF16 = mybir.dt.bfloat16
I32 = mybir.dt.int32
AF = mybir.ActivationFunctionType
ALU = mybir.AluOpType

S = 2048
D = 1024
NL = 2
NTT = 4
TT = 512
NB = 16
EPS = 1e-6
DFF = 2816
NJ = 22
FFN_GROUPS = [list(range(0, 6)), list(range(6, 12)), list(range(12, 17)), list(range(17, 22))]
NSLOT = 12
RG_C = 8.0

_VEC = [("g1", 8), ("g2", 8), ("rgcw", 16), ("rgcb", 4), ("rgba", 4), ("rgbx", 4), ("rglam", 4), ("rgnorm", 4),
        ("mlcw", 32), ("mlcb", 8), ("fcw", 132), ("fcb", 44), ("danorm", 1), ("mlnorm", 4),
        ("ibias", 4), ("fbias", 4), ("dalam", 256)]
VOFF = {}
_o = 0
for _n, _w in _VEC:
    VOFF[_n] = _o
    _o += _w
NV = _o
GOFF = {"fn": 0, "invf": 8, "sgn": 9}
NG = 10

OFF_IN = dict(rg_x=0, rg_g=512, da_q=1024, da_k=1536, da_v=2048, ml_q=2560, ml_k=3072, ml_v=3584, ml_o=4096,
              ml_i=4608, ml_f=4612)


def block_plan():
    plan = []
    for n in range(4):
        plan.append(("in", OFF_IN["rg_x"] + n * 128))
    for n in range(4):
        plan.append(("in", OFF_IN["rg_g"] + n * 128))
    plan.append(("gatew",))
    for pr in range(4):
        plan.append(("out", 0, pr))
    for h in range(4):
        plan.append(("in", OFF_IN["da_q"] + h * 128))
        plan.append(("in", OFF_IN["da_k"] + h * 128))
        plan.append(("in", OFF_IN["da_v"] + h * 128))
    for pr in range(4):
        plan.append(("out", 1, pr))
    plan.append(("gates",))
    for h in range(4):
        plan.append(("in", OFF_IN["ml_q"] + h * 128))
        plan.append(("in", OFF_IN["ml_k"] + h * 128))
        plan.append(("in", OFF_IN["ml_v"] + h * 128))
        plan.append(("in", OFF_IN["ml_o"] + h * 128))
    for pr in range(4):
        plan.append(("out", 2, pr))
    for G in FFN_GROUPS:
        for j in G:
            plan.append(("up", j))
            plan.append(("up", DFF // 128 * 0 + j + 1000))
        for j in G:
            plan.append(("down", j))
    return plan


PLAN = block_plan()
NBLK = len(PLAN)


def pack_blocks(inp, l):
    w_in = inp["w_in"][l]
    w_out = inp["w_out"][l]
    w_up = inp["w_up"][l]
    w_down = inp["w_down"][l]
    out = np.zeros((NBLK, 128, 1024), np.float32)

    def fm(w, c0, ncols=128):
        blk = np.zeros((8, 128, 128), np.float32)
        blk[:, :, :ncols] = w[:, c0:c0 + ncols].reshape(8, 128, ncols)
        return blk.transpose(1, 0, 2).reshape(128, 1024)

    for i, b in enumerate(PLAN):
        k = b[0]
        if k == "in":
            out[i] = fm(w_in, b[1])
        elif k == "gates":
            out[i] = fm(w_in, OFF_IN["ml_i"], 8)
        elif k == "gatew":
            wa = inp["rg_wa"][l].transpose(1, 0, 2)
            wx = inp["rg_wx"][l].transpose(1, 0, 2)
            out[i] = np.stack([wa, wx], axis=1).reshape(128, 1024)
        elif k == "out":
            g, pr = b[1], b[2]
            sub = w_out[g * 512:(g + 1) * 512, pr * 256:(pr + 1) * 256]
            out[i] = sub.reshape(4, 128, 2, 128).transpose(1, 2, 0, 3).reshape(128, 1024)
        elif k == "up":
            j = b[1]
            c0 = j * 128 if j < 1000 else DFF + (j - 1000) * 128
            out[i] = fm(w_up, c0)
        elif k == "down":
            j = b[1]
            out[i] = w_down[j * 128:(j + 1) * 128, :]
    return out


def pack_vec(inp, l):
    v = np.zeros((128, NV), np.float32)

    def put(name, arr):
        arr = np.asarray(arr, np.float32)
        v[:, VOFF[name]:VOFF[name] + arr.shape[1]] = arr

    put("g1", inp["attn_norm"][l].reshape(8, 128).T)
    put("g2", inp["mlp_norm"][l].reshape(8, 128).T)
    put("rgcw", inp["rg_conv_w"][l].reshape(4, 4, 128).transpose(2, 1, 0).reshape(128, 16))
    put("rgcb", inp["rg_conv_b"][l].reshape(4, 128).T)
    put("rgba", inp["rg_ba"][l].reshape(4, 128).T)
    put("rgbx", inp["rg_bx"][l].reshape(4, 128).T)
    put("rglam", inp["rg_lambda"][l].reshape(4, 128).T)
    put("rgnorm", inp["rg_norm"][l].reshape(4, 128).T)
    put("mlcw", inp["ml_conv_w"][l].reshape(4, 8, 128).transpose(2, 1, 0).reshape(128, 32))
    put("mlcb", inp["ml_conv_b"][l].reshape(8, 128).T)
    fw = inp["ffn_conv_w"][l]
    fb = inp["ffn_conv_b"][l]
    fcw = np.zeros((128, 44, 3), np.float32)
    fcb = np.zeros((128, 44), np.float32)
    for j in range(NJ):
        fcw[:, 2 * j, :] = fw[:, j * 128:(j + 1) * 128].T
        fcw[:, 2 * j + 1, :] = fw[:, DFF + j * 128:DFF + (j + 1) * 128].T
        fcb[:, 2 * j] = fb[j * 128:(j + 1) * 128]
        fcb[:, 2 * j + 1] = fb[DFF + j * 128:DFF + (j + 1) * 128]
    put("fcw", fcw.reshape(128, 132))
    put("fcb", fcb)
    put("danorm", inp["da_norm"][l].reshape(128, 1))
    put("mlnorm", inp["ml_norm"][l].reshape(4, 128).T)
    put("ibias", np.broadcast_to(inp["ml_i_bias"][l][None, :], (128, 4)))
    put("fbias", np.broadcast_to(inp["ml_f_bias"][l][None, :], (128, 4)))
    put("dalam", np.broadcast_to(inp["da_lambda"][l].reshape(1, 256), (128, 256)))
    return v


def const_mats():
    ident = np.eye(128, dtype=np.float32)
    perm = np.zeros((128, 128), np.float32)
    for base in (0, 64):
        for r in range(8):
            perm[base + r + 8, base + r] = 1.0
            perm[base + r, base + r + 8] = 1.0
    kk = np.arange(128)[:, None]
    qq = np.arange(128)[None, :]
    maskneg = np.where(kk > qq, -1e30, 0.0).astype(np.float32)
    tri = (qq >= kk).astype(np.float32)
    ones = np.ones((128, 128), np.float32)
    return np.stack([ident, perm, maskneg, tri, ones], axis=1)


def pack_gvec(inp):
    g = np.zeros((128, NG), np.float32)
    g[:, 0:8] = inp["final_norm"].reshape(8, 128).T
    inv = (500000.0 ** (-np.arange(0, 16, 2, dtype=np.float32) / 16.0)).astype(np.float32)
    for base in (0, 64):
        for r in range(8):
            g[base + r, 8] = inv[r]
            g[base + r + 8, 8] = inv[r]
            g[base + r, 9] = -1.0
            g[base + r + 8, 9] = 1.0
    return g


class Dep:
    __slots__ = ("w", "r", "sem", "nd", "x", "e")

    def __init__(self):
        self.x = False
        self.e = []
        self.w = None
        self.r = {}
        self.sem = None
        self.nd = 0


class Queue:
    def __init__(self, name, eng):
        self.name = name
        self.eng = eng
        self.sem = None
        self.cnt = 0
        self.seen = {}
        self.nsem = 0


class Prog:
    MAXC = 20000

    def __init__(self, nc, es):
        self.nc = nc
        self.es = es
        self.deps = {}
        self.pe = Queue("pe", nc.tensor)
        self.act = Queue("act", nc.scalar)
        self.dve = Queue("dve", nc.vector)
        self.pool = Queue("pool", nc.gpsimd)
        self.sp = Queue("sp", nc.sync)
        self.nsems = 0
        self.semkeep = []

    def newsem(self, name):
        self.nsems += 1
        h = self.es.enter_context(self.nc.semaphore(name))
        self.semkeep.append(h)
        return h

    def D(self, *key):
        d = self.deps.get(key)
        if d is None:
            d = Dep()
            self.deps[key] = d
        return d

    def _wait(self, q, toks):
        need = {}
        for t in toks:
            if t is None:
                continue
            sem, val = t
            k = id(sem)
            if q.seen.get(k, 0) >= val:
                continue
            if k not in need or need[k][1] < val:
                need[k] = (sem, val)
        for k, (sem, val) in need.items():
            q.eng.wait_ge(sem, val)
            q.seen[k] = val

    def _collect(self, R, W):
        toks = []
        for d in R:
            toks.append(d.w)
            toks.extend(d.e)
            if d.x:
                toks.extend(d.r.values())
        for d in W:
            toks.append(d.w)
            toks.extend(d.r.values())
        return toks

    def _mark(self, tok, R, W):
        k = id(tok[0])
        for d in R:
            d.r[k] = tok
        for d in W:
            d.w = tok
            d.r = {}

    def _signal(self, q, ins):
        if q.sem is None or q.cnt >= self.MAXC:
            q.nsem += 1
            q.sem = self.newsem(f"{q.name}{q.nsem}")
            q.cnt = 0
        q.cnt += 1
        ins.then_inc(q.sem, 1)
        return (q.sem, q.cnt)

    def op(self, q, fn, R=(), W=()):
        self._wait(q, self._collect(R, W))
        ins = fn(q.eng)
        tok = self._signal(q, ins)
        self._mark(tok, R, W)
        return tok

    def mm(self, mms, R=(), W=()):
        q = self.pe
        self._wait(q, self._collect(R, W))
        ins = None
        for kw in mms:
            if kw.pop("tr", False):
                ins = q.eng.transpose(kw["out"], kw["in_"], kw["identity"])
            else:
                ins = q.eng.matmul(kw["out"], lhsT=kw["lhsT"], rhs=kw["rhs"], start=kw.get("start", True),
                                   stop=kw.get("stop", True), skip_group_check=kw.get("sgc", False))
        tok = self._signal(q, ins)
        self._mark(tok, R, W)
        return tok

    def barrier(self):
        toks = []
        for key, d in self.deps.items():
            if key[0] == "wslot":
                continue
            toks.append(d.w)
            toks.extend(d.r.values())
        self._wait(self.dve, toks)
        ins = self.dve.eng.memset(self.bar_ap, 0.0)
        tok = self._signal(self.dve, ins)
        for q in (self.pe, self.act, self.pool, self.sp):
            self._wait(q, [tok])

    def dma(self, q, out, in_, semdep, R=(), W=(), **kw):
        self._wait(q, self._collect(R, W))
        ins = q.eng.dma_start(out=out, in_=in_, **kw)
        if semdep.sem is None:
            semdep.sem = self.newsem(f"dma{self.nsems}")
        semdep.nd += 1
        ins.then_inc(semdep.sem, 16)
        tok = (semdep.sem, 16 * semdep.nd)
        self._mark(tok, R, W)
        return tok


def build(nl=NL, stage="all", dbg_cols=0):
    nc = bass.Bass("TRN2", target_bir_lowering=False)
    xT_d = nc.dram_tensor("xT", [128, 8, S], F32, kind="ExternalInput").ap()
    pos_d = nc.dram_tensor("pos", [128, S], I32, kind="ExternalInput").ap()
    cm_d = nc.dram_tensor("cmat", [128, 5, 128], F32, kind="ExternalInput").ap()
    gv_d = nc.dram_tensor("gvec", [128, NG], F32, kind="ExternalInput").ap()
    vec_d = nc.dram_tensor("vec", [nl, 128, NV], F32, kind="ExternalInput").ap()
    wb_d = nc.dram_tensor("wblk", [nl * NBLK, 128, 1024], F32, kind="ExternalInput").ap()
    out_d = nc.dram_tensor("outT", [128, 8, S], F32, kind="ExternalOutput").ap()
    dbg_d = None
    if dbg_cols:
        dbg_d = nc.dram_tensor("dbg", [128, dbg_cols], F32, kind="ExternalOutput").ap()

    with ExitStack() as es:
        P = Prog(nc, es)
        D = P.D
        pe, act, dve, pool, sp = P.pe, P.act, P.dve, P.pool, P.sp

        def sb(name, shape, dt=F32):
            return es.enter_context(nc.sbuf_tensor("s_" + name, shape, dt))

        xT = sb("xT", [128, 8, S])
        hT = sb("hT", [128, 8, S], BF16)
        yT = sb("yT", [128, 4, S], BF16)
        ropeC = sb("ropeC", [128, S])
        ropeS = sb("ropeS", [128, S])
        cmb = sb("cmb", [128, 5, 128], BF16)
        cmf = sb("cmf", [128, 2, 128], F32)
        gv = sb("gv", [128, NG])
        vec = sb("vec", [128, nl, NV])
        wst = sb("wst", [128, NSLOT, 1024], BF16)
        SCR = 46 * 1024
        scr = sb("scr", [128, SCR // 4])
        ps = [es.enter_context(nc.psum_tensor(f"ps{i}", [128, 512], F32)) for i in range(7)]
        psT = es.enter_context(nc.psum_tensor("psT", [128, 1024], BF16))
        PSD = [D("ps", i) for i in range(7)]
        PSTD = D("psT")
        for d_ in PSD + [PSTD]:
            d_.x = True

        ident_b = cmb[:, 0, :]
        perm_b = cmb[:, 1, :]
        maskneg_b = cmb[:, 2, :]
        tri_b = cmb[:, 3, :]
        ones_b = cmb[:, 4, :]
        tri_f = cmf[:, 0, :]
        ones_f = cmf[:, 1, :]

        class Carver:
            def __init__(self):
                self.off = 0

            def reset(self):
                self.off = 0

            def take(self, shape, dt=F32):
                n = 1
                for s_ in shape[1:]:
                    n *= s_
                words = n if dt == F32 or dt == I32 else (n + 1) // 2
                assert self.off + words <= SCR // 4, (self.off, words)
                v = scr[:, self.off:self.off + words]
                self.off += words
                if dt == BF16:
                    v = v.bitcast(BF16)[:, 0:n]
                elif dt == I32:
                    v = v.bitcast(I32)
                if len(shape) == 3:
                    v = v.rearrange("p (a b) -> p a b", a=shape[1])
                elif len(shape) == 4:
                    v = v.rearrange("p (a b c) -> p a b c", a=shape[1], b=shape[2])
                return v

        cv = Carver()
        SCRD = D("scr")

        wstate = {"next_load": 0, "next_use": 0}
        total_blocks = nl * NBLK

        def wslotD(i):
            return D("wslot", i % NSLOT)

        def prefetch(upto):
            upto = min(upto, total_blocks)
            while wstate["next_load"] < upto:
                i = wstate["next_load"]
                P.dma(pool, wst[:, i % NSLOT, :], wb_d[i], wslotD(i), W=[wslotD(i)], max_dma_last_dim=4096)
                wstate["next_load"] += 1

        def wnext(expect=None):
            i = wstate["next_use"]
            if expect is not None:
                assert PLAN[i % NBLK][0] == expect, (PLAN[i % NBLK], expect)
            prefetch(i + 1)
            wstate["next_use"] += 1
            return wst[:, i % NSLOT, :], wslotD(i), i

        def wdone(i):
            prefetch(i + NSLOT + 1)

        dbgstate = {"off": 0}

        def dump(ap, deps, ncols, cast=False):
            o = dbgstate["off"]
            q = pool if cast else sp
            P.dma(q, dbg_d[:, o:o + ncols], ap, D("dbgout"), R=deps)
            dbgstate["off"] += ncols

        def finish():
            toks = [(d.sem, 16 * d.nd) for d in P.deps.values() if d.sem is not None]
            for q in (sp, pool):
                P._wait(q, toks)

        XD = [[D("xT", c, t) for t in range(NTT)] for c in range(8)]
        HD = [[D("hT", c, t) for t in range(NTT)] for c in range(8)]
        YD = [[D("yT", c, t) for t in range(NTT)] for c in range(4)]
        CONST = D("const")
        for c in range(8):
            xtok = P.dma(sp, xT[:, c, :], xT_d[:, c, :], D("xload"), W=[XD[c][t] for t in range(NTT)])
        for c in range(8):
            for t in range(NTT):
                XD[c][t].w = xtok
        P.dma(sp, cmf[:], cm_d[:, 3:5, :], CONST, W=[CONST])
        P.dma(sp, gv[:], gv_d, CONST, W=[CONST])
        for l in range(nl):
            P.dma(sp, vec[:, l, :], vec_d[l], CONST, W=[CONST])
        CONST.e.append(P.dma(pool, cmb[:], cm_d, D("constb")))
        prefetch(NSLOT)

        D_F = 1024.0
        epst = sb("epst", [128, 4])
        P.op(dve, lambda e: e.memset(epst[:, 0:1], EPS), W=[D("epst")])
        P.op(dve, lambda e: e.memset(epst[:, 1:2], 1.0), W=[D("epst")])
        P.op(dve, lambda e: e.memset(epst[:, 2:3], 0.0), W=[D("epst")])
        P.bar_ap = epst[:, 3:4]
        EPS_AP = epst[:, 0:1]
        ONE_AP = epst[:, 1:2]
        CONSTS = [CONST, D("epst")]
        ROPE = D("rope")
        cv.reset()
        posi = cv.take([128, S], I32)
        tA = cv.take([128, S])
        tB = cv.take([128, S])
        tK = cv.take([128, S], I32)
        P.dma(sp, posi, pos_d, D("posload"), W=[D("posi")])
        TWO_PI = 2.0 * math.pi
        C1 = 6.28125
        C2 = TWO_PI - C1
        P.op(dve, lambda e: e.tensor_copy(out=tA, in_=posi), R=[D("posi")], W=[D("tA")])
        P.op(dve, lambda e: e.tensor_scalar(out=tA, in0=tA, scalar1=gv[:, 8:9], scalar2=None, op0=ALU.mult),
             R=[CONST], W=[D("tA")])
        P.op(dve, lambda e: e.tensor_scalar(out=tK, in0=tA, scalar1=1.0 / TWO_PI, scalar2=None, op0=ALU.mult),
             R=[D("tA")], W=[D("tK")])
        P.op(dve, lambda e: e.tensor_copy(out=tB, in_=tK), R=[D("tK")], W=[D("tB")])
        P.op(dve, lambda e: e.scalar_tensor_tensor(out=tA, in0=tB, scalar=-C1, in1=tA, op0=ALU.mult, op1=ALU.add),
             R=[D("tB")], W=[D("tA")])
        P.op(dve, lambda e: e.scalar_tensor_tensor(out=tA, in0=tB, scalar=-C2, in1=tA, op0=ALU.mult, op1=ALU.add),
             R=[D("tB")], W=[D("tA")])

        def wrap(t, dname):
            P.op(dve, lambda e: e.tensor_scalar(out=tB, in0=t, scalar1=math.pi, scalar2=-TWO_PI, op0=ALU.is_gt,
                                                op1=ALU.mult), R=[D(dname)], W=[D("tB")])
            P.op(dve, lambda e: e.tensor_tensor(out=t, in0=t, in1=tB, op=ALU.add), R=[D("tB")], W=[D(dname)])
            P.op(dve, lambda e: e.tensor_scalar(out=tB, in0=t, scalar1=-math.pi, scalar2=TWO_PI, op0=ALU.is_lt,
                                                op1=ALU.mult), R=[D(dname)], W=[D("tB")])
            P.op(dve, lambda e: e.tensor_tensor(out=t, in0=t, in1=tB, op=ALU.add), R=[D("tB")], W=[D(dname)])
            P.op(dve, lambda e: e.tensor_scalar(out=t, in0=t, scalar1=3.1415925, scalar2=-3.1415925, op0=ALU.min,
                                                op1=ALU.max), R=[], W=[D(dname)])

        wrap(tA, "tA")
        P.op(act, lambda e: e.activation(out=ropeS[:], in_=tA, func=AF.Sin, scale=gv[:, 9:10]),
             R=[D("tA"), CONST], W=[ROPE])
        P.op(dve, lambda e: e.tensor_scalar(out=tA, in0=tA, scalar1=math.pi / 2, scalar2=None, op0=ALU.add),
             R=[ROPE], W=[D("tA")])
        wrap(tA, "tA")
        P.op(act, lambda e: e.activation(out=ropeC[:], in_=tA, func=AF.Sin), R=[D("tA")], W=[ROPE])
        P.barrier()

        def V(l, name, j=0, n=1):
            o = VOFF[name] + j
            return vec[:, l, o:o + n]

        def rmsnorm_to_hT(l, gname):
            cv.reset()
            sq = [cv.take([128, TT], BF16) for _ in range(4)]
            lnv = [cv.take([128, TT]) for _ in range(2)]
            rstd = [cv.take([128, TT]) for _ in range(2)]
            for t in range(NTT):
                tsl = slice(t * TT, (t + 1) * TT)
                for c in range(8):
                    b = sq[c % 4]
                    bd = D("nsq", c % 4)
                    if c % 2 == 0:
                        P.op(act, lambda e, b=b, c=c: e.activation(out=b, in_=xT[:, c, tsl], func=AF.Square),
                             R=[XD[c][t]], W=[bd])
                    else:
                        P.op(pool, lambda e, b=b, c=c: e.tensor_tensor(out=b, in0=xT[:, c, tsl], in1=xT[:, c, tsl],
                                                                          op=ALU.mult), R=[XD[c][t]], W=[bd])
                    P.mm([dict(out=ps[0][:], lhsT=ones_b, rhs=b, start=(c == 0), stop=(c == 7))],
                         R=[bd, CONST], W=[PSD[0]])
                ld = D("nln", t % 2)
                rd = D("nrs", t % 2)
                P.op(act, lambda e: e.activation(out=lnv[t % 2], in_=ps[0][:], func=AF.Ln, scale=1.0 / D_F, bias=EPS_AP),
                     R=[PSD[0]], W=[ld])
                P.op(act, lambda e: e.activation(out=rstd[t % 2], in_=lnv[t % 2], func=AF.Exp, scale=-0.5),
                     R=[ld], W=[rd])
                for c in range(8):
                    P.op(dve, lambda e, c=c: e.scalar_tensor_tensor(out=hT[:, c, tsl], in0=xT[:, c, tsl],
                                                                    scalar=V(l, gname, c), in1=rstd[t % 2],
                                                                    op0=ALU.mult, op1=ALU.mult),
                         R=[XD[c][t], rd, CONST], W=[HD[c][t]])
            P.barrier()


        def conv_A(src_ps, srcD, u, uD, halo, haloD, t, wcol, bcol, ntap):
            K1 = ntap - 1
            hm = len(haloD)
            P.op(act, lambda e: e.activation(out=u, in_=src_ps, func=AF.Identity, scale=wcol(K1), bias=bcol),
                 R=[srcD] + CONSTS, W=[uD])
            if t < NTT - 1:
                P.op(act, lambda e: e.activation(out=halo[:, t % hm, :], in_=src_ps[:, TT - K1:TT], func=AF.Copy),
                     R=[srcD], W=[haloD[t % hm]])

        def conv_B(src_ps, srcD, u, uD, halo, haloD, t, wcol, ntap):
            K1 = ntap - 1
            hm = len(haloD)
            for j in range(K1):
                sh = K1 - j
                P.op(dve, lambda e, j=j, sh=sh: e.scalar_tensor_tensor(out=u[:, sh:TT], in0=src_ps[:, 0:TT - sh],
                                                                       scalar=wcol(j), in1=u[:, sh:TT],
                                                                       op0=ALU.mult, op1=ALU.add),
                     R=[srcD] + CONSTS, W=[uD])
                if t > 0:
                    hp = halo[:, (t - 1) % hm, :]
                    P.op(dve, lambda e, j=j, sh=sh, hp=hp: e.scalar_tensor_tensor(
                        out=u[:, 0:sh], in0=hp[:, K1 - sh:K1], scalar=wcol(j), in1=u[:, 0:sh], op0=ALU.mult,
                        op1=ALU.add), R=[haloD[(t - 1) % hm]] + CONSTS, W=[uD])

        def conv_taps(src_ps, srcD, u, uD, halo, haloD, t, wcol, bcol, ntap, l):
            conv_A(src_ps, srcD, u, uD, halo, haloD, t, wcol, bcol, ntap)
            conv_B(src_ps, srcD, u, uD, halo, haloD, t, wcol, ntap)

        def wout_group(l, g):
            for pr in range(4):
                wsl, wd, wi = wnext("out")
                w4 = wsl.rearrange("p (d f c) -> p d f c", d=2, f=4)
                for d2 in range(2):
                    dc = pr * 2 + d2
                    for t in range(NTT):
                        tsl = slice(t * TT, (t + 1) * TT)
                        bk = (d2 * NTT + t) % 2
                        P.mm([dict(out=ps[bk][:], lhsT=w4[:, d2, f, :], rhs=yT[:, f, tsl], start=(f == 0),
                                   stop=(f == 3)) for f in range(4)],
                             R=[wd] + [YD[f][t] for f in range(4)], W=[PSD[bk]])
                        P.op(dve, lambda e, dc=dc, bk=bk: e.tensor_tensor(out=xT[:, dc, tsl], in0=ps[bk][:],
                                                                           in1=xT[:, dc, tsl], op=ALU.add),
                             R=[PSD[bk]], W=[XD[dc][t]])
                wdone(wi)

        def rg_group(l):
            cv.reset()
            ws = [wnext("in") for _ in range(8)]
            gw, gwd, gwi = wnext("gatew")
            gw4 = gw.rearrange("p (a n j) -> p a n j", a=2, n=4)
            nls = cv.take([128, 4])
            tmp4 = cv.take([128, 4])
            P.op(act, lambda e: e.activation(out=tmp4, in_=V(l, "rglam", 0, 4), func=AF.Exp, scale=-1.0),
                 R=CONSTS, W=[D("rgtmp4")])
            P.op(act, lambda e: e.activation(out=tmp4, in_=tmp4, func=AF.Ln, bias=ONE_AP), R=CONSTS,
                 W=[D("rgtmp4")])
            P.op(dve, lambda e: e.tensor_scalar(out=nls, in0=tmp4, scalar1=-RG_C, scalar2=None, op0=ALU.mult),
                 R=[D("rgtmp4")], W=[D("rgnls")])
            nls2 = cv.take([128, 4])
            P.op(dve, lambda e: e.tensor_scalar(out=nls2, in0=tmp4, scalar1=-2.0 * RG_C, scalar2=None,
                                                op0=ALU.mult), R=[D("rgtmp4")], W=[D("rgnls")])
            halo = [cv.take([128, 4, 3]) for _ in range(4)]
            HAL = [[D("rghalo", n, i_) for i_ in range(4)] for n in range(4)]
            hst = cv.take([128, 4, 2])
            u = [cv.take([128, TT]) for _ in range(3)]
            gg = [cv.take([128, TT]) for _ in range(3)]
            ub = [cv.take([128, TT], BF16) for _ in range(2)]
            rr = [cv.take([128, TT]) for _ in range(2)]
            ig = [cv.take([128, TT]) for _ in range(2)]
            aa = [cv.take([128, TT]) for _ in range(2)]
            hh = [cv.take([128, TT]) for _ in range(2)]
            btl = [cv.take([128, TT]) for _ in range(2)]
            ypre = cv.take([128, 4, TT])
            ysq = [cv.take([128, TT], BF16)]
            lnv = cv.take([128, TT])
            units = [(t, n) for t in range(NTT) for n in range(4)]
            NU = len(units)

            def S1(i):
                t, n = units[i]
                tsl = slice(t * TT, (t + 1) * TT)
                k3, b = i % 3, i % 2
                xb, gb = b, 2 + b
                wx_, wxd, _ = ws[n]
                wg_, wgd, _ = ws[4 + n]
                w8x = wx_.rearrange("p (k c) -> p k c", k=8)
                w8g = wg_.rearrange("p (k c) -> p k c", k=8)
                P.mm([dict(out=ps[xb][:], lhsT=w8x[:, kc, :], rhs=hT[:, kc, tsl], start=(kc == 0), stop=(kc == 7))
                      for kc in range(8)], R=[wxd] + [HD[kc][t] for kc in range(8)], W=[PSD[xb]])
                P.mm([dict(out=ps[gb][:], lhsT=w8g[:, kc, :], rhs=hT[:, kc, tsl], start=(kc == 0), stop=(kc == 7))
                      for kc in range(8)], R=[wgd] + [HD[kc][t] for kc in range(8)], W=[PSD[gb]])
                conv_A(ps[xb][:], PSD[xb], u[k3], D("rgu", k3), halo[n], HAL[n], t,
                       lambda j, n=n: V(l, "rgcw", n * 4 + j), V(l, "rgcb", n), 4)
                P.op(act, lambda e: e.activation(out=gg[k3], in_=ps[gb][:], func=AF.Square), R=[PSD[gb]],
                     W=[D("rgg", k3)])

            def S2(i):
                t, n = units[i]
                k3, b = i % 3, i % 2
                xb, gb, rb, ib = b, 2 + b, 4, 5
                uD, gD = D("rgu", k3), D("rgg", k3)
                conv_B(ps[xb][:], PSD[xb], u[k3], uD, halo[n], HAL[n], t, lambda j, n=n: V(l, "rgcw", n * 4 + j), 4)
                P.op(dve, lambda e: e.tensor_copy(out=ub[b], in_=u[k3]), R=[uD], W=[D("rgub", b)])
                P.op(dve, lambda e: e.tensor_scalar(out=gg[k3], in0=gg[k3], scalar1=0.044715, scalar2=1.0,
                                                    op0=ALU.mult, op1=ALU.add), R=[], W=[gD])
                P.op(dve, lambda e: e.tensor_tensor(out=gg[k3], in0=ps[gb][:], in1=gg[k3], op=ALU.mult),
                     R=[PSD[gb]], W=[gD])
                P.mm([dict(out=ps[rb][:], lhsT=gw4[:, 0, n, :], rhs=ub[b])], R=[gwd, D("rgub", b)], W=[PSD[rb]])
                P.mm([dict(out=ps[ib][:], lhsT=gw4[:, 1, n, :], rhs=ub[b])], R=[gwd, D("rgub", b)], W=[PSD[ib]])
                P.op(act, lambda e: e.activation(out=gg[k3], in_=gg[k3], func=AF.Sigmoid, scale=1.5957691216057308),
                     R=[], W=[gD])
                P.op(act, lambda e: e.activation(out=rr[b], in_=ps[rb][:], func=AF.Sigmoid, bias=V(l, "rgba", n)),
                     R=[PSD[rb]] + CONSTS, W=[D("rgr", b)])
                P.op(act, lambda e: e.activation(out=ig[b], in_=ps[ib][:], func=AF.Sigmoid, bias=V(l, "rgbx", n)),
                     R=[PSD[ib]] + CONSTS, W=[D("rgi", b)])
                P.op(dve, lambda e: e.tensor_tensor(out=gg[k3], in0=ps[gb][:], in1=gg[k3], op=ALU.mult),
                     R=[PSD[gb]], W=[gD])
                P.op(pool, lambda e: e.tensor_tensor(out=ig[b], in0=ig[b], in1=u[k3], op=ALU.mult), R=[uD],
                     W=[D("rgi", b)])

            def S3(i):
                t, n = units[i]
                tsl = slice(t * TT, (t + 1) * TT)
                k3, b = i % 3, i % 2
                uD, gD = D("rgu", k3), D("rgg", k3)
                aD, bD, hD = D("rga", b), D("rgbt", b), D("rgh", b)
                bt = btl[b]
                P.op(act, lambda e: e.activation(out=aa[b], in_=rr[b], func=AF.Exp, scale=nls[:, n:n + 1]),
                     R=[D("rgr", b), D("rgnls")], W=[aD])
                P.op(act, lambda e: e.activation(out=bt, in_=rr[b], func=AF.Exp, scale=nls2[:, n:n + 1]),
                     R=[D("rgr", b), D("rgnls")], W=[bD])
                P.op(act, lambda e: e.activation(out=bt, in_=bt, func=AF.Ln, scale=-1.0, bias=ONE_AP), R=CONSTS,
                     W=[bD])
                P.op(act, lambda e: e.activation(out=bt, in_=bt, func=AF.Exp, scale=0.5), R=[], W=[bD])

            def S3b(i):
                t, n = units[i]
                tsl = slice(t * TT, (t + 1) * TT)
                k3, b = i % 3, i % 2
                uD, gD = D("rgu", k3), D("rgg", k3)
                aD, bD, hD = D("rga", b), D("rgbt", b), D("rgh", b)
                bt = btl[b]
                P.op(dve, lambda e: e.tensor_tensor(out=bt, in0=bt, in1=ig[b], op=ALU.mult), R=[D("rgi", b)],
                     W=[bD])
                if t == 0:
                    init, initR = 0.0, []
                else:
                    init = hst[:, n, (t - 1) % 2:(t - 1) % 2 + 1]
                    initR = [D("rghst", n, (t - 1) % 2)]
                P.op(dve, lambda e: e.tensor_tensor_scan(out=hh[b], data0=aa[b], data1=bt, initial=init,
                                                         op0=ALU.mult, op1=ALU.add), R=[aD, bD] + initR, W=[hD])
                if t < NTT - 1:
                    P.op(pool, lambda e: e.tensor_copy(out=hst[:, n, t % 2:t % 2 + 1], in_=hh[b][:, TT - 1:TT]),
                         R=[hD], W=[D("rghst", n, t % 2)])
                yD = D("rgy", n)
                P.op(dve, lambda e: e.tensor_tensor(out=ypre[:, n, :], in0=gg[k3], in1=hh[b], op=ALU.mult),
                     R=[gD, hD], W=[yD])
                sD = D("rgysq", 0)
                P.op(pool, lambda e: e.tensor_tensor(out=ysq[0], in0=ypre[:, n, :], in1=ypre[:, n, :], op=ALU.mult),
                     R=[yD], W=[sD])

            def SS(i):
                t, n = units[i]
                tsl = slice(t * TT, (t + 1) * TT)
                b = i % 2
                P.mm([dict(out=ps[6][:], lhsT=ones_b, rhs=ysq[0], start=(n == 0), stop=(n == 3))],
                     R=[D("rgysq", 0), CONST], W=[PSD[6]])
                if n == 3:
                    P.op(act, lambda e: e.activation(out=lnv, in_=ps[6][:], func=AF.Ln, scale=1.0 / 512.0,
                                                     bias=EPS_AP), R=[PSD[6]] + CONSTS, W=[D("rgln")])
                    P.op(act, lambda e: e.activation(out=lnv, in_=lnv, func=AF.Exp, scale=-0.5), R=[],
                         W=[D("rgln")])
                    for n2 in range(4):
                        P.op(dve, lambda e, n2=n2: e.scalar_tensor_tensor(out=yT[:, n2, tsl], in0=ypre[:, n2, :],
                                                                          scalar=V(l, "rgnorm", n2), in1=lnv,
                                                                          op0=ALU.mult, op1=ALU.mult),
                             R=[D("rgy", n2), D("rgln")] + CONSTS, W=[YD[n2][t]])

            S1(0)
            S1(1)
            S2(0)
            for i in range(NU):
                if i + 2 < NU:
                    S1(i + 2)
                S3(i)
                if i + 1 < NU:
                    S2(i + 1)
                if i > 0:
                    SS(i - 1)
                S3b(i)
            SS(NU - 1)
            wdone(gwi)
            P.barrier()

        def da_group(l):
            lambda_init = 0.8 - 0.6 * math.exp(-0.3 * l)
            cv.reset()
            junk = cv.take([128, 128])
            s12 = cv.take([128, 2])
            nlam = cv.take([128, 1])
            dl = V(l, "dalam", 0, 256)
            LD = D("dalam_t")
            for i_ in range(2):
                P.op(dve, lambda e, i_=i_: e.tensor_tensor(out=junk[:, 0:64], in0=dl[:, i_ * 128:i_ * 128 + 64],
                                                           in1=dl[:, i_ * 128 + 64:i_ * 128 + 128], op=ALU.mult),
                     R=CONSTS, W=[LD])
                P.op(dve, lambda e, i_=i_: e.tensor_scalar(out=junk[:, 64:128], in0=junk[:, 0:64], scalar1=1.0,
                                                           scalar2=None, op0=ALU.mult, op1=ALU.add,
                                                           accum_out=s12[:, i_:i_ + 1]), R=[], W=[LD])
            P.op(act, lambda e: e.activation(out=s12, in_=s12, func=AF.Exp), R=[], W=[LD])
            P.op(dve, lambda e: e.tensor_tensor(out=nlam, in0=s12[:, 1:2], in1=s12[:, 0:1], op=ALU.subtract), R=[],
                 W=[LD])
            P.op(dve, lambda e: e.tensor_scalar(out=nlam, in0=nlam, scalar1=-lambda_init, scalar2=None, op0=ALU.add),
                 R=[], W=[LD])
            if stage == "dalam":
                dump(nlam, [LD], 1)
                return True
            qT = cv.take([128, S], BF16)
            kTz = [cv.take([128, S], BF16) for _ in range(2)]
            kT = kTz[0]
            vaug = cv.take([128, NB, 130], BF16)
            qb = [cv.take([128, TT], BF16) for _ in range(2)]
            t1 = [cv.take([128, TT]) for _ in range(2)]
            t2 = [cv.take([128, TT]) for _ in range(2)]
            PT = [[cv.take([128, TT], BF16) for _ in range(2)] for _ in range(2)]
            rrA = cv.take([128, TT])
            rrB = cv.take([128, TT])
            o1 = cv.take([128, TT])
            o2 = cv.take([128, TT])
            sqb = cv.take([128, TT], BF16)
            lnv = cv.take([128, TT])
            lbias = cv.take([128, 1])
            P.op(dve, lambda e: e.memset(lbias, math.log(1.0 - lambda_init)), W=[D("dalb")])
            psTf = psT[:].bitcast(F32)
            pending = []
            QD = [D("daq", t) for t in range(NTT)]
            KD = [D("dak", t) for t in range(NTT)]
            VD = [D("dav", g) for g in range(4)]
            P.op(dve, lambda e: e.memset(vaug[:, :, 128:129], 1.0), W=VD)
            P.op(dve, lambda e: e.memset(kTz[0][64:128, :], 0.0), W=[D("dakz")])
            P.op(dve, lambda e: e.memset(kTz[1][0:64, :], 0.0), W=[D("dakz")])
            ctr = {"rp": 0, "st": 0, "ep": 0, "tp": 0}
            for h in range(4):
                wq, wqd, _ = wnext("in")
                wk, wkd, _ = wnext("in")
                wv, wvd, wvi = wnext("in")
                wq8 = wq.rearrange("p (k c) -> p k c", k=8)
                wk8 = wk.rearrange("p (k c) -> p k c", k=8)
                wv8 = wv.rearrange("p (k c) -> p k c", k=8)
                punits = [(t, which) for t in range(NTT) for which in range(2)]
                pinfo = [(wq8, wqd, qT, QD, 0.125), (wk8, wkd, kT, KD, 1.0)]
                base = ctr["rp"]

                def prA(ui):
                    t, which = punits[ui]
                    w8, wd_, dst, dstD, scl = pinfo[which]
                    tsl = slice(t * TT, (t + 1) * TT)
                    k = (base + ui) % 2
                    pb = k
                    P.mm([dict(out=ps[pb][:], lhsT=w8[:, kc, :], rhs=hT[:, kc, tsl], start=(kc == 0),
                               stop=(kc == 7)) for kc in range(8)],
                         R=[wd_] + [HD[kc][t] for kc in range(8)], W=[PSD[pb]])
                    P.op(act, lambda e: e.activation(out=qb[k], in_=ps[pb][:], func=AF.Copy), R=[PSD[pb]],
                         W=[D("daqb", k)])

                def prB(ui):
                    t, which = punits[ui]
                    w8, wd_, dst, dstD, scl = pinfo[which]
                    tsl = slice(t * TT, (t + 1) * TT)
                    k = (base + ui) % 2
                    pb, sbk = k, 2 + k
                    P.mm([dict(out=ps[sbk][:], lhsT=perm_b, rhs=qb[k])], R=[D("daqb", k), CONST], W=[PSD[sbk]])
                    P.op(dve, lambda e: e.scalar_tensor_tensor(out=t1[k], in0=ps[pb][:], scalar=scl,
                                                               in1=ropeC[:, tsl], op0=ALU.mult, op1=ALU.mult),
                         R=[PSD[pb], ROPE], W=[D("dat1", k)])
                    P.op(dve, lambda e: e.scalar_tensor_tensor(out=t2[k], in0=ps[sbk][:], scalar=scl,
                                                               in1=ropeS[:, tsl], op0=ALU.mult, op1=ALU.mult),
                         R=[PSD[sbk], ROPE], W=[D("dat2", k)])
                    if which == 0:
                        P.op(pool, lambda e: e.tensor_tensor(out=dst[:, tsl], in0=t1[k], in1=t2[k], op=ALU.add),
                             R=[D("dat1", k), D("dat2", k)], W=[dstD[t]])
                    else:
                        for c_ in range(2):
                            pr_ = slice(c_ * 64, (c_ + 1) * 64)
                            P.op(pool, lambda e, c_=c_, pr_=pr_: e.tensor_tensor(
                                out=kTz[c_][pr_, tsl], in0=t1[k][pr_, :], in1=t2[k][pr_, :], op=ALU.add),
                                R=[D("dat1", k), D("dat2", k), D("dakz")], W=[dstD[t]])

                prA(0)
                for ui in range(len(punits)):
                    if ui + 1 < len(punits):
                        prA(ui + 1)
                    prB(ui)
                ctr["rp"] += len(punits)
                for g4 in range(4):
                    vb = 4 + (g4 % 2)
                    mms = []
                    for i in range(4):
                        tb = g4 * 4 + i
                        for kc in range(8):
                            mms.append(dict(out=ps[vb][:, i * 128:(i + 1) * 128],
                                            lhsT=hT[:, kc, tb * 128:(tb + 1) * 128], rhs=wv8[:, kc, :],
                                            start=(kc == 0), stop=(kc == 7)))
                    P.mm(mms, R=[wvd] + [HD[kc][g4] for kc in range(8)], W=[PSD[vb]])
                    P.op(act, lambda e, g4=g4, vb=vb: e.activation(
                        out=vaug[:, g4 * 4:(g4 + 1) * 4, 0:128],
                        in_=ps[vb][:].rearrange("p (a b) -> p a b", a=4), func=AF.Copy), R=[PSD[vb]], W=[VD[g4]])
                wdone(wvi)
                if stage == "daq":
                    dump(qT, QD, S, cast=True)
                    dump(kTz[0], KD, S, cast=True)
                    dump(vaug.rearrange("p a b -> p (a b)"), VD, NB * 130, cast=True)
                    return True
                for qg in range(4 if stage != "da1" else 1):
                    qsl = slice(qg * TT, (qg + 1) * TT)
                    UB = [ps[4], ps[5]]
                    RB = [ps[6], psTf]
                    UD_ = [PSD[4], PSD[5]]
                    RD_ = [PSD[6], PSTD]
                    nj = 4 * qg + 4
                    deferred = []
                    for j in range(nj):
                        r = max(0, j - 4 * qg)
                        c0 = r * 128
                        par = j % 2
                        cur = []
                        for c in range(2):
                            sbank = c * 2 + par
                            prow = slice(c * 64, (c + 1) * 64)
                            mms = [dict(out=ps[sbank][:, c0:TT], lhsT=kTz[c][:, j * 128:(j + 1) * 128],
                                        rhs=qT[:, qg * TT + c0:(qg + 1) * TT], start=True, stop=(j < 4 * qg),
                                        sgc=True)]
                            if j >= 4 * qg:
                                mms.append(dict(out=ps[sbank][:, c0:c0 + 128], lhsT=ident_b, rhs=maskneg_b,
                                                start=False, stop=True, sgc=True))
                            P.mm(mms, R=[KD[j // 4], QD[qg], CONST], W=[PSD[sbank]])
                        for c in range(2):
                            sbank = c * 2 + par
                            ptD = D("dapt", c, par)
                            P.op(act, lambda e, c=c, par=par, sbank=sbank, c0=c0: e.activation(
                                out=PT[c][par][:, c0:TT], in_=ps[sbank][:, c0:TT], func=AF.Exp),
                                R=[PSD[sbank]], W=[ptD])

                            def acc(c=c, par=par, c0=c0, j=j, ptD=ptD):
                                P.mm([dict(out=UB[c][:, c0:TT], lhsT=vaug[:, j, 0:128], rhs=PT[c][par][:, c0:TT],
                                           start=(j == 0), stop=(j == nj - 1), sgc=True)],
                                     R=[ptD, VD[j // 4]], W=[UD_[c]])
                                P.mm([dict(out=RB[c][:, c0:TT], lhsT=ones_b, rhs=PT[c][par][:, c0:TT],
                                           start=(j == 0), stop=(j == nj - 1), sgc=True)],
                                     R=[ptD, CONST], W=[RD_[c]])

                            cur.append(acc)
                        if j == 1 and pending:
                            pending.pop(0)()
                        for f_ in deferred:
                            f_()
                        deferred = cur
                    for f_ in deferred:
                        f_()

                    aD, bD, o1D, o2D = D("darrA"), D("darrB"), D("dao1"), D("dao2")
                    P.op(act, lambda e: e.activation(out=rrA, in_=RB[0][:, :], func=AF.Ln), R=[RD_[0]], W=[aD])
                    P.op(act, lambda e: e.activation(out=rrB, in_=RB[1][:, :], func=AF.Ln), R=[RD_[1]], W=[bD])
                    P.op(act, lambda e: e.activation(out=rrA, in_=rrA, func=AF.Exp, scale=-1.0), R=[], W=[aD])
                    P.op(act, lambda e: e.activation(out=rrB, in_=rrB, func=AF.Exp, scale=-1.0), R=[], W=[bD])
                    P.op(dve, lambda e: e.tensor_tensor(out=o1, in0=UB[0][:, :], in1=rrA, op=ALU.mult),
                         R=[UD_[0], aD], W=[o1D])
                    P.op(dve, lambda e: e.scalar_tensor_tensor(out=o2, in0=UB[1][:, :], scalar=nlam[:, 0:1], in1=rrB,
                                                               op0=ALU.mult, op1=ALU.mult),
                         R=[UD_[1], bD, LD], W=[o2D])

                    def tail(qg=qg, h=h, qsl=qsl):
                        o1D, o2D, sD, lD = D("dao1"), D("dao2"), D("dasq"), D("daln")
                        P.op(pool, lambda e: e.tensor_tensor(out=o1, in0=o1, in1=o2, op=ALU.add), R=[o2D], W=[o1D])
                        P.op(pool, lambda e: e.tensor_tensor(out=sqb, in0=o1, in1=o1, op=ALU.mult), R=[o1D], W=[sD])
                        P.mm([dict(out=ps[6][:, :], lhsT=ones_b, rhs=sqb)], R=[sD, CONST], W=[PSD[6]])
                        P.op(act, lambda e: e.activation(out=lnv, in_=ps[6][:, :], func=AF.Ln, scale=1.0 / 128.0,
                                                         bias=EPS_AP), R=[PSD[6]] + CONSTS, W=[lD])
                        P.op(act, lambda e: e.activation(out=lnv, in_=lnv, func=AF.Exp, scale=-0.5, bias=lbias),
                             R=[D("dalb")], W=[lD])
                        P.op(dve, lambda e: e.scalar_tensor_tensor(out=yT[:, h, qsl], in0=o1,
                                                                   scalar=V(l, "danorm", 0), in1=lnv, op0=ALU.mult,
                                                                   op1=ALU.mult),
                             R=[o1D, lD] + CONSTS, W=[YD[h][qg]])

                    pending.append(tail)
                while pending:
                    pending.pop(0)()
            P.barrier()

        def ml_group(l):
            cv.reset()
            wg, wgd, wgi = wnext("gates")
            wg8 = wg.rearrange("p (k c) -> p k c", k=8)
            mms = []
            for c in range(NB):
                for kc in range(8):
                    mms.append(dict(out=ps[0][:, c * 8:(c + 1) * 8], lhsT=hT[:, kc, c * 128:(c + 1) * 128],
                                    rhs=wg8[:, kc, 0:8], start=(kc == 0), stop=(kc == 7)))
            P.mm(mms, R=[wgd] + [HD[kc][t] for kc in range(8) for t in range(NTT)], W=[PSD[0]])
            wdone(wgi)
            gview = ps[0][:, 0:128].rearrange("p (c g) -> p g c", g=8)
            li = cv.take([128, 4, NB])
            fp = cv.take([128, 4, NB])
            aa = cv.take([128, 4, NB])
            ebL = cv.take([128, 4, NB])
            d1 = cv.take([128, 4, NB])
            d0 = cv.take([128, 4, NB])
            Em = cv.take([128, 4, NB])
            rZ = cv.take([128, 4, NB])
            dcy = cv.take([128, 4, NB + 1])
            esk = cv.take([128, 4, NB])
            bnd = cv.take([128, 4, NB])
            Emp = cv.take([128, 4, NB])
            GD = D("mlg")

            def f2(x):
                return x.rearrange("p a b -> p (a b)")

            for h in range(4):
                P.op(dve, lambda e, h=h: e.tensor_scalar(out=li[:, h, :], in0=gview[:, h, :],
                                                         scalar1=V(l, "ibias", h), scalar2=None, op0=ALU.add),
                     R=[PSD[0]] + CONSTS, W=[GD])
                P.op(dve, lambda e, h=h: e.tensor_scalar(out=fp[:, h, :], in0=gview[:, 4 + h, :],
                                                         scalar1=V(l, "fbias", h), scalar2=None, op0=ALU.add),
                     R=[PSD[0]] + CONSTS, W=[GD])
            P.op(act, lambda e: e.activation(out=f2(fp), in_=f2(fp), func=AF.Exp, scale=-1.0), R=[], W=[GD])
            P.op(act, lambda e: e.activation(out=f2(fp), in_=f2(fp), func=AF.Ln, bias=ONE_AP), R=CONSTS, W=[GD])
            P.mm([dict(out=ps[1][:, 0:64], lhsT=tri_f, rhs=f2(fp))], R=[GD, CONST], W=[PSD[1]])
            P.mm([dict(out=ps[2][:, 0:64], lhsT=ones_f, rhs=f2(fp))], R=[GD, CONST], W=[PSD[2]])
            P.op(dve, lambda e: e.tensor_tensor(out=f2(aa), in0=ps[1][:, 0:64], in1=f2(li), op=ALU.add),
                 R=[PSD[1]], W=[GD])
            P.op(act, lambda e: e.activation(out=f2(aa), in_=f2(aa), func=AF.Exp), R=[], W=[GD])
            P.mm([dict(out=ps[3][:, 0:64], lhsT=ones_f, rhs=f2(aa))], R=[GD, CONST], W=[PSD[3]])
            P.op(act, lambda e: e.activation(out=f2(ebL), in_=ps[2][:, 0:64], func=AF.Exp, scale=-1.0), R=[PSD[2]],
                 W=[GD])
            P.op(dve, lambda e: e.tensor_tensor(out=f2(d1), in0=ps[3][:, 0:64], in1=f2(ebL), op=ALU.mult),
                 R=[PSD[3]], W=[GD])
            P.op(dve, lambda e: e.tensor_tensor(out=d1[:, :, 0:1], in0=d1[:, :, 0:1], in1=ebL[:, :, 0:1], op=ALU.add),
                 R=[], W=[GD])
            P.op(dve, lambda e: e.tensor_copy(out=f2(d0), in_=f2(ebL)), R=[], W=[GD])
            P.op(dve, lambda e: e.memset(d0[:, :, 0:1], 0.0), R=[], W=[GD])
            P.op(dve, lambda e: e.tensor_tensor_scan(out=f2(Em), data0=f2(d0), data1=f2(d1), initial=0.0,
                                                     op0=ALU.mult, op1=ALU.add), R=[], W=[GD])
            P.op(dve, lambda e: e.reciprocal(out=f2(rZ), in_=f2(Em)), R=[], W=[GD])
            P.op(dve, lambda e: e.tensor_tensor(out=f2(rZ), in0=f2(rZ), in1=f2(ebL), op=ALU.mult), R=[], W=[GD])
            P.op(dve, lambda e: e.memset(Emp[:, :, 0:1], 1.0), R=[], W=[GD])
            P.op(dve, lambda e: e.tensor_copy(out=Emp[:, :, 1:NB], in_=Em[:, :, 0:NB - 1]), R=[], W=[GD])
            P.op(dve, lambda e: e.memset(dcy[:, :, NB:NB + 1], 1.0), R=[], W=[GD])
            P.op(dve, lambda e: e.tensor_tensor(out=dcy[:, :, 0:NB], in0=Emp[:, :, :], in1=rZ[:, :, :], op=ALU.mult),
                 R=[], W=[GD])
            P.op(dve, lambda e: e.scalar_tensor_tensor(out=f2(esk), in0=f2(aa), scalar=128.0 ** -0.5, in1=f2(rZ),
                                                       op0=ALU.mult, op1=ALU.mult), R=[], W=[GD])
            P.op(act, lambda e: e.activation(out=f2(bnd), in_=ps[1][:, 0:64], func=AF.Exp), R=[PSD[1]], W=[GD])
            P.op(dve, lambda e: e.tensor_tensor(out=f2(bnd), in0=f2(bnd), in1=f2(rZ), op=ALU.mult), R=[], W=[GD])

            qT = cv.take([128, S], BF16)
            kT = cv.take([128, S], BF16)
            vaug = cv.take([128, NB, 130], BF16)
            sgoT = cv.take([128, S], BF16)
            u = [cv.take([128, TT]) for _ in range(3)]
            halo = [cv.take([128, 4, 3]) for _ in range(2)]
            MLH = [[D("mlhalo", w_, i_) for i_ in range(4)] for w_ in range(2)]
            WT = [cv.take([128, 128], BF16) for _ in range(2)]
            ktok = [cv.take([128, 128], BF16) for _ in range(2)]
            Cst = cv.take([128, 130])
            Cs = [cv.take([128, 130], BF16) for _ in range(2)]
            E4 = [cv.take([128, 4, 129]) for _ in range(2)]
            hn = cv.take([128, 4, 128])
            sq4 = cv.take([128, 4, 128])
            yb4 = [cv.take([128, 4, 128], BF16) for _ in range(2)]
            a4 = cv.take([128, 4])
            ss4 = cv.take([128, 4])
            QD = [D("mlq", t) for t in range(NTT)]
            KD = [D("mlk", t) for t in range(NTT)]
            VD = [D("mlv", g) for g in range(4)]
            OD = [D("mlo", g) for g in range(4)]
            P.op(dve, lambda e: e.memset(vaug[:, :, 128:129], 1.0), W=VD)
            ctr = {"u": 0, "c": 0}
            epi_parts = []
            for h in range(4):
                wq, wqd, _ = wnext("in")
                wk, wkd, _ = wnext("in")
                wv, wvd, _ = wnext("in")
                wo, wod, woi = wnext("in")
                w8 = [x.rearrange("p (k c) -> p k c", k=8) for x in (wq, wk, wv, wo)]
                punits = [(which, t) for which in range(3) for t in range(NTT)]
                pinfo = [(wqd, qT, QD), (wkd, kT, KD), (wod, sgoT, OD)]
                widx = [0, 1, 3]

                def pA(ui):
                    which, t = punits[ui]
                    wd_, dst, dstD = pinfo[which]
                    ch = which * 4 + h
                    tsl = slice(t * TT, (t + 1) * TT)
                    n_ = ctr["u"] + ui
                    k, pb = n_ % 3, n_ % 2
                    P.mm([dict(out=ps[pb][:], lhsT=w8[widx[which]][:, kc, :], rhs=hT[:, kc, tsl],
                               start=(kc == 0), stop=(kc == 7)) for kc in range(8)],
                         R=[wd_] + [HD[kc][t] for kc in range(8)], W=[PSD[pb]])
                    if which < 2:
                        conv_A(ps[pb][:], PSD[pb], u[k], D("mlu", k), halo[which], MLH[which], t,
                               lambda j, ch=ch: V(l, "mlcw", ch * 4 + j), V(l, "mlcb", ch), 4)

                def pB(ui):
                    which, t = punits[ui]
                    wd_, dst, dstD = pinfo[which]
                    ch = which * 4 + h
                    tsl = slice(t * TT, (t + 1) * TT)
                    n_ = ctr["u"] + ui
                    k, pb = n_ % 3, n_ % 2
                    if which == 2:
                        P.op(act, lambda e, dst=dst, pb=pb: e.activation(out=dst[:, tsl], in_=ps[pb][:],
                                                                         func=AF.Sigmoid),
                             R=[PSD[pb]], W=[dstD[t]])
                        return
                    conv_B(ps[pb][:], PSD[pb], u[k], D("mlu", k), halo[which], MLH[which], t,
                           lambda j, ch=ch: V(l, "mlcw", ch * 4 + j), 4)
                    P.op(act, lambda e, k=k, dst=dst: e.activation(out=dst[:, tsl], in_=u[k], func=AF.Silu),
                         R=[D("mlu", k)], W=[dstD[t]])

                pA(0)
                for ui in range(len(punits)):
                    if ui + 1 < len(punits):
                        pA(ui + 1)
                    pB(ui)
                    if epi_parts:
                        epi_parts.pop(0)()
                ctr["u"] += len(punits)
                for g4 in range(4):
                    vb = 2 + (g4 % 2)
                    mms = []
                    for i_ in range(4):
                        tb = g4 * 4 + i_
                        for kc in range(8):
                            mms.append(dict(out=ps[vb][:, i_ * 128:(i_ + 1) * 128],
                                            lhsT=hT[:, kc, tb * 128:(tb + 1) * 128], rhs=w8[2][:, kc, :],
                                            start=(kc == 0), stop=(kc == 7)))
                    P.mm(mms, R=[wvd] + [HD[kc][g4] for kc in range(8)], W=[PSD[vb]])
                    P.op(act, lambda e, g4=g4, vb=vb: e.activation(
                        out=vaug[:, g4 * 4:(g4 + 1) * 4, 0:128],
                        in_=ps[vb][:].rearrange("p (a b) -> p a b", a=4), func=AF.Copy), R=[PSD[vb]], W=[VD[g4]])
                wdone(woi)
                CD = D("mlC")
                P.op(dve, lambda e: e.memset(Cst, 0.0), R=[], W=[CD])

                def stageA(c):
                    csl = slice(c * 128, (c + 1) * 128)
                    k = c % 2
                    eskc = f2(esk)[:, h * NB + c:h * NB + c + 1]
                    sb_ = 4 + k
                    P.mm([dict(out=ps[sb_][:, 0:128], lhsT=kT[:, csl], rhs=qT[:, csl])], R=[KD[c // 4], QD[c // 4]],
                         W=[PSD[sb_]])
                    P.op(dve, lambda e: e.scalar_tensor_tensor(out=WT[k], in0=ps[sb_][:, 0:128], scalar=eskc,
                                                               in1=tri_f, op0=ALU.mult, op1=ALU.mult),
                         R=[PSD[sb_], GD, CONST], W=[D("mlWT", k)])
                    P.mm([dict(tr=True, out=psT[:, k * 128:(k + 1) * 128], in_=kT[:, csl], identity=ident_b)],
                         R=[KD[c // 4], CONST], W=[PSTD])
                    P.op(act, lambda e: e.activation(out=ktok[k], in_=psT[:, k * 128:(k + 1) * 128], func=AF.Copy,
                                                     scale=eskc), R=[PSTD, GD], W=[D("mlktok", k)])
                    if c < NB - 1:
                        P.mm([dict(out=ps[2 + k][:, 0:129], lhsT=ktok[k], rhs=vaug[:, c, 0:129])],
                             R=[D("mlktok", k), VD[c // 4]], W=[PSD[2 + k]])

                stageA(0)
                for c in range(NB):
                    csl = slice(c * 128, (c + 1) * 128)
                    k = c % 2
                    eb = (c // 4) % 2
                    if c + 1 < NB:
                        stageA(c + 1)
                    mms = []
                    if c > 0:
                        mms.append(dict(out=ps[6][:, 0:129], lhsT=qT[:, csl], rhs=Cs[(c - 1) % 2][:, 0:129],
                                        start=True, stop=False))
                    mms.append(dict(out=ps[6][:, 0:129], lhsT=WT[k], rhs=vaug[:, c, 0:129], start=(c == 0),
                                    stop=True))
                    P.mm(mms, R=[QD[c // 4], D("mlWT", k), VD[c // 4]] + ([D("mlCs", (c - 1) % 2)] if c > 0 else []),
                         W=[PSD[6]])
                    P.op(act, lambda e, c=c, eb=eb: e.activation(out=E4[eb][:, c % 4, :], in_=ps[6][:, 0:129],
                                                                 func=AF.Copy), R=[PSD[6]], W=[D("mlE4", eb)])
                    if c < NB - 1:
                        dc_ = dcy[:, h, c:c + 1]
                        dn_ = dcy[:, h, c + 1:c + 2]
                        P.op(dve, lambda e, dc_=dc_, k=k: e.scalar_tensor_tensor(
                            out=Cst[:, 0:129], in0=Cst[:, 0:129], scalar=dc_, in1=ps[2 + k][:, 0:129],
                            op0=ALU.mult, op1=ALU.add), R=[PSD[2 + k], GD], W=[CD])
                        P.op(dve, lambda e, dn_=dn_, c=c: e.tensor_scalar(out=Cs[c % 2][:, 0:129], in0=Cst[:, 0:129],
                                                                          scalar1=dn_, scalar2=None, op0=ALU.mult),
                             R=[GD], W=[D("mlCs", c % 2)])
                    if epi_parts:
                        epi_parts.pop(0)()
                    if c % 4 == 3:
                        c0 = c - 3
                        E_ = E4[eb]
                        eD, aD, hD, qD, sD, yD = (D("mlE4", eb), D("mla4"), D("mlhn"), D("mlsq4"), D("mlss4"),
                                                  D("mlyb4", eb))

                        def p1(E_=E_, eD=eD, aD=aD, hD=hD, qD=qD, c0=c0, h=h):
                            dn4 = E_[:, :, 128]
                            P.op(dve, lambda e: e.scalar_tensor_tensor(out=a4, in0=dn4, scalar=-1.0, in1=dn4,
                                                                       op0=ALU.mult, op1=ALU.max), R=[eD], W=[aD])
                            P.op(dve, lambda e: e.tensor_tensor(out=a4, in0=a4, in1=bnd[:, h, c0:c0 + 4],
                                                                op=ALU.max), R=[GD], W=[aD])
                            P.op(dve, lambda e: e.reciprocal(out=a4, in_=a4), R=[], W=[aD])
                            P.op(dve, lambda e: e.tensor_tensor(
                                out=hn, in0=E_[:, :, 0:128], in1=a4.unsqueeze(2).to_broadcast([128, 4, 128]),
                                op=ALU.mult), R=[eD, aD], W=[hD])
                            P.op(pool, lambda e: e.tensor_tensor(out=sq4, in0=hn, in1=hn, op=ALU.mult), R=[hD],
                                 W=[qD])

                        def p2(qD=qD, sD=sD):
                            P.op(dve, lambda e: e.tensor_reduce(out=ss4, in_=sq4, axis=mybir.AxisListType.X,
                                                                op=ALU.add), R=[qD], W=[sD])
                            P.op(act, lambda e: e.activation(out=ss4, in_=ss4, func=AF.Ln, scale=1.0 / 128.0,
                                                             bias=EPS_AP), R=CONSTS, W=[sD])
                            P.op(act, lambda e: e.activation(out=ss4, in_=ss4, func=AF.Exp, scale=-0.5), R=[],
                                 W=[sD])

                        def p3(sD=sD, qD=qD, hD=hD, yD=yD, eb=eb):
                            P.op(dve, lambda e: e.tensor_tensor(
                                out=yb4[eb], in0=hn, in1=ss4.unsqueeze(2).to_broadcast([128, 4, 128]), op=ALU.mult),
                                R=[sD, qD, hD], W=[yD])

                        def p4(yD=yD, c0=c0, eb=eb, c=c, h=h):
                            P.mm([dict(tr=True, out=psT[:, 512 + ii * 128:512 + (ii + 1) * 128],
                                       in_=yb4[eb][:, ii, :], identity=ident_b) for ii in range(4)],
                                 R=[yD, CONST], W=[PSTD])
                            P.op(dve, lambda e: e.scalar_tensor_tensor(
                                out=yT[:, h, c0 * 128:(c0 + 4) * 128], in0=psT[:, 512:1024],
                                scalar=V(l, "mlnorm", h), in1=sgoT[:, c0 * 128:(c0 + 4) * 128], op0=ALU.mult,
                                op1=ALU.mult), R=[PSTD, OD[c // 4]] + CONSTS, W=[YD[h][c // 4]])

                        while epi_parts:
                            epi_parts.pop(0)()
                        epi_parts.extend([p1, p2, p3, p4])
                if h == 3:
                    while epi_parts:
                        epi_parts.pop(0)()
            P.barrier()

        def ffn(l):
            cv.reset()
            aT = cv.take([128, 6, S], BF16)
            NBUF = 3
            ug = [cv.take([128, TT]) for _ in range(NBUF)]
            uv = [cv.take([128, TT]) for _ in range(NBUF)]
            sg = [cv.take([128, TT]) for _ in range(NBUF)]
            hg = cv.take([128, 4, 2])
            hv = cv.take([128, 4, 2])
            HG = [D("fhg", i_) for i_ in range(4)]
            HV = [D("fhv", i_) for i_ in range(4)]
            it = 0
            for G in FFN_GROUPS:
                units = []
                for gi, j in enumerate(G):
                    wgk, wgd, _ = wnext("up")
                    wvk, wvd, wvi = wnext("up")
                    wg8 = wgk.rearrange("p (k c) -> p k c", k=8)
                    wv8 = wvk.rearrange("p (k c) -> p k c", k=8)
                    for t in range(NTT):
                        units.append((gi, j, t, wg8, wgd, wv8, wvd, wvi))

                def stA(u_, it_):
                    gi, j, t, wg8, wgd, wv8, wvd, wvi = u_
                    tsl = slice(t * TT, (t + 1) * TT)
                    k, b = it_ % NBUF, it_ % 2
                    gbk, vbk = b, 2 + b
                    P.mm([dict(out=ps[gbk][:], lhsT=wg8[:, kc, :], rhs=hT[:, kc, tsl], start=(kc == 0),
                               stop=(kc == 7)) for kc in range(8)],
                         R=[wgd] + [HD[kc][t] for kc in range(8)], W=[PSD[gbk]])
                    P.mm([dict(out=ps[vbk][:], lhsT=wv8[:, kc, :], rhs=hT[:, kc, tsl], start=(kc == 0),
                               stop=(kc == 7)) for kc in range(8)],
                         R=[wvd] + [HD[kc][t] for kc in range(8)], W=[PSD[vbk]])
                    if t == NTT - 1:
                        wdone(wvi)
                    conv_A(ps[gbk][:], PSD[gbk], ug[k], D("fug", k), hg, HG, t,
                           lambda jj, j=j: V(l, "fcw", (2 * j) * 3 + jj), V(l, "fcb", 2 * j), 3)
                    conv_A(ps[vbk][:], PSD[vbk], uv[k], D("fuv", k), hv, HV, t,
                           lambda jj, j=j: V(l, "fcw", (2 * j + 1) * 3 + jj), V(l, "fcb", 2 * j + 1), 3)

                def stBC(u_, it_):
                    gi, j, t, wg8, wgd, wv8, wvd, wvi = u_
                    tsl = slice(t * TT, (t + 1) * TT)
                    k, b = it_ % NBUF, it_ % 2
                    gbk, vbk = b, 2 + b
                    conv_B(ps[gbk][:], PSD[gbk], ug[k], D("fug", k), hg, HG, t,
                           lambda jj, j=j: V(l, "fcw", (2 * j) * 3 + jj), 3)
                    conv_B(ps[vbk][:], PSD[vbk], uv[k], D("fuv", k), hv, HV, t,
                           lambda jj, j=j: V(l, "fcw", (2 * j + 1) * 3 + jj), 3)
                    P.op(act, lambda e, k=k: e.activation(out=sg[k], in_=ug[k], func=AF.Silu), R=[D("fug", k)],
                         W=[D("fsg", k)])
                    P.op(pool, lambda e, k=k, gi=gi: e.tensor_tensor(out=aT[:, gi, tsl], in0=sg[k], in1=uv[k],
                                                                      op=ALU.mult),
                         R=[D("fsg", k), D("fuv", k)], W=[D("faT", gi, t)])

                stA(units[0], it)
                for ui in range(len(units)):
                    if ui + 1 < len(units):
                        stA(units[ui + 1], it + ui + 1)
                    stBC(units[ui], it + ui)
                it += len(units)
                wds = [wnext("down") for _ in G]
                for dc in range(8):
                    for t in range(NTT):
                        tsl = slice(t * TT, (t + 1) * TT)
                        bk = 4 + (dc * NTT + t) % 2
                        P.mm([dict(out=ps[bk][:], lhsT=wds[gi][0][:, dc * 128:(dc + 1) * 128], rhs=aT[:, gi, tsl],
                                   start=(gi == 0), stop=(gi == len(G) - 1)) for gi in range(len(G))],
                             R=[w_[1] for w_ in wds] + [D("faT", gi, t) for gi in range(len(G))], W=[PSD[bk]])
                        P.op(dve, lambda e, dc=dc, bk=bk: e.tensor_tensor(out=xT[:, dc, tsl], in0=ps[bk][:],
                                                                           in1=xT[:, dc, tsl], op=ALU.add),
                             R=[PSD[bk]], W=[XD[dc][t]])
                wdone(wds[-1][2])
            P.barrier()

        def final_norm():
            cv.reset()
            sq = [cv.take([128, TT], BF16) for _ in range(4)]
            lnv = [cv.take([128, TT]) for _ in range(2)]
            rstd = [cv.take([128, TT]) for _ in range(2)]
            ost = [cv.take([128, 8, TT]) for _ in range(2)]
            OUTD = D("outd")
            for t in range(NTT):
                tsl = slice(t * TT, (t + 1) * TT)
                for c in range(8):
                    b = sq[c % 4]
                    bd = D("nsq", c % 4)
                    P.op(act, lambda e, b=b, c=c: e.activation(out=b, in_=xT[:, c, tsl], func=AF.Square),
                         R=[XD[c][t]], W=[bd])
                    P.mm([dict(out=ps[0][:], lhsT=ones_b, rhs=b, start=(c == 0), stop=(c == 7))],
                         R=[bd, CONST], W=[PSD[0]])
                ld = D("nln", t % 2)
                rd = D("nrs", t % 2)
                P.op(act, lambda e: e.activation(out=lnv[t % 2], in_=ps[0][:], func=AF.Ln, scale=1.0 / D_F,
                                                 bias=EPS_AP), R=[PSD[0]] + CONSTS, W=[ld])
                P.op(act, lambda e: e.activation(out=rstd[t % 2], in_=lnv[t % 2], func=AF.Exp, scale=-0.5),
                     R=[ld], W=[rd])
                oD = D("ost", t % 2)
                for c in range(8):
                    P.op(dve, lambda e, c=c: e.scalar_tensor_tensor(out=ost[t % 2][:, c, :], in0=xT[:, c, tsl],
                                                                    scalar=gv[:, c:c + 1], in1=rstd[t % 2],
                                                                    op0=ALU.mult, op1=ALU.mult),
                         R=[XD[c][t], rd, CONST], W=[oD])
                P.dma(sp, out_d[:, :, tsl], ost[t % 2], D("outd", t % 2), R=[oD])

        for l in range(nl):
            last = (l == nl - 1)
            rmsnorm_to_hT(l, "g1")
            if stage == "big" and last:
                for rep in range(200):
                    P.mm([dict(out=ps[5][:, 0:128], lhsT=ident_b, rhs=ones_b) for _ in range(125)],
                         R=[CONST], W=[PSD[5]])
                stage = "n1"
            if stage == "n1" and last:
                dump(hT[:, :, :].rearrange("p c s -> p (c s)"), [HD[c][t] for c in range(8) for t in range(NTT)],
                     8 * S, cast=True)
                finish()
                return nc
            rg_group(l)
            if stage == "rg" and last:
                dump(yT[:, :, :].rearrange("p c s -> p (c s)"), [YD[c][t] for c in range(4) for t in range(NTT)],
                     4 * S, cast=True)
                finish()
                return nc
            wout_group(l, 0)
            if stage == "wo0" and last:
                dump(xT[:, :, :].rearrange("p c s -> p (c s)"), [XD[c][t] for c in range(8) for t in range(NTT)],
                     8 * S)
                finish()
                return nc
            if da_group(l):
                finish()
                return nc
            if stage in ("da", "da1") and last:
                dump(yT[:, :, :].rearrange("p c s -> p (c s)"), [YD[c][t] for c in range(4) for t in range(NTT)],
                     4 * S, cast=True)
                finish()
                return nc
            wout_group(l, 1)
            ml_group(l)
            if stage == "ml" and last:
                dump(yT[:, :, :].rearrange("p c s -> p (c s)"), [YD[c][t] for c in range(4) for t in range(NTT)],
                     4 * S, cast=True)
                finish()
                return nc
            wout_group(l, 2)
            if stage == "x1" and last:
                dump(xT[:, :, :].rearrange("p c s -> p (c s)"), [XD[c][t] for c in range(8) for t in range(NTT)],
                     8 * S)
                finish()
                return nc
            rmsnorm_to_hT(l, "g2")
            ffn(l)
            if stage == "x2" and last:
                dump(xT[:, :, :].rearrange("p c s -> p (c s)"), [XD[c][t] for c in range(8) for t in range(NTT)],
                     8 * S)
                finish()
                return nc
        final_norm()
        finish()
    return nc


_CACHE = {}


def make_in_maps(inputs, nl=NL, ncores=8):
    inp = {k: np.asarray(v) for k, v in inputs.items()}
    cm = const_mats()
    gvv = pack_gvec(inp)
    vecs = np.stack([pack_vec(inp, l) for l in range(nl)], axis=0)
    wblk = np.concatenate([pack_blocks(inp, l) for l in range(nl)], axis=0)
    maps = []
    for b in range(ncores):
        xb = np.ascontiguousarray(inp["x"][b].T.reshape(8, 128, S).transpose(1, 0, 2))
        posb = np.ascontiguousarray(np.broadcast_to(inp["positions"][b][None, :].astype(np.int32), (128, S)))
        maps.append({"xT": xb, "pos": posb, "cmat": cm, "gvec": gvv, "vec": vecs, "wblk": wblk})
    return maps


def kernel(**inputs):
    if "nc" not in _CACHE:
        _CACHE["nc"] = build()
    nc = _CACHE["nc"]
    maps = make_in_maps(inputs)
    res = run_bass_kernel_spmd(nc, maps, core_ids=list(range(8)))
    outs = []
    for b in range(8):
        o = res.results[b]["outT"]
        outs.append(o.transpose(2, 1, 0).reshape(S, 1024))
    return np.stack(outs, axis=0).astype(np.float32)
```
